# Optimizing a Trainium2 kernel written in Bass

```python
import jax, jax.numpy as jnp
from jax import lax
import numpy as np

D_MODEL = 1024
BATCH = 8
SEQ = 2048
DEPTH = 2
DEC_BATCH = 128
DEC_SEQ = 8
PAST_LEN = 16384
PAGE_SIZE = 128

N_EVEN = (DEPTH + 1) // 2
N_ODD = DEPTH // 2
D_POOL = D_MODEL // 2
POOL_WINDOWS = (2, 4, 8, 16)
N_POOL_GROUPS = len(POOL_WINDOWS)
POOL_GROUP = D_POOL // N_POOL_GROUPS
POOL_BUF = max(POOL_WINDOWS) - 1
D_SCONV = D_MODEL // 2
CONV_W = 3
CONV_BUF = CONV_W - 1
D_PROJ0 = D_POOL + 3 * D_SCONV
D_MIX0 = D_POOL + D_SCONV
D_GATE = D_MODEL
CHUNK = 128
N_SG_HEADS = 8
SG_HEAD = D_GATE // N_SG_HEADS
D_FF = 2816
N_MOD = 6
EPS = 1e-6

kernel_name = "hybrid_pool_sconv_sgmlp_convffn_step"


def rmsnorm(x, g):
    xf = x.astype(jnp.float32)
    y = xf * lax.rsqrt(jnp.mean(xf * xf, axis=-1, keepdims=True) + EPS)
    return (y * g.astype(jnp.float32)).astype(x.dtype)


def causal_dwconv(x, prev, w, b):
    L = x.shape[1]
    ext = jnp.concatenate([prev.astype(x.dtype), x], axis=1)
    y = b
    for k in range(CONV_W):
        y = y + w[k] * ext[:, k:k + L]
    return y, ext[:, -CONV_BUF:]


def pool_mixer(a, prev, start_pos, w_pool, s_pool):
    B, L, _ = a.shape
    ext = jnp.concatenate([prev.astype(a.dtype), a], axis=1)
    cs = jnp.cumsum(ext.astype(jnp.float32), axis=1)
    cs = jnp.concatenate([jnp.zeros((B, 1, D_POOL), jnp.float32), cs], axis=1)
    end = cs[:, POOL_BUF + 1:]
    pos = jnp.arange(L, dtype=jnp.int32) + start_pos
    means = []
    for g, w in enumerate(POOL_WINDOWS):
        sl = slice(g * POOL_GROUP, (g + 1) * POOL_GROUP)
        s = end[..., sl] - cs[:, POOL_BUF + 1 - w:POOL_BUF + 1 - w + L, sl]
        cnt = jnp.minimum(w, pos + 1).astype(jnp.float32)[None, :, None]
        means.append(s / cnt)
    d = (jnp.concatenate(means, axis=-1) - a.astype(jnp.float32)).astype(a.dtype)
    d = d.reshape(B, L, N_POOL_GROUPS, POOL_GROUP)
    y = jnp.einsum('blgc,gcd->blgd', d, w_pool).reshape(B, L, D_POOL) * s_pool
    return y, ext[:, -POOL_BUF:]


def spatial_gating(v, w_s, b_s):
    B, L, _ = v.shape
    n_chunks = -(-L // CHUNK)
    pad = n_chunks * CHUNK - L
    vp = jnp.pad(v, ((0, 0), (0, pad), (0, 0))).reshape(B, n_chunks, CHUNK, N_SG_HEADS, SG_HEAD)
    mask = jnp.tril(jnp.ones((CHUNK, CHUNK), dtype=bool))
    wm = jnp.where(mask[None], w_s, jnp.zeros((), w_s.dtype))
    z = jnp.einsum('hts,bnshd->bnthd', wm, vp) + b_s.T[None, None, :, :, None]
    return z.reshape(B, n_chunks * CHUNK, D_GATE)[:, :L]


def trunk(x, c, start_pos, pool_prev, sconv_prev, ffn_prev,
          norm_mix, norm_ffn, w_ada, b_ada, w_in0, w_pool, s_pool, sconv_w, sconv_b,
          w_out0, w_uv1, g_v1, w_s1, b_s1, w_out1, ffn_up, ffn_conv_w, ffn_conv_b,
          ffn_down, norm_final):
    new_pool, new_sconv, new_ffn, new_v = [], [], [], []
    for l in range(DEPTH):
        mod = jax.nn.silu(c) @ w_ada[l] + b_ada[l]
        sh1, sc1, g1, sh2, sc2, g2 = [m[:, None, :] for m in jnp.split(mod, N_MOD, axis=-1)]
        h = rmsnorm(x, norm_mix[l]) * (1 + sc1) + sh1
        if l % 2 == 0:
            e = l // 2
            proj = h @ w_in0[e]
            a, xin, bg, cg = jnp.split(proj, [D_POOL, D_POOL + D_SCONV, D_POOL + 2 * D_SCONV], axis=-1)
            ya, npool = pool_mixer(a, pool_prev[e], start_pos, w_pool[e], s_pool[e])
            yc, nconv = causal_dwconv(cg * xin, sconv_prev[e], sconv_w[e], sconv_b[e])
            yb = bg * yc
            mix = jnp.concatenate([ya, yb], axis=-1) @ w_out0[e]
            new_pool.append(npool)
            new_sconv.append(nconv)
        else:
            o = l // 2
            uv = jax.nn.gelu(h @ w_uv1[o])
            u, v = jnp.split(uv, 2, axis=-1)
            v = rmsnorm(v, g_v1[o])
            mix = (u * spatial_gating(v, w_s1[o], b_s1[o])) @ w_out1[o]
            new_v.append(v)
        x = x + g1 * mix
        h = rmsnorm(x, norm_ffn[l]) * (1 + sc2) + sh2
        up, nffn = causal_dwconv(h @ ffn_up[l], ffn_prev[l], ffn_conv_w[l], ffn_conv_b[l])
        ga, va = jnp.split(up, 2, axis=-1)
        x = x + g2 * ((jax.nn.gelu(ga) * va) @ ffn_down[l])
        new_ffn.append(nffn)
    y = rmsnorm(x, norm_final)
    return y, jnp.stack(new_pool), jnp.stack(new_sconv), jnp.stack(new_ffn), jnp.stack(new_v)


def setup_inputs(seed: int = 0) -> dict:
    key = jax.random.key(seed)
    ks = jax.random.split(key, 32)
    n = lambda k, s, sc=1.0: jax.random.normal(k, s, jnp.float32) * sc
    return {
        "x_prompt": n(ks[0], (BATCH, SEQ, D_MODEL)),
        "x_sample": n(ks[1], (DEC_BATCH, DEC_SEQ, D_MODEL)),
        "state_pool": n(ks[2], (N_EVEN, DEC_BATCH, POOL_BUF, D_POOL)),
        "state_sconv": n(ks[3], (N_EVEN, DEC_BATCH, CONV_BUF, D_SCONV)),
        "state_ffn": n(ks[4], (DEPTH, DEC_BATCH, CONV_BUF, 2 * D_FF)),
        "c_prompt": n(ks[5], (BATCH, D_MODEL)),
        "c_sample": n(ks[6], (DEC_BATCH, D_MODEL)),
        "norm_mix": 1.0 + n(ks[7], (DEPTH, D_MODEL), 0.02),
        "norm_ffn": 1.0 + n(ks[8], (DEPTH, D_MODEL), 0.02),
        "w_ada": n(ks[9], (DEPTH, D_MODEL, N_MOD * D_MODEL), 0.3 * D_MODEL ** -0.5),
        "b_ada": n(ks[10], (DEPTH, N_MOD * D_MODEL), 0.02),
        "w_in0": n(ks[11], (N_EVEN, D_MODEL, D_PROJ0), D_MODEL ** -0.5),
        "w_pool": n(ks[12], (N_EVEN, N_POOL_GROUPS, POOL_GROUP, POOL_GROUP), POOL_GROUP ** -0.5),
        "s_pool": 1.0 + n(ks[13], (N_EVEN, D_POOL), 0.05),
        "sconv_w": n(ks[14], (N_EVEN, CONV_W, D_SCONV), CONV_W ** -0.5),
        "sconv_b": n(ks[15], (N_EVEN, D_SCONV), 0.02),
        "w_out0": n(ks[16], (N_EVEN, D_MIX0, D_MODEL), D_MIX0 ** -0.5),
        "w_uv1": n(ks[17], (N_ODD, D_MODEL, 2 * D_GATE), D_MODEL ** -0.5),
        "g_v1": 1.0 + n(ks[18], (N_ODD, D_GATE), 0.02),
        "w_s1": n(ks[19], (N_ODD, N_SG_HEADS, CHUNK, CHUNK), CHUNK ** -0.5),
        "b_s1": 1.0 + n(ks[20], (N_ODD, N_SG_HEADS, CHUNK), 0.02),
        "w_out1": n(ks[21], (N_ODD, D_GATE, D_MODEL), D_GATE ** -0.5),
        "ffn_up": n(ks[22], (DEPTH, D_MODEL, 2 * D_FF), D_MODEL ** -0.5),
        "ffn_conv_w": n(ks[23], (DEPTH, CONV_W, 2 * D_FF), CONV_W ** -0.5),
        "ffn_conv_b": n(ks[24], (DEPTH, 2 * D_FF), 0.02),
        "ffn_down": n(ks[25], (DEPTH, D_FF, D_MODEL), D_FF ** -0.5),
        "norm_final": 1.0 + n(ks[26], (D_MODEL,), 0.02),
    }


def reference(x_prompt, x_sample, state_pool, state_sconv, state_ffn, c_prompt, c_sample,
              norm_mix, norm_ffn, w_ada, b_ada, w_in0, w_pool, s_pool, sconv_w, sconv_b,
              w_out0, w_uv1, g_v1, w_s1, b_s1, w_out1, ffn_up, ffn_conv_w, ffn_conv_b,
              ffn_down, norm_final):
    weights = (norm_mix, norm_ffn, w_ada, b_ada, w_in0, w_pool, s_pool, sconv_w, sconv_b,
               w_out0, w_uv1, g_v1, w_s1, b_s1, w_out1, ffn_up, ffn_conv_w, ffn_conv_b,
               ffn_down, norm_final)
    dt = x_prompt.dtype
    pool0 = jnp.zeros((N_EVEN, BATCH, POOL_BUF, D_POOL), dt)
    sconv0 = jnp.zeros((N_EVEN, BATCH, CONV_BUF, D_SCONV), dt)
    ffn0 = jnp.zeros((DEPTH, BATCH, CONV_BUF, 2 * D_FF), dt)
    y_prompt, pool_prompt, sconv_prompt, ffn_prompt, _ = trunk(
        x_prompt, c_prompt, 0, pool0, sconv0, ffn0, *weights)
    y_sample, pool_sample, sconv_sample, ffn_sample, sg_v_sample = trunk(
        x_sample, c_sample, PAST_LEN, state_pool, state_sconv, state_ffn, *weights)
    return (y_prompt, y_sample, pool_prompt, pool_sample, sconv_prompt, sconv_sample,
            ffn_prompt, ffn_sample, sg_v_sample)
```

```python
import numpy as np
from contextlib import ExitStack
import concourse.bass as bass
import concourse.mybir as mybir
from concourse.bass_utils import run_bass_kernel_spmd

F32 = mybir.dt.float32
BF16 = mybir.dt.bfloat16
AF = mybir.ActivationFunctionType
ALU = mybir.AluOpType

NCORES = 8
D = 1024
SEQ = 2048
DFF = 2816
NCH_FF = 44
EPS = 1e-6
NS_W = 4
NS_D = 2
PREFETCH = 3
QSTART = (0, 6, 11, 17, 22)

P_NM, P_NF, P_NFIN, P_BADA, P_SPOOL, P_SCW, P_SCB, P_FCW, P_FCB = 0, 16, 32, 40, 136, 140, 152, 156, 420
NPAR = 508


class _Op:
    __slots__ = ("eng", "fn", "deps", "lidx", "sig", "val", "dma", "lane", "gen", "need", "phase", "nmm")


class Sched:
    ENGS = ("pe", "act", "dve", "pool", "sp")
    CE = ("pe", "act", "dve", "pool")

    def __init__(self, nlanes):
        self.ops = []
        self.eng_ops = {e: [] for e in self.ENGS}
        self.lw = {}
        self.rd = {}
        self.nlanes = nlanes
        self.dma_count = {q: 0 for q in nlanes}
        self.lane_last = {}
        self.phase = "setup"

    @staticmethod
    def _expand(keys):
        out = []
        for k in keys:
            if isinstance(k, tuple) and k[0] == "scr" and len(k) == 2:
                out += [("scr", k[1], 0), ("scr", k[1], 1), ("scr", k[1], 2)]
            else:
                out.append(k)
        return out

    def add(self, eng, fn, reads=(), writes=(), dma=False):
        reads = self._expand(reads)
        writes = self._expand(writes)
        op = _Op()
        op.phase, op.nmm = self.phase, 1
        op.eng, op.fn, op.dma, op.sig, op.val = eng, fn, dma, False, None
        op.lidx = len(self.eng_ops[eng])
        deps = {}
        for k in reads:
            w = self.lw.get(k)
            if w is not None:
                deps[w] = True
        for k in writes:
            w = self.lw.get(k)
            if w is not None and w not in deps:
                deps[w] = False
            for r in self.rd.get(k, ()):
                if r not in deps:
                    deps[r] = False
        if dma:
            nl = self.nlanes[eng]
            i = self.dma_count[eng]
            self.dma_count[eng] += 1
            op.lane, op.gen = i % nl, i // nl
            prev = self.lane_last.get((eng, op.lane))
            if prev is not None:
                deps[prev] = True
            self.lane_last[(eng, op.lane)] = op
        op.deps = deps
        for k in reads:
            self.rd.setdefault(k, []).append(op)
        for k in writes:
            self.lw[k] = op
            self.rd[k] = []
        self.ops.append(op)
        self.eng_ops[eng].append(op)
        return op

    def resolve(self):
        waited = {e: {} for e in self.ENGS}
        for op in self.ops:
            F = op.eng
            best = {}
            for y, raw in op.deps.items():
                if y.dma:
                    key = ("dma", y.eng, y.lane)
                    if key not in best or best[key].gen < y.gen:
                        best[key] = y
                else:
                    E = y.eng
                    if E == F and not op.dma:
                        if F == "pe":
                            continue
                    if E not in best or best[E].lidx < y.lidx:
                        best[E] = y
            need = []
            for key, y in best.items():
                if y.dma:
                    if waited[F].get(key, -1) >= y.gen:
                        continue
                    waited[F][key] = y.gen
                else:
                    if waited[F].get(key, -1) >= y.lidx:
                        continue
                    waited[F][key] = y.lidx
                    y.sig = True
                need.append(y)
            op.need = need
        for E in self.CE:
            c = 0
            for op in self.eng_ops[E]:
                if op.sig and not op.dma:
                    c += 1
                    op.val = c

    def emit(self, block, handles, sems, lane_sems):
        names = {"pe": "tensor", "act": "scalar", "dve": "vector", "pool": "gpsimd", "sp": "sync"}
        for eng in self.ENGS:
            ops = self.eng_ops[eng]

            def body(e, ops=ops, eng=eng):
                for op in ops:
                    for y in op.need:
                        if y.dma:
                            e.wait_ge(lane_sems[(y.eng, y.lane)], 16 * (y.gen + 1))
                        else:
                            e.wait_ge(sems[y.eng], y.val)
                    ins = op.fn(e)
                    if op.dma:
                        ins.then_inc(lane_sems[(eng, op.lane)], 16)
                    elif op.sig:
                        ins.then_inc(sems[eng], 1)

            getattr(block, names[eng])(body)


def build_program():
    nc = bass.Bass("TRN2", target_bir_lowering=False)

    def din(name, shape):
        return nc.dram_tensor(name, list(shape), F32, kind="ExternalInput").ap()

    def dout(name, shape):
        return nc.dram_tensor(name, list(shape), F32, kind="ExternalOutput").ap()

    xp_d = din("xp", [128, 8, SEQ])
    xs_d = din("xs", [128, 8, 128])
    ct_d = din("ct", [128, 8, 17])
    par_d = din("par", [128, NPAR])
    stp_d = din("stp", [128, 4, 16, 15])
    sts_d = din("sts", [128, 4, 16, 2])
    stf_d = din("stf", [2, 128, NCH_FF, 16, 2])
    gv_d = din("gv", [128, D])
    ws1_d = din("ws1", [128, 8, 128])
    bs1_d = din("bs1", [1, D])
    wada_d = din("w_ada", [2, D, 6 * D])
    win0_d = din("w_in0", [D, 2 * D])
    wpool_d = din("w_pool", [128, 4, 128])
    wout0_d = din("w_out0", [D, D])
    wuv1_d = din("w_uv1", [D, 2 * D])
    wout1_d = din("w_out1", [D, D])
    fup_d = din("ffn_up", [2, D, 2 * DFF])
    fdn_d = din("ffn_down", [2, DFF, D])

    yp_d = dout("yp", [128, 8, SEQ])
    ys_d = dout("ys", [128, 8, 128])
    opp_d = dout("o_pool_p", [128, 4, 15])
    ops_d = dout("o_pool_s", [128, 4, 16, 15])
    osp_d = dout("o_sconv_p", [128, 4, 2])
    oss_d = dout("o_sconv_s", [128, 4, 16, 2])
    ofp_d = dout("o_ffn_p", [2, 128, NCH_FF, 2])
    ofs_d = dout("o_ffn_s", [2, 128, NCH_FF, 16, 2])
    osg_d = dout("o_sgv", [128, D])

    es = ExitStack()

    def sb(name, shape, dt=F32):
        return es.enter_context(nc.sbuf_tensor(name, list(shape), dt))

    SW = 1152
    x_t = sb("x_t", [128, 8, SW])
    h_t = sb("h_t", [128, 8, SW], BF16)
    R_t = sb("R_t", [128, 11 * SW], BF16)
    NSCR = 8
    SCRW = 1152
    scr = [sb(f"scr{i}", [128, SCRW]) for i in range(NSCR)]
    wsl = [sb(f"wsl{i}", [128, 8, 512], BF16) for i in range(NS_W)]
    dsl = [sb(f"dsl{i}", [128, 6, 512], BF16) for i in range(NS_D)]
    par_t = sb("par_t", [128, NPAR])
    ct_t = sb("ct_t", [128, 8, 17])
    sc_t = sb("sc_t", [128, 8, 17], BF16)
    mod_t = sb("mod_t", [128, 2, 48, 17])
    A_t = sb("A_t", [128, 2, 2, 8, 17])
    ones_bf = sb("ones_bf", [128, 128], BF16)
    nh_t = sb("nh_t", [128, 1])
    eps_t = sb("eps_t", [128, 1])
    dmy_t = sb("dmy_t", [128, 1])
    rc_t = sb("rc_t", [128, 16])
    gv_t = sb("gv_t", [128, D])
    wpool_t = sb("wpool_t", [128, 4, 128], BF16)
    wmT_t = sb("wmT_t", [128, 8, 128], BF16)
    wblk_t = sb("wblk_t", [128, 8, 128], BF16)
    bsh_t = sb("bsh_t", [1, D], BF16)
    bsl_t = sb("bsl_t", [1, D], BF16)
    bssh_t = sb("bssh_t", [1, D], BF16)
    bssl_t = sb("bssl_t", [1, D], BF16)
    exts2 = [sb(f"exts{i}", [128, 16, 23]) for i in range(2)]
    sps = [sb(f"sps{i}", [128, 16, 23]) for i in range(2)]
    extus2 = [sb(f"extus{i}", [128, 16, 10]) for i in range(2)]
    ycs_t = sb("ycs_t", [128, 16, 8])
    extcs = [[sb(f"extcs{c}{p}", [128, 16, 10]) for p in range(2)] for c in range(2)]
    yss = [[sb(f"yss{c}{p}", [128, 16, 8]) for p in range(2)] for c in range(2)]
    tms_t = sb("tms_t", [128, 16, 8])
    stf_t = sb("stf_t", [128, NCH_FF, 16, 2])
    cpool_t = sb("cpool_t", [128, 4, 15])
    csc_t = sb("csc_t", [128, 4, 2])
    cffn_t = sb("cffn_t", [128, 2, NCH_FF, 2])
    opp_t = sb("opp_t", [128, 4, 15])
    osp_t = sb("osp_t", [128, 4, 2])
    ffp_t = sb("ffp_t", [128, NCH_FF, 2])
    ssv_t = sb("ssv_t", [128, 16])
    rv_t = sb("rv_t", [128, 16])
    fixt = sb("fixt", [128, 16])

    ps = [es.enter_context(nc.psum_tensor(f"ps{i}", [128, 512], F32)) for i in range(8)]

    nlanes = {"sp": 8, "pool": 8}
    S = Sched(nlanes)
    bank_ctr = [0]

    def newbank():
        b = bank_ctr[0] % 8
        bank_ctr[0] += 1
        return b

    def PK(b):
        return ("ps", b)

    def RK(b0, nb):
        return [("R", g) for g in range(b0 // 256, (b0 + nb + 255) // 256)]

    act_v = R_t[:, :].rearrange("p (j s) -> p j s", s=SW)
    vn_v = R_t[:, 0:9 * D].rearrange("p (j d) -> p j d", d=D)

    def actK(j, off, n):
        return RK(j * SW * 2 + off * 2, n * 2)

    def vnK(jb, d0=0, nd=D):
        return RK(jb * D * 2 + d0 * 2, nd * 2)

    bsf_t = scr[5][0:1, 0:D]
    bshf_t = scr[4][0:1, 0:D]

    def scrbf(i):
        return scr[i][:, :].bitcast(BF16)

    def par(c0, n=1):
        return par_t[:, c0:c0 + n]

    def v3(ap):
        return ap.rearrange("p (b s) -> p b s", s=8)

    def bc3(ap16):
        return ap16.unsqueeze(2).to_broadcast([128, 16, 8])

    def wK(s):
        return [("w", s, 0), ("w", s, 1)]

    class SlotRef(int):
        pass

    class WStream:
        def __init__(self):
            self.specs = []
            self.record = True
            self.i = 0
            self.issued = 0
            self.busy = {"w": [None] * NS_W, "d": [None] * NS_D}
            self.slot_of = {}

        @staticmethod
        def typ(spec):
            return "d" if spec[0] == "down" else "w"

        def _pump(self):
            while self.issued < min(len(self.specs), self.i + PREFETCH):
                t = self.typ(self.specs[self.issued])
                if None not in self.busy[t]:
                    break
                self._issue(self.issued)
                self.issued += 1

        def get(self, spec):
            if self.record:
                if self.specs and self.specs[-1] == spec and spec == ("ada", 0, 0):
                    return 0
                self.specs.append(spec)
                return 0
            i = self.i
            self.i += 1
            assert self.specs[i] == spec, (self.specs[i], spec)
            self._pump()
            assert self.issued > i, ("weight slot ring exhausted", spec)
            r = SlotRef(self.slot_of[i])
            r.item = i
            return r

        def unget(self, ref):
            self.i -= 1

        def free(self, ref):
            if self.record:
                return
            t = self.typ(self.specs[ref.item])
            assert self.busy[t][int(ref)] == ref.item
            self.busy[t][int(ref)] = None
            self._pump()

        def _issue(self, j):
            spec = self.specs[j]
            typ = self.typ(spec)
            s = self.busy[typ].index(None)
            self.busy[typ][s] = j
            self.slot_of[j] = s
            kind = spec[0]
            if kind == "down":
                _, l, q, cg = spec
                j0, j1 = QSTART[q], QSTART[q + 1]
                src = fdn_d[l].rearrange("(j p) n -> p j n", p=128)[:, j0:j1, cg * 512:(cg + 1) * 512]
                dst = dsl[s][:, 0:j1 - j0, :]
                S.add("pool", lambda e, d=dst, r=src: e.dma_start(out=d, in_=r), writes=[("d", s)], dma=True)
            elif kind == "up":
                _, l, j0, npair = spec
                w = fup_d[l].rearrange("(k p) n -> p k n", p=128)
                for c in range(2):
                    src = w[:, :, c * DFF + j0 * 128: c * DFF + (j0 + npair) * 128]
                    dst = wsl[s][:, :, c * 256: c * 256 + npair * 128]
                    S.add("pool", lambda e, d=dst, r=src: e.dma_start(out=d, in_=r), writes=[("w", s, c)], dma=True)
            else:
                if kind == "ada":
                    w = wada_d[spec[1]]
                else:
                    w = {"in0": win0_d, "out0": wout0_d, "uv1": wuv1_d, "out1": wout1_d}[kind]
                c = spec[-1]
                src = w.rearrange("(k p) n -> p k n", p=128)[:, :, c * 512:(c + 1) * 512]
                dst = wsl[s][:, :, :]
                S.add("pool", lambda e, d=dst, r=src: e.dma_start(out=d, in_=r), writes=wK(s), dma=True)

    W = WStream()

    def gemm(out_ap, terms, reads, bank, per_term_reads=None):
        n = len(terms)
        for i, (l, r) in enumerate(terms):
            rk = list(reads) + (list(per_term_reads[i]) if per_term_reads is not None else [])
            S.add("pe", lambda e, l=l, r=r, i=i: e.matmul(out_ap, lhsT=l, rhs=r, start=(i == 0), stop=(i == n - 1)),
                  reads=rk, writes=[PK(bank)])

    def tiles_of(st):
        return [(0, 512), (512, 512)] + ([(1024, 128)] if st == 1 else [])

    def hK(k, ti):
        return ("h", k, ti)

    def xK(k, ti):
        return ("x", k, ti)

    def scK(i, ti=None):
        return ("scr", i)

    def modp(l, c):
        return mod_t[:, l, c, 0:1]

    def mods(l, c):
        return mod_t[:, l, c, 1:17]

    def setup_a():
        S.add("sp", lambda e: e.dma_start(out=par_t[:, :], in_=par_d[:, :]), writes=["par"], dma=True)
        S.add("sp", lambda e: e.dma_start(out=ct_t[:, :, :], in_=ct_d[:, :, :]), writes=["ct"], dma=True)
        S.add("act", lambda e: e.activation(out=sc_t[:, :, :], in_=ct_t[:, :, :], func=AF.Silu),
              reads=["ct"], writes=["sc"])

    def setup_b():
        S.add("sp", lambda e: e.dma_start(out=gv_t[:, :], in_=gv_d[:, :]), writes=["gv"], dma=True)
        S.add("sp", lambda e: e.dma_start(out=bsf_t, in_=bs1_d[:, :]), writes=["bsf", scK(5)], dma=True)
        ws_stage = scr[7][:, 0:1024].rearrange("p (h t) -> p h t", t=128)
        blk_stage = scr[6][:, 0:1024].rearrange("p (h t) -> p h t", t=128)
        S.add("sp", lambda e: e.dma_start(out=ws_stage, in_=ws1_d[:, :, :]), writes=[scK(7)], dma=True)
        S.add("pool", lambda e: e.memset(ones_bf[:, :], 1.0), writes=["ones"])
        S.add("pool", lambda e: e.memset(nh_t[:, :], -0.5), writes=["nh"])
        S.add("pool", lambda e: e.memset(eps_t[:, :], EPS), writes=["eps"])
        for t in range(15):
            S.add("pool", lambda e, t=t: e.memset(rc_t[:, t:t + 1], 1.0 / (t + 1)), writes=["rc"])
        S.add("pool", lambda e: e.memset(scr[6][:, :], 0.0), writes=[scK(6)])
        for b in range(16):
            S.add("sp", lambda e, b=b: e.dma_start(out=blk_stage[b * 8:(b + 1) * 8, :, b * 8:(b + 1) * 8],
                                                   in_=ws1_d[0:8, :, 0:8]),
                  writes=[("scr6b", b)], reads=[scK(6)], dma=True)
        S.add("pool", lambda e: e.affine_select(out=wmT_t[:, :, :], in_=ws_stage, pattern=[[0, 8], [1, 128]],
                                                compare_op=ALU.is_ge, fill=0.0, base=0, channel_multiplier=-1),
              reads=[scK(7)], writes=["wmT"])
        S.add("pool", lambda e: e.affine_select(out=wblk_t[:, :, :], in_=blk_stage, pattern=[[0, 8], [1, 128]],
                                                compare_op=ALU.is_ge, fill=0.0, base=0, channel_multiplier=-1),
              reads=[scK(6)] + [("scr6b", b) for b in range(16)], writes=["wblk"])
        S.add("dve", lambda e: e.tensor_copy(out=bsh_t[:, :], in_=bsf_t), reads=["bsf", scK(5)], writes=["bsh"])
        S.add("dve", lambda e: e.tensor_copy(out=bshf_t, in_=bsh_t[:, :]), reads=["bsh"], writes=["bshf", scK(4)])
        S.add("dve", lambda e: e.tensor_tensor(out=bsl_t[:, :], in0=bsf_t, in1=bshf_t, op=ALU.subtract),
              reads=["bsf", "bshf", scK(4), scK(5)], writes=["bsl"])
        for hd in range(8):
            for src, dst, kn in ((bsh_t, bssh_t, "bssh"), (bsl_t, bssl_t, "bssl")):
                S.add("dve", lambda e, hd=hd, src=src, dst=dst: e.tensor_copy(
                    out=dst[0:1, hd * 128:(hd + 1) * 128].rearrange("p (b s) -> p b s", s=8),
                    in_=src[0:1, hd * 128:hd * 128 + 8].unsqueeze(1).to_broadcast([1, 16, 8])),
                    reads=["bsh", "bsl"], writes=[(kn, hd)])
        S.add("pool", lambda e: e.dma_start(out=wpool_t[:, :, :], in_=wpool_d[:, :, :]), writes=["wpool"], dma=True)

    def adaln(l, srange):
        for s in srange:
            slot = W.get(("ada", l, s))
            if W.record:
                continue
            bank = newbank()
            for j in range(4):
                for k in range(8):
                    S.add("pe", lambda e, slot=slot, bank=bank, j=j, k=k: e.matmul(
                        ps[bank][:, j * 17:(j + 1) * 17], lhsT=wsl[slot][:, k, j * 128:(j + 1) * 128],
                        rhs=sc_t[:, k, :], start=(k == 0), stop=(k == 7)),
                        reads=wK(slot) + ["sc"], writes=[PK(bank)])
            S.add("dve", lambda e, s=s, bank=bank, l=l: e.tensor_tensor(
                out=mod_t[:, l, 4 * s:4 * s + 4, :],
                in0=ps[bank][:, 0:68].rearrange("p (j n) -> p j n", n=17),
                in1=par_t[:, P_BADA + l * 48 + 4 * s: P_BADA + l * 48 + 4 * s + 4].unsqueeze(2).to_broadcast([128, 4, 17]),
                op=ALU.add), reads=[PK(bank), "par"], writes=[("mod", l, s)])
            W.free(slot)
            if W.record:
                continue
            for which, (cbase, pbase) in enumerate(((8, P_NM), (32, P_NF))):
                if s == cbase // 4 + 1:
                    S.add("dve", lambda e, which=which, cbase=cbase, pbase=pbase, l=l: e.scalar_tensor_tensor(
                        out=A_t[:, l, which, :, :], in0=mod_t[:, l, cbase:cbase + 8, :], scalar=1.0,
                        in1=par_t[:, pbase + l * 8: pbase + l * 8 + 8].unsqueeze(2).to_broadcast([128, 8, 17]),
                        op0=ALU.add, op1=ALU.mult),
                        reads=[("mod", l, cbase // 4), ("mod", l, cbase // 4 + 1), "par"], writes=[("A", l, which)])

    def load_x(st):
        if st == 0:
            S.add("sp", lambda e: e.dma_start(out=x_t[:, :, 1024:1152], in_=xs_d[:, :, :]),
                  writes=[xK(k, 2) for k in range(8)], dma=True)
        for ti in range(2):
            for k in range(8):
                S.add("sp", lambda e, k=k, st=st, ti=ti: e.dma_start(
                    out=x_t[:, k, ti * 512:(ti + 1) * 512], in_=xp_d[:, k, st * 1024 + ti * 512: st * 1024 + (ti + 1) * 512]),
                    writes=[xK(k, ti)], dma=True)

    def sumsq_rstd(st):
        tiles = tiles_of(st)
        S.add("act", lambda e: e.activation(out=dmy_t[:, 0:1], in_=eps_t[:, 0:1], func=AF.Sqrt), reads=["eps"], writes=["dmy"])
        for ti, (off, n) in enumerate(tiles):
            b = newbank()
            for k in range(8):
                sq = scrbf(k % 2)
                S.add("act", lambda e, k=k, sq=sq, off=off, n=n: e.activation(
                    out=sq[:, off:off + n], in_=x_t[:, k, off:off + n], func=AF.Square),
                    reads=[xK(k, ti)], writes=[("scr", k % 2, ti)])
                S.add("pe", lambda e, k=k, sq=sq, off=off, n=n, b=b: e.matmul(
                    ps[b][:, 0:n], lhsT=ones_bf[:, :], rhs=sq[:, off:off + n], start=(k == 0), stop=(k == 7)),
                    reads=[("scr", k % 2, ti), "ones"], writes=[PK(b)])
            S.add("act", lambda e, off=off, n=n, b=b: e.activation(
                out=scr[2][:, off:off + n], in_=ps[b][:, 0:n], func=AF.Sqrt, scale=1.0 / D, bias=eps_t[:, 0:1]),
                reads=[PK(b), "eps"], writes=[("scr", 2, ti)])
            S.add("dve", lambda e, off=off, n=n: e.reciprocal(out=scr[2][:, off:off + n], in_=scr[2][:, off:off + n]),
                  reads=[("scr", 2, ti)], writes=[("scr", 2, ti)])

    def norm_mod(st, l, which, do_stats=True):
        if do_stats:
            sumsq_rstd(st)
        tiles = tiles_of(st)
        shc = 0 if which == 0 else 24
        for ti, (off, n) in enumerate(tiles):
            for k in range(8):
                tmp = scr[3 + (k % 2)]
                tk = ("scr", 3 + k % 2, ti)
                if ti < 2:
                    S.add("dve", lambda e, k=k, tmp=tmp, off=off, n=n: e.scalar_tensor_tensor(
                        out=tmp[:, off:off + n], in0=x_t[:, k, off:off + n], scalar=A_t[:, l, which, k, 0:1],
                        in1=scr[2][:, off:off + n], op0=ALU.mult, op1=ALU.mult),
                        reads=[xK(k, ti), ("A", l, which), ("scr", 2, ti)], writes=[tk])
                    S.add("act", lambda e, k=k, tmp=tmp, off=off, n=n: e.activation(
                        out=h_t[:, k, off:off + n], in_=tmp[:, off:off + n], func=AF.Identity, bias=modp(l, shc + k)),
                        reads=[tk, ("mod", l, (shc + k) // 4)], writes=[hK(k, ti)])
                else:
                    S.add("dve", lambda e, k=k: e.tensor_tensor(
                        out=tms_t[:, :, :], in0=v3(x_t[:, k, 1024:1152]), in1=v3(scr[2][:, 1024:1152]), op=ALU.mult),
                        reads=[xK(k, 2), ("scr", 2, 2)], writes=["tms"])
                    S.add("dve", lambda e, k=k: e.tensor_tensor(
                        out=tms_t[:, :, :], in0=tms_t[:, :, :], in1=bc3(A_t[:, l, which, k, 1:17]), op=ALU.mult),
                        reads=["tms", ("A", l, which)], writes=["tms"])
                    S.add("dve", lambda e, k=k: e.tensor_tensor(
                        out=v3(h_t[:, k, 1024:1152]), in0=tms_t[:, :, :], in1=bc3(mods(l, shc + k)), op=ALU.add),
                        reads=["tms", ("mod", l, (shc + k) // 4)], writes=[hK(k, 2)])

    def final_norm(st):
        sumsq_rstd(st)
        tiles = tiles_of(st)
        for ti, (off, n) in enumerate(tiles):
            for k in range(8):
                tmp = scr[3 + (k % 2)]
                tk = ("scr", 3 + k % 2, ti)
                S.add("dve", lambda e, k=k, tmp=tmp, off=off, n=n: e.scalar_tensor_tensor(
                    out=tmp[:, off:off + n], in0=x_t[:, k, off:off + n], scalar=par(P_NFIN + k), in1=scr[2][:, off:off + n],
                    op0=ALU.mult, op1=ALU.mult),
                    reads=[xK(k, ti), "par", ("scr", 2, ti)], writes=[tk])
                if ti < 2:
                    S.add("sp", lambda e, k=k, tmp=tmp, off=off, n=n: e.dma_start(
                        out=yp_d[:, k, st * 1024 + off: st * 1024 + off + n], in_=tmp[:, off:off + n]),
                        reads=[tk], writes=[("out", "yp", st, k, ti)], dma=True)
                else:
                    S.add("sp", lambda e, k=k, tmp=tmp: e.dma_start(out=ys_d[:, k, :], in_=tmp[:, 1024:1152]),
                          reads=[tk], writes=[("out", "ys", k)], dma=True)

    def residual(st, bank, m, ti, off, n, l, gc):
        if ti < 2:
            S.add("dve", lambda e: e.scalar_tensor_tensor(
                out=x_t[:, m, off:off + n], in0=ps[bank][:, 0:n], scalar=modp(l, gc + m), in1=x_t[:, m, off:off + n],
                op0=ALU.mult, op1=ALU.add),
                reads=[PK(bank), xK(m, ti), ("mod", l, (gc + m) // 4)], writes=[xK(m, ti)])
        else:
            S.add("dve", lambda e: e.tensor_tensor(
                out=tms_t[:, :, :], in0=v3(ps[bank][:, 0:128]), in1=bc3(mods(l, gc + m)), op=ALU.mult),
                reads=[PK(bank), ("mod", l, (gc + m) // 4)], writes=["tms"])
            S.add("dve", lambda e: e.tensor_tensor(
                out=v3(x_t[:, m, 1024:1152]), in0=v3(x_t[:, m, 1024:1152]), in1=tms_t[:, :, :], op=ALU.add),
                reads=["tms", xK(m, 2)], writes=[xK(m, 2)])

    def out_gemm(st, l, kind, rhs_fn, rhs_keys_fn, gc):
        tiles = tiles_of(st)
        slots = [W.get((kind, 0)), W.get((kind, 1))]
        if W.record:
            return
        for ti, (off, n) in enumerate(tiles):
            for m in range(8):
                slot = slots[m // 4]
                mm = m % 4
                bank = newbank()
                terms = [(wsl[slot][:, k, mm * 128:(mm + 1) * 128], rhs_fn(k, off, n)) for k in range(8)]
                gemm(ps[bank][:, 0:n], terms, wK(slot), bank, [rhs_keys_fn(k, ti, off, n) for k in range(8)])
                residual(st, bank, m, ti, off, n, l, gc)
        W.free(slots[0])
        W.free(slots[1])

    def mixer0(st):
        tiles = tiles_of(st)
        slots = [W.get(("in0", 0)), None, None, None]
        if W.record:
            ada_some(6)
            for c_ in (1, 2, 3):
                W.get(("in0", c_))
            W.get(("out0", 0)); W.get(("out0", 1))
            return
        mix_v = act_v

        def P_S1(g):
            ext = scr[g % 2]
            exts_t, exk_h, exk_n = exts2[g % 2], ("exts", g % 2, "h"), ("exts", g % 2, "n")
            if st == 0:
                S.add("pool", lambda e, ext=ext: e.memset(ext[:, 0:15], 0.0), writes=[scK(g % 2)])
            else:
                S.add("pool", lambda e, ext=ext, g=g: e.tensor_copy(out=ext[:, 0:15], in_=cpool_t[:, g, :]),
                      reads=[("cpool", g)], writes=[scK(g % 2)])
                S.add("sp", lambda e, g=g: e.dma_start(out=exts_t[:, :, 0:15], in_=stp_d[:, g, :, :]),
                      writes=[exk_h], dma=True)
            for ti, (off, n) in enumerate(tiles):
                bank = newbank()
                terms = [(wsl[slots[0]][:, k, g * 128:(g + 1) * 128], h_t[:, k, off:off + n]) for k in range(8)]
                gemm(ps[bank][:, 0:n], terms, wK(slots[0]), bank, [[hK(k, ti)] for k in range(8)])
                if ti < 2:
                    S.add("act", lambda e, ext=ext, off=off, bank=bank: e.activation(
                        out=ext[:, 15 + off:15 + off + 512], in_=ps[bank][:, 0:512], func=AF.Copy),
                        reads=[PK(bank)], writes=[("scr", g % 2, ti)])
                else:
                    S.add("act", lambda e, bank=bank: e.activation(
                        out=exts_t[:, :, 15:23], in_=v3(ps[bank][:, 0:128]), func=AF.Copy),
                        reads=[PK(bank)], writes=[exk_n])

        def P_S2(g):
            Wd = 2 << g
            ext = scr[g % 2]
            exts_t, exk_h, exk_n = exts2[g % 2], ("exts", g % 2, "h"), ("exts", g % 2, "n")
            dbf = scrbf(4 + g % 2)
            cur, ln, curk = ext, 1039, scK(g % 2)
            for step in range(g + 1):
                w = 1 << step
                nxt = scr[2 + step % 2]
                S.add("dve", lambda e, cur=cur, nxt=nxt, w=w, ln=ln: e.tensor_tensor(
                    out=nxt[:, 0:ln - w], in0=cur[:, 0:ln - w], in1=cur[:, w:ln], op=ALU.add),
                    reads=[curk], writes=[scK(2 + step % 2)])
                cur, ln, curk = nxt, ln - w, scK(2 + step % 2)
            o = 16 - Wd
            S.add("dve", lambda e, cur=cur, o=o, Wd=Wd, ext=ext, dbf=dbf: e.scalar_tensor_tensor(
                out=dbf[:, 0:1024], in0=cur[:, o:o + 1024], scalar=1.0 / Wd, in1=ext[:, 15:1039],
                op0=ALU.mult, op1=ALU.subtract),
                reads=[curk, scK(g % 2)], writes=[scK(4 + g % 2)])
            if st == 0:
                nf = Wd - 1
                S.add("dve", lambda e, cur=cur, o=o, nf=nf: e.tensor_tensor(
                    out=fixt[:, 0:nf], in0=cur[:, o:o + nf], in1=rc_t[:, 0:nf], op=ALU.mult),
                    reads=[curk, "rc"], writes=["fix"])
                S.add("dve", lambda e, nf=nf, ext=ext, dbf=dbf: e.tensor_tensor(
                    out=dbf[:, 0:nf], in0=fixt[:, 0:nf], in1=ext[:, 15:15 + nf], op=ALU.subtract),
                    reads=["fix", scK(g % 2)], writes=[scK(4 + g % 2)])
                S.add("pool", lambda e, ext=ext, g=g: e.tensor_copy(out=cpool_t[:, g, :], in_=ext[:, 1024:1039]),
                      reads=[scK(g % 2)], writes=[("cpool", g)])
            else:
                S.add("pool", lambda e, ext=ext, g=g: e.tensor_copy(out=opp_t[:, g, :], in_=ext[:, 1024:1039]),
                      reads=[scK(g % 2)], writes=[("opp", g)])
                curs, lns, cursk = exts_t, 23, None
                for step in range(g + 1):
                    w = 1 << step
                    nxt = sps[step % 2]
                    S.add("dve", lambda e, curs=curs, nxt=nxt, w=w, lns=lns: e.tensor_tensor(
                        out=nxt[:, :, 0:lns - w], in0=curs[:, :, 0:lns - w], in1=curs[:, :, w:lns], op=ALU.add),
                        reads=([exk_h, exk_n] if cursk is None else [cursk]), writes=[("sps", step % 2)])
                    curs, lns, cursk = nxt, lns - w, ("sps", step % 2)
                S.add("dve", lambda e, curs=curs, o=o, Wd=Wd, dbf=dbf: e.scalar_tensor_tensor(
                    out=v3(dbf[:, 1024:1152]), in0=curs[:, :, o:o + 8], scalar=1.0 / Wd, in1=exts_t[:, :, 15:23],
                    op0=ALU.mult, op1=ALU.subtract),
                    reads=[cursk, exk_n], writes=[scK(4 + g % 2)])
                S.add("sp", lambda e, g=g: e.dma_start(out=ops_d[:, g, :, :], in_=exts_t[:, :, 8:23]),
                      reads=[exk_h, exk_n], writes=[("out", "ops", g)], dma=True)
            for ti, (off, n) in enumerate(tiles):
                bank = newbank()
                gemm(ps[bank][:, 0:n], [(wpool_t[:, g, :], dbf[:, off:off + n])], ["wpool", scK(4 + g % 2)], bank)
                S.add("act", lambda e, bank=bank, off=off, n=n, g=g: e.activation(
                    out=mix_v[:, g, off:off + n], in_=ps[bank][:, 0:n], func=AF.Identity, scale=par(P_SPOOL + g)),
                    reads=[PK(bank), "par"], writes=actK(g, off, n))
            if st == 1 and g == 3:
                S.add("sp", lambda e: e.dma_start(out=opp_d[:, :, :], in_=opp_t[:, :, :]),
                      reads=[("opp", g_) for g_ in range(4)], writes=[("out", "opp")], dma=True)

        def cbufs(i):
            return scr[i % 2], scr[4 + i % 2], scr[6], (scr[7] if i % 2 == 0 else scr[3]), (7 if i % 2 == 0 else 3)

        def C_S1(i):
            extu, yc, xsb, bgs, bgi = cbufs(i)
            extus_t, exuk_h, exuk = extus2[i % 2], ("extus", i % 2, "h"), ("extus", i % 2, "n")
            if st == 0:
                S.add("pool", lambda e, extu=extu: e.memset(extu[:, 0:2], 0.0), writes=[scK(i % 2)])
            else:
                S.add("pool", lambda e, extu=extu, i=i: e.tensor_copy(out=extu[:, 0:2], in_=csc_t[:, i, :]),
                      reads=[("csc", i)], writes=[scK(i % 2)])
                S.add("sp", lambda e, i=i: e.dma_start(out=extus_t[:, :, 0:2], in_=sts_d[:, i, :, :]),
                      writes=[exuk_h], dma=True)
            for ti, (off, n) in enumerate(tiles):
                bx, bb, bcg = newbank(), newbank(), newbank()
                for bnk, sl in ((bx, 1), (bb, 2), (bcg, 3)):
                    terms = [(wsl[slots[sl]][:, k, i * 128:(i + 1) * 128], h_t[:, k, off:off + n]) for k in range(8)]
                    gemm(ps[bnk][:, 0:n], terms, wK(slots[sl]), bnk, [[hK(k, ti)] for k in range(8)])
                S.add("act", lambda e, bx=bx, off=off, n=n: e.activation(out=xsb[:, off:off + n], in_=ps[bx][:, 0:n], func=AF.Copy),
                      reads=[PK(bx)], writes=[("scr", 6, ti)])
                S.add("act", lambda e, bb=bb, off=off, n=n: e.activation(out=bgs[:, off:off + n], in_=ps[bb][:, 0:n], func=AF.Copy),
                      reads=[PK(bb)], writes=[("scr", bgi, ti)])
                if ti < 2:
                    S.add("dve", lambda e, bcg=bcg, off=off, extu=extu: e.tensor_tensor(
                        out=extu[:, 2 + off:2 + off + 512], in0=ps[bcg][:, 0:512], in1=xsb[:, off:off + 512], op=ALU.mult),
                        reads=[PK(bcg), ("scr", 6, ti)], writes=[("scr", i % 2, ti)])
                else:
                    S.add("dve", lambda e, bcg=bcg: e.tensor_tensor(
                        out=extus_t[:, :, 2:10], in0=v3(ps[bcg][:, 0:128]), in1=v3(xsb[:, 1024:1152]), op=ALU.mult),
                        reads=[PK(bcg), ("scr", 6, 2)], writes=[exuk])

        def C_S2(i):
            extu, yc, xsb, bgs, bgi = cbufs(i)
            extus_t, exuk_h, exuk = extus2[i % 2], ("extus", i % 2, "h"), ("extus", i % 2, "n")
            wb = P_SCW + i * 3
            S.add("act", lambda e, extu=extu, yc=yc, wb=wb, i=i: e.activation(
                out=yc[:, 0:1024], in_=extu[:, 2:1026], func=AF.Identity, scale=par(wb + 2), bias=par(P_SCB + i)),
                reads=[scK(i % 2), "par"], writes=[scK(4 + i % 2)])
            for tap in (1, 0):
                S.add("dve", lambda e, extu=extu, yc=yc, wb=wb, tap=tap: e.scalar_tensor_tensor(
                    out=yc[:, 0:1024], in0=extu[:, tap:tap + 1024], scalar=par(wb + tap), in1=yc[:, 0:1024],
                    op0=ALU.mult, op1=ALU.add),
                    reads=[scK(i % 2), scK(4 + i % 2), "par"], writes=[scK(4 + i % 2)])
            S.add("dve", lambda e, yc=yc, i=i, bgs=bgs: e.tensor_tensor(
                out=mix_v[:, 4 + i, 0:1024], in0=bgs[:, 0:1024], in1=yc[:, 0:1024], op=ALU.mult),
                reads=[scK(bgi), scK(4 + i % 2)], writes=actK(4 + i, 0, 1024))
            if st == 0:
                S.add("pool", lambda e, extu=extu, i=i: e.tensor_copy(out=csc_t[:, i, :], in_=extu[:, 1024:1026]),
                      reads=[scK(i % 2)], writes=[("csc", i)])
            else:
                S.add("pool", lambda e, extu=extu, i=i: e.tensor_copy(out=osp_t[:, i, :], in_=extu[:, 1024:1026]),
                      reads=[scK(i % 2)], writes=[("osp", i)])
                S.add("act", lambda e, wb=wb, i=i: e.activation(
                    out=ycs_t[:, :, :], in_=extus_t[:, :, 2:10], func=AF.Identity, scale=par(wb + 2), bias=par(P_SCB + i)),
                    reads=[exuk, "par"], writes=["ycs"])
                for tap in (1, 0):
                    S.add("dve", lambda e, wb=wb, tap=tap: e.scalar_tensor_tensor(
                        out=ycs_t[:, :, :], in0=extus_t[:, :, tap:tap + 8], scalar=par(wb + tap), in1=ycs_t[:, :, :],
                        op0=ALU.mult, op1=ALU.add),
                        reads=[exuk, exuk_h, "ycs", "par"], writes=["ycs"])
                S.add("dve", lambda e, i=i, bgs=bgs: e.tensor_tensor(
                    out=v3(mix_v[:, 4 + i, 1024:1152]), in0=v3(bgs[:, 1024:1152]), in1=ycs_t[:, :, :], op=ALU.mult),
                    reads=[scK(bgi), "ycs"], writes=actK(4 + i, 1024, 128))
                S.add("sp", lambda e, i=i: e.dma_start(out=oss_d[:, i, :, :], in_=extus_t[:, :, 8:10]),
                      reads=[exuk, exuk_h], writes=[("out", "oss", i)], dma=True)
                if i == 3:
                    S.add("sp", lambda e: e.dma_start(out=osp_d[:, :, :], in_=osp_t[:, :, :]),
                          reads=[("osp", i_) for i_ in range(4)], writes=[("out", "osp")], dma=True)

        units = [("P", g) for g in range(4)] + [("C", i) for i in range(4)]
        s1 = {"P": P_S1, "C": C_S1}
        s2 = {"P": P_S2, "C": C_S2}
        for n_, (kind, idx) in enumerate(units):
            if (kind, idx) == ("C", 0):
                for c_ in (1, 2, 3):
                    slots[c_] = W.get(("in0", c_))
            s1[kind](idx)
            if kind == "P" and idx < 3:
                ada_some(2)
            if (kind, idx) == ("P", 3):
                W.free(slots[0])
            if (kind, idx) == ("C", 3):
                for c_ in (1, 2, 3):
                    W.free(slots[c_])
            if n_ > 0:
                pk, pi = units[n_ - 1]
                s2[pk](pi)
        s2["C"](3)
        out_gemm(st, 0, "out0", lambda k, off, n: mix_v[:, k, off:off + n],
                 lambda k, ti, off, n: actK(k, off, n), 16)

    def mixer1(st):
        tiles = tiles_of(st)
        sv = [W.get(("uv1", 2)), W.get(("uv1", 3))]
        if W.record:
            W.get(("uv1", 0)); W.get(("uv1", 1))
            W.get(("out1", 0)); W.get(("out1", 1))
            return
        nblk = 9 if st == 1 else 8
        for jb in range(nblk):
            ti = jb // 4
            vg = scr[jb % 2]
            for half in range(2):
                bank = newbank()
                terms = [(h_t[:, k, jb * 128:(jb + 1) * 128], wsl[sv[half]][:, k, :]) for k in range(8)]
                gemm(ps[bank][:, :], terms, wK(sv[half]), bank, [[hK(k, ti)] for k in range(8)])
                S.add("act", lambda e, bank=bank, vg=vg, half=half: e.activation(
                    out=vg[:, half * 512:(half + 1) * 512], in_=ps[bank][:, :], func=AF.Gelu_apprx_tanh),
                    reads=[PK(bank)], writes=[("scr", jb % 2, half)])
            S.add("act", lambda e, vg=vg, jb=jb: e.activation(
                out=scrbf(2)[:, 0:1024], in_=vg[:, 0:1024], func=AF.Square, accum_out=ssv_t[:, jb:jb + 1]),
                reads=[scK(jb % 2)], writes=[scK(2), ("ssv", jb)])
            S.add("dve", lambda e, jb=jb: e.tensor_scalar(
                out=rv_t[:, jb:jb + 1], in0=ssv_t[:, jb:jb + 1], scalar1=1.0 / D, scalar2=EPS, op0=ALU.mult, op1=ALU.add),
                reads=[("ssv", jb)], writes=[("rv", jb)])
            S.add("pool", lambda e, jb=jb: e.tensor_tensor(
                out=rv_t[:, jb:jb + 1], in0=rv_t[:, jb:jb + 1], in1=nh_t[:, 0:1], op=ALU.pow),
                reads=[("rv", jb), "nh"], writes=[("rv", jb)])
            S.add("dve", lambda e, vg=vg, jb=jb: e.scalar_tensor_tensor(
                out=vn_v[:, jb, :], in0=vg[:, 0:1024], scalar=rv_t[:, jb:jb + 1], in1=gv_t[:, :], op0=ALU.mult, op1=ALU.mult),
                reads=[scK(jb % 2), ("rv", jb), "gv"], writes=vnK(jb))
            if jb == 8:
                S.add("dve", lambda e, vg=vg, jb=jb: e.scalar_tensor_tensor(
                    out=scr[3][:, 0:1024], in0=vg[:, 0:1024], scalar=rv_t[:, jb:jb + 1], in1=gv_t[:, :],
                    op0=ALU.mult, op1=ALU.mult),
                    reads=[scK(jb % 2), ("rv", jb), "gv"], writes=[scK(3)])
                S.add("sp", lambda e: e.dma_start(out=osg_d[:, :], in_=scr[3][:, 0:1024]),
                      reads=[scK(3)], writes=[("out", "osg")], dma=True)
        W.free(sv[0])
        W.free(sv[1])
        su = [W.get(("uv1", 0)), W.get(("uv1", 1))]
        for hd in range(8):
            for ti, (off, n) in enumerate(tiles):
                bank = newbank()
                sl = su[hd // 4]
                terms = [(wsl[sl][:, k, (hd % 4) * 128:(hd % 4 + 1) * 128], h_t[:, k, off:off + n]) for k in range(8)]
                gemm(ps[bank][:, 0:n], terms, wK(sl), bank, [[hK(k, ti)] for k in range(8)])
                S.add("act", lambda e, bank=bank, hd=hd, off=off, n=n: e.activation(
                    out=scr[hd][:, off:off + n], in_=ps[bank][:, 0:n], func=AF.Gelu_apprx_tanh),
                    reads=[PK(bank)], writes=[("scr", hd, ti)])
        W.free(su[0])
        W.free(su[1])
        for hd in range(8):
            for ti, (off, n) in enumerate(tiles):
                bank = newbank()

                def fn(e, hd=hd, ti=ti, off=off, n=n, bank=bank):
                    ins = None
                    for bi in range(n // 128):
                        jb = off // 128 + bi
                        o = ps[bank][:, bi * 128:(bi + 1) * 128]
                        rhs = wmT_t[:, hd, :] if ti < 2 else wblk_t[:, hd, :]
                        bh = bsh_t if ti < 2 else bssh_t
                        bl = bsl_t if ti < 2 else bssl_t
                        e.matmul(o, lhsT=vn_v[:, jb, hd * 128:(hd + 1) * 128], rhs=rhs, start=True, stop=False)
                        e.matmul(o, lhsT=ones_bf[0:1, :], rhs=bh[0:1, hd * 128:(hd + 1) * 128], start=False, stop=False)
                        ins = e.matmul(o, lhsT=ones_bf[0:1, :], rhs=bl[0:1, hd * 128:(hd + 1) * 128], start=False, stop=True)
                    return ins
                reads = ["wmT", "wblk", "ones", "bsh", "bsl", ("bssh", hd), ("bssl", hd)]
                for bi in range(n // 128):
                    reads += vnK(off // 128 + bi, hd * 128, 128)
                zop = S.add("pe", fn, reads=reads, writes=[PK(bank)])
                zop.nmm = 3 * (n // 128)
                S.add("dve", lambda e, bank=bank, hd=hd, off=off, n=n: e.tensor_tensor(
                    out=h_t[:, hd, off:off + n], in0=ps[bank][:, 0:n], in1=scr[hd][:, off:off + n], op=ALU.mult),
                    reads=[PK(bank), ("scr", hd, ti)], writes=[hK(hd, ti)])
        out_gemm(st, 1, "out1", lambda k, off, n: h_t[:, k, off:off + n],
                 lambda k, ti, off, n: [hK(k, ti)], 16)

    def ffn(st, l):
        tiles = tiles_of(st)
        if not W.record:
            norm_mod(st, l, 1)
            if st == 1:
                S.add("sp", lambda e: e.dma_start(out=stf_t[:, :, :, :], in_=stf_d[l]),
                      writes=[("stf", c) for c in range(NCH_FF)], dma=True)
        cur = {}

        def q_of(j):
            return max(q for q in range(4) if QSTART[q] <= j)

        def abase(q):
            return 0 if q % 2 == 0 else 6

        def group_start(j):
            q = q_of(j)
            return (j - QSTART[q]) % 2 == 0, min(2, QSTART[q + 1] - j)

        def S1(j):
            first, npair = group_start(j)
            if first:
                cur["slot"] = W.get(("up", l, j, npair))
                cur["j0"], cur["np"] = j, npair
                ada_some(1)
            if W.record:
                return
            slot = cur["slot"]
            jj = j - cur["j0"]
            pr = j % 2
            for c, cc in ((0, j), (1, 22 + j)):
                si = (0 if c == 0 else 2) + pr
                ext = scr[si]
                es_ = extcs[c][pr]
                esk = ("extcs", c, pr)
                if st == 0:
                    S.add("pool", lambda e, ext=ext: e.memset(ext[:, 0:2], 0.0), writes=[scK(si)])
                else:
                    S.add("pool", lambda e, ext=ext, cc=cc: e.tensor_copy(out=ext[:, 0:2], in_=cffn_t[:, l, cc, :]),
                          reads=[("cffn", l, cc)], writes=[scK(si)])
                    S.add("pool", lambda e, es_=es_, cc=cc: e.tensor_copy(out=es_[:, :, 0:2], in_=stf_t[:, cc, :, :]),
                          reads=[("stf", cc)], writes=[esk + ("h",)])
                for ti, (off, n) in enumerate(tiles):
                    bank = newbank()
                    col = c * 256 + jj * 128
                    terms = [(wsl[slot][:, k, col:col + 128], h_t[:, k, off:off + n]) for k in range(8)]
                    gemm(ps[bank][:, 0:n], terms, wK(slot), bank, [[hK(k, ti)] for k in range(8)])
                    if ti < 2:
                        S.add("act", lambda e, ext=ext, off=off, bank=bank: e.activation(
                            out=ext[:, 2 + off:2 + off + 512], in_=ps[bank][:, 0:512], func=AF.Copy),
                            reads=[PK(bank)], writes=[("scr", si, ti)])
                    else:
                        S.add("act", lambda e, es_=es_, bank=bank: e.activation(
                            out=es_[:, :, 2:10], in_=v3(ps[bank][:, 0:128]), func=AF.Copy),
                            reads=[PK(bank)], writes=[esk])
            if jj == cur["np"] - 1:
                W.free(slot)

        def S2(j):
            if W.record:
                return
            q = q_of(j)
            jl = abase(q) + j - QSTART[q]
            pr = j % 2
            for c, cc in ((0, j), (1, 22 + j)):
                si = (0 if c == 0 else 2) + pr
                yi = (4 if c == 0 else 6) + pr
                ext, y = scr[si], scr[yi]
                es_, ys_ = extcs[c][pr], yss[c][pr]
                esk, ysk = ("extcs", c, pr), ("yss", c, pr)
                wb = P_FCW + (l * NCH_FF + cc) * 3
                bb = P_FCB + l * NCH_FF + cc
                S.add("dve", lambda e, ext=ext, y=y, wb=wb, bb=bb: e.tensor_scalar(
                    out=y[:, 0:1024], in0=ext[:, 2:1026], scalar1=par(wb + 2), scalar2=par(bb),
                    op0=ALU.mult, op1=ALU.add),
                    reads=[scK(si), "par"], writes=[scK(yi)])
                for tap in (1, 0):
                    S.add("dve", lambda e, ext=ext, y=y, wb=wb, tap=tap: e.scalar_tensor_tensor(
                        out=y[:, 0:1024], in0=ext[:, tap:tap + 1024], scalar=par(wb + tap), in1=y[:, 0:1024],
                        op0=ALU.mult, op1=ALU.add),
                        reads=[scK(si), scK(yi), "par"], writes=[scK(yi)])
                if st == 0:
                    S.add("pool", lambda e, ext=ext, cc=cc: e.tensor_copy(out=cffn_t[:, l, cc, :], in_=ext[:, 1024:1026]),
                          reads=[scK(si)], writes=[("cffn", l, cc)])
                else:
                    S.add("pool", lambda e, ext=ext, cc=cc: e.tensor_copy(out=ffp_t[:, cc, :], in_=ext[:, 1024:1026]),
                          reads=[scK(si)], writes=[("ffp", cc)])
                if st == 1:
                    S.add("act", lambda e, es_=es_, ys_=ys_, wb=wb, bb=bb: e.activation(
                        out=ys_[:, :, :], in_=es_[:, :, 2:10], func=AF.Identity, scale=par(wb + 2), bias=par(bb)),
                        reads=[esk, "par"], writes=[ysk])
                    for tap in (1, 0):
                        S.add("dve", lambda e, es_=es_, ys_=ys_, wb=wb, tap=tap: e.scalar_tensor_tensor(
                            out=ys_[:, :, :], in0=es_[:, :, tap:tap + 8], scalar=par(wb + tap), in1=ys_[:, :, :],
                            op0=ALU.mult, op1=ALU.add),
                            reads=[esk, esk + ("h",), ysk, "par"], writes=[ysk])
                    S.add("pool", lambda e, es_=es_, cc=cc: e.tensor_copy(out=stf_t[:, cc, :, :], in_=es_[:, :, 8:10]),
                          reads=[esk, esk + ("h",)], writes=[("stf", cc)])
                if c == 0:
                    S.add("act", lambda e, y=y: e.activation(out=y[:, 0:1024], in_=y[:, 0:1024], func=AF.Gelu_apprx_tanh),
                          reads=[scK(yi)], writes=[scK(yi)])
                    if st == 1:
                        S.add("act", lambda e, ys_=ys_: e.activation(out=ys_[:, :, :], in_=ys_[:, :, :], func=AF.Gelu_apprx_tanh),
                              reads=[ysk], writes=[ysk])
            yg, yv = scr[4 + pr], scr[6 + pr]
            S.add("dve", lambda e, yg=yg, yv=yv, jl=jl: e.tensor_tensor(
                out=act_v[:, jl, 0:1024], in0=yg[:, 0:1024], in1=yv[:, 0:1024], op=ALU.mult),
                reads=[scK(4 + pr), scK(6 + pr)], writes=actK(jl, 0, 1024))
            if st == 1:
                S.add("dve", lambda e, pr=pr, jl=jl: e.tensor_tensor(
                    out=v3(act_v[:, jl, 1024:1152]), in0=yss[0][pr][:, :, :], in1=yss[1][pr][:, :, :], op=ALU.mult),
                    reads=[("yss", 0, pr), ("yss", 1, pr)], writes=actK(jl, 1024, 128))

        pending = []
        dstate = {}

        def down_group(q, cg, ti, mm):
            key = (q, cg)
            if key not in dstate:
                dstate[key] = [W.get(("down", l, q, cg)), 0]
            slot = dstate[key][0]
            dstate[key][1] += 1
            last = dstate[key][1] == 4 * len(tiles)
            if not W.record:
                nk = QSTART[q + 1] - QSTART[q]
                off, n = tiles[ti]
                m = cg * 4 + mm
                bank = newbank()
                terms = [(dsl[slot][:, jl, mm * 128:(mm + 1) * 128], act_v[:, abase(q) + jl, off:off + n]) for jl in range(nk)]
                gemm(ps[bank][:, 0:n], terms, [("d", slot)], bank, [actK(abase(q) + jl, off, n) for jl in range(nk)])
                residual(st, bank, m, ti, off, n, l, 40)
            if last and not W.record:
                W.free(slot)

        def push_down(q):
            if q == 3:
                for ti in range(len(tiles)):
                    for cg in range(2):
                        for mm in range(4):
                            pending.append((q, cg, ti, mm))
                return
            for cg in range(2):
                for mm in range(4):
                    for ti in range(len(tiles)):
                        pending.append((q, cg, ti, mm))

        def emit_down(k):
            for _ in range(min(k, len(pending))):
                down_group(*pending.pop(0))

        per_step = -(-8 * len(tiles) // 5) + 1
        def flush_quarter(qq):
            while pending and pending[0][0] <= qq:
                down_group(*pending.pop(0))

        for j in range(22):
            S1(j)
            if j > 0:
                qp = q_of(j - 1)
                if j - 1 == QSTART[qp] and qp >= 2:
                    flush_quarter(qp - 2)
                S2(j - 1)
            emit_down(per_step)
            q = q_of(j)
            if q > 0 and j == QSTART[q] + 1:
                push_down(q - 1)
        S2(21)
        push_down(3)
        emit_down(10 ** 6)
        ada_some(2)
        if st == 1 and not W.record:
            S.add("sp", lambda e: e.dma_start(out=ofp_d[l], in_=ffp_t[:, :, :]),
                  reads=[("ffp", c) for c in range(NCH_FF)], writes=[("out", "ofp", l)], dma=True)
            S.add("sp", lambda e: e.dma_start(out=ofs_d[l], in_=stf_t[:, :, :, :]),
                  reads=[("stf", c) for c in range(NCH_FF)], writes=[("out", "ofs", l)], dma=True)

    ADA = {"pending": []}

    def ada_some(n):
        for _ in range(n):
            if ADA["pending"]:
                l, sidx = ADA["pending"].pop(0)
                adaln(l, [sidx])

    def program():
        ADA["pending"] = [(0, s_) for s_ in range(4, 12)] + [(1, s_) for s_ in range(12)]
        if not W.record:
            setup_a()
        pre = W.get(("ada", 0, 0))
        if not W.record:
            W.unget(pre)
            setup_b()
        if not W.record:
            load_x(0)
            sumsq_rstd(0)
        adaln(0, range(0, 4))
        for st in range(2):
            S.phase = f"st{st}.norm0"
            if not W.record:
                if st == 0:
                    norm_mod(st, 0, 0, do_stats=False)
                else:
                    load_x(st)
                    norm_mod(st, 0, 0)
            S.phase = f"st{st}.mixer0"
            mixer0(st)
            S.phase = f"st{st}.ffn0"
            ffn(st, 0)
            S.phase = f"st{st}.norm1"
            if not W.record:
                norm_mod(st, 1, 0)
            S.phase = f"st{st}.mixer1"
            mixer1(st)
            S.phase = f"st{st}.ffn1"
            ffn(st, 1)
            S.phase = f"st{st}.final"
            if not W.record:
                final_norm(st)

    program()
    W.record = False
    bank_ctr[0] = 0
    program()

    out_keys = [k for k in S.lw if isinstance(k, tuple) and k[0] == "out"]
    S.add("sp", lambda e: None, reads=out_keys)
    S.resolve()

    sems = {e: es.enter_context(nc.semaphore(f"sem_{e}")) for e in Sched.CE}
    lane_sems = {}
    for q, n in nlanes.items():
        for i in range(n):
            lane_sems[(q, i)] = es.enter_context(nc.semaphore(f"dl_{q}{i}"))
    block = es.enter_context(nc.Block())

    names = {"pe": "tensor", "act": "scalar", "dve": "vector", "pool": "gpsimd", "sp": "sync"}
    for eng in Sched.ENGS:
        ops = S.eng_ops[eng]

        def body(e, ops=ops, eng=eng):
            for op in ops:
                for y in op.need:
                    if y.dma:
                        e.wait_ge(lane_sems[(y.eng, y.lane)], 16 * (y.gen + 1))
                    else:
                        e.wait_ge(sems[y.eng], y.val)
                ins = op.fn(e)
                if ins is None:
                    continue
                if op.dma:
                    ins.then_inc(lane_sems[(eng, op.lane)], 16)
                elif op.sig:
                    ins.then_inc(sems[eng], 1)

        getattr(block, names[eng])(body)
    es.close()
    global _LAST_SCHED
    _LAST_SCHED = S
    return nc


def _fm(v, nch):
    return np.ascontiguousarray(np.asarray(v, np.float32).reshape(nch, 128).T)


_PROG = None
_LAST_SCHED = None


def kernel(x_prompt, x_sample, state_pool, state_sconv, state_ffn, c_prompt, c_sample,
           norm_mix, norm_ffn, w_ada, b_ada, w_in0, w_pool, s_pool, sconv_w, sconv_b,
           w_out0, w_uv1, g_v1, w_s1, b_s1, w_out1, ffn_up, ffn_conv_w, ffn_conv_b,
           ffn_down, norm_final):
    global _PROG
    f32 = np.float32
    A = lambda a: np.ascontiguousarray(np.asarray(a, f32))
    x_prompt, x_sample = A(x_prompt), A(x_sample)
    state_pool, state_sconv, state_ffn = A(state_pool), A(state_sconv), A(state_ffn)
    c_prompt, c_sample = A(c_prompt), A(c_sample)

    par = np.zeros((128, NPAR), f32)
    for l in range(2):
        par[:, P_NM + l * 8:P_NM + l * 8 + 8] = _fm(norm_mix[l], 8)
        par[:, P_NF + l * 8:P_NF + l * 8 + 8] = _fm(norm_ffn[l], 8)
        par[:, P_BADA + l * 48:P_BADA + l * 48 + 48] = _fm(b_ada[l], 48)
        fcw = np.asarray(ffn_conv_w[l], f32)
        par[:, P_FCW + l * 132:P_FCW + (l + 1) * 132] = \
            fcw.reshape(3, NCH_FF, 128).transpose(2, 1, 0).reshape(128, 132)
        par[:, P_FCB + l * 44:P_FCB + (l + 1) * 44] = _fm(ffn_conv_b[l], 44)
    par[:, P_NFIN:P_NFIN + 8] = _fm(norm_final, 8)
    par[:, P_SPOOL:P_SPOOL + 4] = _fm(s_pool[0], 4)
    scw = np.asarray(sconv_w[0], f32)
    par[:, P_SCW:P_SCW + 12] = scw.reshape(3, 4, 128).transpose(2, 1, 0).reshape(128, 12)
    par[:, P_SCB:P_SCB + 4] = _fm(sconv_b[0], 4)

    gv = np.ascontiguousarray(np.broadcast_to(np.asarray(g_v1[0], f32)[None, :], (128, D)))
    ws1 = np.ascontiguousarray(np.asarray(w_s1[0], f32).transpose(2, 0, 1))
    bs1 = np.ascontiguousarray(np.asarray(b_s1[0], f32).reshape(1, D))
    wpool = np.ascontiguousarray(np.asarray(w_pool[0], f32).transpose(1, 0, 2))
    shared = {
        "par": par, "gv": gv, "ws1": ws1, "bs1": bs1, "w_pool": wpool,
        "w_ada": A(w_ada), "w_in0": A(w_in0[0]), "w_out0": A(w_out0[0]), "w_uv1": A(w_uv1[0]),
        "w_out1": A(w_out1[0]), "ffn_up": A(ffn_up), "ffn_down": A(ffn_down),
    }
    in_maps = []
    for i in range(NCORES):
        sl = slice(16 * i, 16 * i + 16)
        xp = x_prompt[i].T.reshape(8, 128, SEQ).transpose(1, 0, 2)
        xs = x_sample[sl].reshape(128, D).T.reshape(8, 128, 128).transpose(1, 0, 2)
        c17 = np.concatenate([c_prompt[i:i + 1], c_sample[sl]], axis=0)
        ct = c17.T.reshape(8, 128, 17).transpose(1, 0, 2)
        stp = state_pool[0, sl].transpose(2, 0, 1).reshape(4, 128, 16, 15).transpose(1, 0, 2, 3)
        sts = state_sconv[0, sl].transpose(2, 0, 1).reshape(4, 128, 16, 2).transpose(1, 0, 2, 3)
        stf = state_ffn[:, sl].transpose(0, 3, 1, 2).reshape(2, NCH_FF, 128, 16, 2).transpose(0, 2, 1, 3, 4)
        m = dict(shared)
        m.update({"xp": A(xp), "xs": A(xs), "ct": A(ct), "stp": A(stp), "sts": A(sts), "stf": A(stf)})
        in_maps.append(m)

    if _PROG is None:
        _PROG = build_program()
    res = run_bass_kernel_spmd(_PROG, in_maps, core_ids=list(range(NCORES)))
    R = res.results

    y_prompt = np.empty((8, SEQ, D), f32)
    y_sample = np.empty((128, 8, D), f32)
    pool_prompt = np.empty((1, 8, 15, 512), f32)
    pool_sample = np.empty((1, 128, 15, 512), f32)
    sconv_prompt = np.empty((1, 8, 2, 512), f32)
    sconv_sample = np.empty((1, 128, 2, 512), f32)
    ffn_prompt = np.empty((2, 8, 2, 2 * DFF), f32)
    ffn_sample = np.empty((2, 128, 2, 2 * DFF), f32)
    sg_v_sample = np.empty((1, 128, 8, D), f32)
    for i in range(NCORES):
        r = R[i]
        sl = slice(16 * i, 16 * i + 16)
        y_prompt[i] = r["yp"].transpose(1, 0, 2).reshape(D, SEQ).T
        y_sample[sl] = r["ys"].transpose(1, 0, 2).reshape(D, 128).T.reshape(16, 8, D)
        pool_prompt[0, i] = r["o_pool_p"].transpose(1, 0, 2).reshape(512, 15).T
        pool_sample[0, sl] = r["o_pool_s"].transpose(1, 0, 2, 3).reshape(512, 16, 15).transpose(1, 2, 0)
        sconv_prompt[0, i] = r["o_sconv_p"].transpose(1, 0, 2).reshape(512, 2).T
        sconv_sample[0, sl] = r["o_sconv_s"].transpose(1, 0, 2, 3).reshape(512, 16, 2).transpose(1, 2, 0)
        ffn_prompt[:, i] = r["o_ffn_p"].transpose(0, 2, 1, 3).reshape(2, 2 * DFF, 2).transpose(0, 2, 1)
        ffn_sample[:, sl] = r["o_ffn_s"].transpose(0, 2, 1, 3, 4).reshape(2, 2 * DFF, 16, 2).transpose(0, 2, 3, 1)
        sg_v_sample[0, sl] = r["o_sgv"].reshape(16, 8, D)
    return (y_prompt, y_sample, pool_prompt, pool_sample, sconv_prompt, sconv_sample,
            ffn_prompt, ffn_sample, sg_v_sample)
```

```python
import numpy as np
from contextlib import ExitStack
import concourse.bass as bass
import concourse.mybir as mybir
from concourse.bass_utils import run_bass_kernel_spmd

F32 = mybir.dt.float32
BF16 = mybir.dt.bfloat16
AF = mybir.ActivationFunctionType
ALU = mybir.AluOpType

NCORES = 8
D = 1024
SEQ = 2048
DFF = 2816
NCH_FF = 44
EPS = 1e-6
NS_W = 4
NS_D = 2
PREFETCH = 3
QSTART = (0, 6, 11, 17, 22)

P_NM, P_NF, P_NFIN, P_BADA, P_SPOOL, P_SCW, P_SCB, P_FCW, P_FCB = 0, 16, 32, 40, 136, 140, 152, 156, 420
NPAR = 508


class _Op:
    __slots__ = ("eng", "fn", "deps", "lidx", "sig", "val", "dma", "lane", "gen", "need", "phase", "nmm")


class Sched:
    ENGS = ("pe", "act", "dve", "pool", "sp")
    CE = ("pe", "act", "dve", "pool")

    def __init__(self, nlanes):
        self.ops = []
        self.eng_ops = {e: [] for e in self.ENGS}
        self.lw = {}
        self.rd = {}
        self.nlanes = nlanes
        self.dma_count = {q: 0 for q in nlanes}
        self.lane_last = {}
        self.phase = "setup"

    @staticmethod
    def _expand(keys):
        out = []
        for k in keys:
            if isinstance(k, tuple) and k[0] == "scr" and len(k) == 2:
                out += [("scr", k[1], 0), ("scr", k[1], 1), ("scr", k[1], 2)]
            else:
                out.append(k)
        return out

    def add(self, eng, fn, reads=(), writes=(), dma=False):
        reads = self._expand(reads)
        writes = self._expand(writes)
        op = _Op()
        op.phase, op.nmm = self.phase, 1
        op.eng, op.fn, op.dma, op.sig, op.val = eng, fn, dma, False, None
        op.lidx = len(self.eng_ops[eng])
        deps = {}
        for k in reads:
            w = self.lw.get(k)
            if w is not None:
                deps[w] = True
        for k in writes:
            w = self.lw.get(k)
            if w is not None and w not in deps:
                deps[w] = False
            for r in self.rd.get(k, ()):
                if r not in deps:
                    deps[r] = False
        if dma:
            nl = self.nlanes[eng]
            i = self.dma_count[eng]
            self.dma_count[eng] += 1
            op.lane, op.gen = i % nl, i // nl
            prev = self.lane_last.get((eng, op.lane))
            if prev is not None:
                deps[prev] = True
            self.lane_last[(eng, op.lane)] = op
        op.deps = deps
        for k in reads:
            self.rd.setdefault(k, []).append(op)
        for k in writes:
            self.lw[k] = op
            self.rd[k] = []
        self.ops.append(op)
        self.eng_ops[eng].append(op)
        return op

    def resolve(self):
        waited = {e: {} for e in self.ENGS}
        for op in self.ops:
            F = op.eng
            best = {}
            for y, raw in op.deps.items():
                if y.dma:
                    key = ("dma", y.eng, y.lane)
                    if key not in best or best[key].gen < y.gen:
                        best[key] = y
                else:
                    E = y.eng
                    if E == F and not op.dma:
                        if F == "pe":
                            continue
                    if E not in best or best[E].lidx < y.lidx:
                        best[E] = y
            need = []
            for key, y in best.items():
                if y.dma:
                    if waited[F].get(key, -1) >= y.gen:
                        continue
                    waited[F][key] = y.gen
                else:
                    if waited[F].get(key, -1) >= y.lidx:
                        continue
                    waited[F][key] = y.lidx
                    y.sig = True
                need.append(y)
            op.need = need
        for E in self.CE:
            c = 0
            for op in self.eng_ops[E]:
                if op.sig and not op.dma:
                    c += 1
                    op.val = c

    def emit(self, block, handles, sems, lane_sems):
        names = {"pe": "tensor", "act": "scalar", "dve": "vector", "pool": "gpsimd", "sp": "sync"}
        for eng in self.ENGS:
            ops = self.eng_ops[eng]

            def body(e, ops=ops, eng=eng):
                for op in ops:
                    for y in op.need:
                        if y.dma:
                            e.wait_ge(lane_sems[(y.eng, y.lane)], 16 * (y.gen + 1))
                        else:
                            e.wait_ge(sems[y.eng], y.val)
                    ins = op.fn(e)
                    if op.dma:
                        ins.then_inc(lane_sems[(eng, op.lane)], 16)
                    elif op.sig:
                        ins.then_inc(sems[eng], 1)

            getattr(block, names[eng])(body)


def build_program():
    nc = bass.Bass("TRN2", target_bir_lowering=False)

    def din(name, shape):
        return nc.dram_tensor(name, list(shape), F32, kind="ExternalInput").ap()

    def dout(name, shape):
        return nc.dram_tensor(name, list(shape), F32, kind="ExternalOutput").ap()

    xp_d = din("xp", [128, 8, SEQ])
    xs_d = din("xs", [128, 8, 128])
    ct_d = din("ct", [128, 8, 17])
    par_d = din("par", [128, NPAR])
    stp_d = din("stp", [128, 4, 16, 15])
    sts_d = din("sts", [128, 4, 16, 2])
    stf_d = din("stf", [2, 128, NCH_FF, 16, 2])
    gv_d = din("gv", [128, D])
    ws1_d = din("ws1", [128, 8, 128])
    bs1_d = din("bs1", [1, D])
    wada_d = din("w_ada", [2, D, 6 * D])
    win0_d = din("w_in0", [D, 2 * D])
    wpool_d = din("w_pool", [128, 4, 128])
    wout0_d = din("w_out0", [D, D])
    wuv1_d = din("w_uv1", [D, 2 * D])
    wout1_d = din("w_out1", [D, D])
    fup_d = din("ffn_up", [2, D, 2 * DFF])
    fdn_d = din("ffn_down", [2, DFF, D])

    yp_d = dout("yp", [128, 8, SEQ])
    ys_d = dout("ys", [128, 8, 128])
    opp_d = dout("o_pool_p", [128, 4, 15])
    ops_d = dout("o_pool_s", [128, 4, 16, 15])
    osp_d = dout("o_sconv_p", [128, 4, 2])
    oss_d = dout("o_sconv_s", [128, 4, 16, 2])
    ofp_d = dout("o_ffn_p", [2, 128, NCH_FF, 2])
    ofs_d = dout("o_ffn_s", [2, 128, NCH_FF, 16, 2])
    osg_d = dout("o_sgv", [128, D])

    es = ExitStack()

    def sb(name, shape, dt=F32):
        return es.enter_context(nc.sbuf_tensor(name, list(shape), dt))

    SW = 1152
    x_t = sb("x_t", [128, 8, SW])
    h_t = sb("h_t", [128, 8, SW], BF16)
    R_t = sb("R_t", [128, 11 * SW], BF16)
    NSCR = 8
    SCRW = 1152
    scr = [sb(f"scr{i}", [128, SCRW]) for i in range(NSCR)]
    wsl = [sb(f"wsl{i}", [128, 8, 512], BF16) for i in range(NS_W)]
    dsl = [sb(f"dsl{i}", [128, 6, 512], BF16) for i in range(NS_D)]
    par_t = sb("par_t", [128, NPAR])
    ct_t = sb("ct_t", [128, 8, 17])
    sc_t = sb("sc_t", [128, 8, 17], BF16)
    mod_t = sb("mod_t", [128, 2, 48, 17])
    A_t = sb("A_t", [128, 2, 2, 8, 17])
    ones_bf = sb("ones_bf", [128, 128], BF16)
    nh_t = sb("nh_t", [128, 1])
    eps_t = sb("eps_t", [128, 1])
    dmy_t = sb("dmy_t", [128, 1])
    rc_t = sb("rc_t", [128, 16])
    gv_t = sb("gv_t", [128, D])
    wpool_t = sb("wpool_t", [128, 4, 128], BF16)
    wmT_t = sb("wmT_t", [128, 8, 128], BF16)
    wblk_t = sb("wblk_t", [128, 8, 128], BF16)
    bsh_t = sb("bsh_t", [1, D], BF16)
    bsl_t = sb("bsl_t", [1, D], BF16)
    bssh_t = sb("bssh_t", [1, D], BF16)
    bssl_t = sb("bssl_t", [1, D], BF16)
    exts2 = [sb(f"exts{i}", [128, 16, 23]) for i in range(2)]
    sps = [sb(f"sps{i}", [128, 16, 23]) for i in range(2)]
    extus2 = [sb(f"extus{i}", [128, 16, 10]) for i in range(2)]
    ycs_t = sb("ycs_t", [128, 16, 8])
    extcs = [[sb(f"extcs{c}{p}", [128, 16, 10]) for p in range(2)] for c in range(2)]
    yss = [[sb(f"yss{c}{p}", [128, 16, 8]) for p in range(2)] for c in range(2)]
    tms_t = sb("tms_t", [128, 16, 8])
    stf_t = sb("stf_t", [128, NCH_FF, 16, 2])
    cpool_t = sb("cpool_t", [128, 4, 15])
    csc_t = sb("csc_t", [128, 4, 2])
    cffn_t = sb("cffn_t", [128, 2, NCH_FF, 2])
    opp_t = sb("opp_t", [128, 4, 15])
    osp_t = sb("osp_t", [128, 4, 2])
    ffp_t = sb("ffp_t", [128, NCH_FF, 2])
    ssv_t = sb("ssv_t", [128, 16])
    rv_t = sb("rv_t", [128, 16])
    fixt = sb("fixt", [128, 16])

    ps = [es.enter_context(nc.psum_tensor(f"ps{i}", [128, 512], F32)) for i in range(8)]

    nlanes = {"sp": 8, "pool": 8}
    S = Sched(nlanes)
    bank_ctr = [0]

    def newbank():
        b = bank_ctr[0] % 8
        bank_ctr[0] += 1
        return b

    def PK(b):
        return ("ps", b)

    def RK(b0, nb):
        return [("R", g) for g in range(b0 // 256, (b0 + nb + 255) // 256)]

    act_v = R_t[:, :].rearrange("p (j s) -> p j s", s=SW)
    vn_v = R_t[:, 0:9 * D].rearrange("p (j d) -> p j d", d=D)

    def actK(j, off, n):
        return RK(j * SW * 2 + off * 2, n * 2)

    def vnK(jb, d0=0, nd=D):
        return RK(jb * D * 2 + d0 * 2, nd * 2)

    bsf_t = scr[5][0:1, 0:D]
    bshf_t = scr[4][0:1, 0:D]

    def scrbf(i):
        return scr[i][:, :].bitcast(BF16)

    def par(c0, n=1):
        return par_t[:, c0:c0 + n]

    def v3(ap):
        return ap.rearrange("p (b s) -> p b s", s=8)

    def bc3(ap16):
        return ap16.unsqueeze(2).to_broadcast([128, 16, 8])

    def wK(s):
        return [("w", s, 0), ("w", s, 1)]

    class SlotRef(int):
        pass

    class WStream:
        def __init__(self):
            self.specs = []
            self.record = True
            self.i = 0
            self.issued = 0
            self.busy = {"w": [None] * NS_W, "d": [None] * NS_D}
            self.slot_of = {}

        @staticmethod
        def typ(spec):
            return "d" if spec[0] == "down" else "w"

        def _pump(self):
            while self.issued < min(len(self.specs), self.i + PREFETCH):
                t = self.typ(self.specs[self.issued])
                if None not in self.busy[t]:
                    break
                self._issue(self.issued)
                self.issued += 1

        def get(self, spec):
            if self.record:
                if self.specs and self.specs[-1] == spec and spec == ("ada", 0, 0):
                    return 0
                self.specs.append(spec)
                return 0
            i = self.i
            self.i += 1
            assert self.specs[i] == spec, (self.specs[i], spec)
            self._pump()
            assert self.issued > i, ("weight slot ring exhausted", spec)
            r = SlotRef(self.slot_of[i])
            r.item = i
            return r

        def unget(self, ref):
            self.i -= 1

        def free(self, ref):
            if self.record:
                return
            t = self.typ(self.specs[ref.item])
            assert self.busy[t][int(ref)] == ref.item
            self.busy[t][int(ref)] = None
            self._pump()

        def _issue(self, j):
            spec = self.specs[j]
            typ = self.typ(spec)
            s = self.busy[typ].index(None)
            self.busy[typ][s] = j
            self.slot_of[j] = s
            kind = spec[0]
            if kind == "down":
                _, l, q, cg = spec
                j0, j1 = QSTART[q], QSTART[q + 1]
                src = fdn_d[l].rearrange("(j p) n -> p j n", p=128)[:, j0:j1, cg * 512:(cg + 1) * 512]
                dst = dsl[s][:, 0:j1 - j0, :]
                S.add("pool", lambda e, d=dst, r=src: e.dma_start(out=d, in_=r), writes=[("d", s)], dma=True)
            elif kind == "up":
                _, l, j0, npair = spec
                w = fup_d[l].rearrange("(k p) n -> p k n", p=128)
                for c in range(2):
                    src = w[:, :, c * DFF + j0 * 128: c * DFF + (j0 + npair) * 128]
                    dst = wsl[s][:, :, c * 256: c * 256 + npair * 128]
                    S.add("pool", lambda e, d=dst, r=src: e.dma_start(out=d, in_=r), writes=[("w", s, c)], dma=True)
            else:
                if kind == "ada":
                    w = wada_d[spec[1]]
                else:
                    w = {"in0": win0_d, "out0": wout0_d, "uv1": wuv1_d, "out1": wout1_d}[kind]
                c = spec[-1]
                src = w.rearrange("(k p) n -> p k n", p=128)[:, :, c * 512:(c + 1) * 512]
                dst = wsl[s][:, :, :]
                S.add("pool", lambda e, d=dst, r=src: e.dma_start(out=d, in_=r), writes=wK(s), dma=True)

    W = WStream()

    def gemm(out_ap, terms, reads, bank, per_term_reads=None):
        n = len(terms)
        for i, (l, r) in enumerate(terms):
            rk = list(reads) + (list(per_term_reads[i]) if per_term_reads is not None else [])
            S.add("pe", lambda e, l=l, r=r, i=i: e.matmul(out_ap, lhsT=l, rhs=r, start=(i == 0), stop=(i == n - 1)),
                  reads=rk, writes=[PK(bank)])

    def tiles_of(st):
        return [(0, 512), (512, 512)] + ([(1024, 128)] if st == 1 else [])

    def hK(k, ti):
        return ("h", k, ti)

    def xK(k, ti):
        return ("x", k, ti)

    def scK(i, ti=None):
        return ("scr", i)

    def modp(l, c):
        return mod_t[:, l, c, 0:1]

    def mods(l, c):
        return mod_t[:, l, c, 1:17]

    def setup_a():
        S.add("sp", lambda e: e.dma_start(out=par_t[:, :], in_=par_d[:, :]), writes=["par"], dma=True)
        S.add("sp", lambda e: e.dma_start(out=ct_t[:, :, :], in_=ct_d[:, :, :]), writes=["ct"], dma=True)
        S.add("act", lambda e: e.activation(out=sc_t[:, :, :], in_=ct_t[:, :, :], func=AF.Silu),
              reads=["ct"], writes=["sc"])

    def setup_b():
        S.add("sp", lambda e: e.dma_start(out=gv_t[:, :], in_=gv_d[:, :]), writes=["gv"], dma=True)
        S.add("sp", lambda e: e.dma_start(out=bsf_t, in_=bs1_d[:, :]), writes=["bsf", scK(5)], dma=True)
        ws_stage = scr[7][:, 0:1024].rearrange("p (h t) -> p h t", t=128)
        blk_stage = scr[6][:, 0:1024].rearrange("p (h t) -> p h t", t=128)
        S.add("sp", lambda e: e.dma_start(out=ws_stage, in_=ws1_d[:, :, :]), writes=[scK(7)], dma=True)
        S.add("pool", lambda e: e.memset(ones_bf[:, :], 1.0), writes=["ones"])
        S.add("pool", lambda e: e.memset(nh_t[:, :], -0.5), writes=["nh"])
        S.add("pool", lambda e: e.memset(eps_t[:, :], EPS), writes=["eps"])
        for t in range(15):
            S.add("pool", lambda e, t=t: e.memset(rc_t[:, t:t + 1], 1.0 / (t + 1)), writes=["rc"])
        S.add("pool", lambda e: e.memset(scr[6][:, :], 0.0), writes=[scK(6)])
        for b in range(16):
            S.add("sp", lambda e, b=b: e.dma_start(out=blk_stage[b * 8:(b + 1) * 8, :, b * 8:(b + 1) * 8],
                                                   in_=ws1_d[0:8, :, 0:8]),
                  writes=[("scr6b", b)], reads=[scK(6)], dma=True)
        S.add("pool", lambda e: e.affine_select(out=wmT_t[:, :, :], in_=ws_stage, pattern=[[0, 8], [1, 128]],
                                                compare_op=ALU.is_ge, fill=0.0, base=0, channel_multiplier=-1),
              reads=[scK(7)], writes=["wmT"])
        S.add("pool", lambda e: e.affine_select(out=wblk_t[:, :, :], in_=blk_stage, pattern=[[0, 8], [1, 128]],
                                                compare_op=ALU.is_ge, fill=0.0, base=0, channel_multiplier=-1),
              reads=[scK(6)] + [("scr6b", b) for b in range(16)], writes=["wblk"])
        S.add("dve", lambda e: e.tensor_copy(out=bsh_t[:, :], in_=bsf_t), reads=["bsf", scK(5)], writes=["bsh"])
        S.add("dve", lambda e: e.tensor_copy(out=bshf_t, in_=bsh_t[:, :]), reads=["bsh"], writes=["bshf", scK(4)])
        S.add("dve", lambda e: e.tensor_tensor(out=bsl_t[:, :], in0=bsf_t, in1=bshf_t, op=ALU.subtract),
              reads=["bsf", "bshf", scK(4), scK(5)], writes=["bsl"])
        for hd in range(8):
            for src, dst, kn in ((bsh_t, bssh_t, "bssh"), (bsl_t, bssl_t, "bssl")):
                S.add("dve", lambda e, hd=hd, src=src, dst=dst: e.tensor_copy(
                    out=dst[0:1, hd * 128:(hd + 1) * 128].rearrange("p (b s) -> p b s", s=8),
                    in_=src[0:1, hd * 128:hd * 128 + 8].unsqueeze(1).to_broadcast([1, 16, 8])),
                    reads=["bsh", "bsl"], writes=[(kn, hd)])
        S.add("pool", lambda e: e.dma_start(out=wpool_t[:, :, :], in_=wpool_d[:, :, :]), writes=["wpool"], dma=True)

    def adaln(l, srange):
        for s in srange:
            slot = W.get(("ada", l, s))
            if W.record:
                continue
            bank = newbank()
            for j in range(4):
                for k in range(8):
                    S.add("pe", lambda e, slot=slot, bank=bank, j=j, k=k: e.matmul(
                        ps[bank][:, j * 17:(j + 1) * 17], lhsT=wsl[slot][:, k, j * 128:(j + 1) * 128],
                        rhs=sc_t[:, k, :], start=(k == 0), stop=(k == 7)),
                        reads=wK(slot) + ["sc"], writes=[PK(bank)])
            S.add("dve", lambda e, s=s, bank=bank, l=l: e.tensor_tensor(
                out=mod_t[:, l, 4 * s:4 * s + 4, :],
                in0=ps[bank][:, 0:68].rearrange("p (j n) -> p j n", n=17),
                in1=par_t[:, P_BADA + l * 48 + 4 * s: P_BADA + l * 48 + 4 * s + 4].unsqueeze(2).to_broadcast([128, 4, 17]),
                op=ALU.add), reads=[PK(bank), "par"], writes=[("mod", l, s)])
            W.free(slot)
            if W.record:
                continue
            for which, (cbase, pbase) in enumerate(((8, P_NM), (32, P_NF))):
                if s == cbase // 4 + 1:
                    S.add("dve", lambda e, which=which, cbase=cbase, pbase=pbase, l=l: e.scalar_tensor_tensor(
                        out=A_t[:, l, which, :, :], in0=mod_t[:, l, cbase:cbase + 8, :], scalar=1.0,
                        in1=par_t[:, pbase + l * 8: pbase + l * 8 + 8].unsqueeze(2).to_broadcast([128, 8, 17]),
                        op0=ALU.add, op1=ALU.mult),
                        reads=[("mod", l, cbase // 4), ("mod", l, cbase // 4 + 1), "par"], writes=[("A", l, which)])

    def load_x(st):
        if st == 0:
            S.add("sp", lambda e: e.dma_start(out=x_t[:, :, 1024:1152], in_=xs_d[:, :, :]),
                  writes=[xK(k, 2) for k in range(8)], dma=True)
        for ti in range(2):
            for k in range(8):
                S.add("sp", lambda e, k=k, st=st, ti=ti: e.dma_start(
                    out=x_t[:, k, ti * 512:(ti + 1) * 512], in_=xp_d[:, k, st * 1024 + ti * 512: st * 1024 + (ti + 1) * 512]),
                    writes=[xK(k, ti)], dma=True)

    def sumsq_rstd(st):
        tiles = tiles_of(st)
        S.add("act", lambda e: e.activation(out=dmy_t[:, 0:1], in_=eps_t[:, 0:1], func=AF.Sqrt), reads=["eps"], writes=["dmy"])
        for ti, (off, n) in enumerate(tiles):
            b = newbank()
            for k in range(8):
                sq = scrbf(k % 2)
                S.add("act", lambda e, k=k, sq=sq, off=off, n=n: e.activation(
                    out=sq[:, off:off + n], in_=x_t[:, k, off:off + n], func=AF.Square),
                    reads=[xK(k, ti)], writes=[("scr", k % 2, ti)])
                S.add("pe", lambda e, k=k, sq=sq, off=off, n=n, b=b: e.matmul(
                    ps[b][:, 0:n], lhsT=ones_bf[:, :], rhs=sq[:, off:off + n], start=(k == 0), stop=(k == 7)),
                    reads=[("scr", k % 2, ti), "ones"], writes=[PK(b)])
            S.add("act", lambda e, off=off, n=n, b=b: e.activation(
                out=scr[2][:, off:off + n], in_=ps[b][:, 0:n], func=AF.Sqrt, scale=1.0 / D, bias=eps_t[:, 0:1]),
                reads=[PK(b), "eps"], writes=[("scr", 2, ti)])
            S.add("dve", lambda e, off=off, n=n: e.reciprocal(out=scr[2][:, off:off + n], in_=scr[2][:, off:off + n]),
                  reads=[("scr", 2, ti)], writes=[("scr", 2, ti)])

    def norm_mod(st, l, which, do_stats=True):
        if do_stats:
            sumsq_rstd(st)
        tiles = tiles_of(st)
        shc = 0 if which == 0 else 24
        for ti, (off, n) in enumerate(tiles):
            for k in range(8):
                tmp = scr[3 + (k % 2)]
                tk = ("scr", 3 + k % 2, ti)
                if ti < 2:
                    S.add("dve", lambda e, k=k, tmp=tmp, off=off, n=n: e.scalar_tensor_tensor(
                        out=tmp[:, off:off + n], in0=x_t[:, k, off:off + n], scalar=A_t[:, l, which, k, 0:1],
                        in1=scr[2][:, off:off + n], op0=ALU.mult, op1=ALU.mult),
                        reads=[xK(k, ti), ("A", l, which), ("scr", 2, ti)], writes=[tk])
                    S.add("act", lambda e, k=k, tmp=tmp, off=off, n=n: e.activation(
                        out=h_t[:, k, off:off + n], in_=tmp[:, off:off + n], func=AF.Identity, bias=modp(l, shc + k)),
                        reads=[tk, ("mod", l, (shc + k) // 4)], writes=[hK(k, ti)])
                else:
                    S.add("dve", lambda e, k=k: e.tensor_tensor(
                        out=tms_t[:, :, :], in0=v3(x_t[:, k, 1024:1152]), in1=v3(scr[2][:, 1024:1152]), op=ALU.mult),
                        reads=[xK(k, 2), ("scr", 2, 2)], writes=["tms"])
                    S.add("dve", lambda e, k=k: e.tensor_tensor(
                        out=tms_t[:, :, :], in0=tms_t[:, :, :], in1=bc3(A_t[:, l, which, k, 1:17]), op=ALU.mult),
                        reads=["tms", ("A", l, which)], writes=["tms"])
                    S.add("dve", lambda e, k=k: e.tensor_tensor(
                        out=v3(h_t[:, k, 1024:1152]), in0=tms_t[:, :, :], in1=bc3(mods(l, shc + k)), op=ALU.add),
                        reads=["tms", ("mod", l, (shc + k) // 4)], writes=[hK(k, 2)])

    def final_norm(st):
        sumsq_rstd(st)
        tiles = tiles_of(st)
        for ti, (off, n) in enumerate(tiles):
            for k in range(8):
                tmp = scr[3 + (k % 2)]
                tk = ("scr", 3 + k % 2, ti)
                S.add("dve", lambda e, k=k, tmp=tmp, off=off, n=n: e.scalar_tensor_tensor(
                    out=tmp[:, off:off + n], in0=x_t[:, k, off:off + n], scalar=par(P_NFIN + k), in1=scr[2][:, off:off + n],
                    op0=ALU.mult, op1=ALU.mult),
                    reads=[xK(k, ti), "par", ("scr", 2, ti)], writes=[tk])
                if ti < 2:
                    S.add("sp", lambda e, k=k, tmp=tmp, off=off, n=n: e.dma_start(
                        out=yp_d[:, k, st * 1024 + off: st * 1024 + off + n], in_=tmp[:, off:off + n]),
                        reads=[tk], writes=[("out", "yp", st, k, ti)], dma=True)
                else:
                    S.add("sp", lambda e, k=k, tmp=tmp: e.dma_start(out=ys_d[:, k, :], in_=tmp[:, 1024:1152]),
                          reads=[tk], writes=[("out", "ys", k)], dma=True)

    def residual(st, bank, m, ti, off, n, l, gc):
        if ti < 2:
            S.add("dve", lambda e: e.scalar_tensor_tensor(
                out=x_t[:, m, off:off + n], in0=ps[bank][:, 0:n], scalar=modp(l, gc + m), in1=x_t[:, m, off:off + n],
                op0=ALU.mult, op1=ALU.add),
                reads=[PK(bank), xK(m, ti), ("mod", l, (gc + m) // 4)], writes=[xK(m, ti)])
        else:
            S.add("dve", lambda e: e.tensor_tensor(
                out=tms_t[:, :, :], in0=v3(ps[bank][:, 0:128]), in1=bc3(mods(l, gc + m)), op=ALU.mult),
                reads=[PK(bank), ("mod", l, (gc + m) // 4)], writes=["tms"])
            S.add("dve", lambda e: e.tensor_tensor(
                out=v3(x_t[:, m, 1024:1152]), in0=v3(x_t[:, m, 1024:1152]), in1=tms_t[:, :, :], op=ALU.add),
                reads=["tms", xK(m, 2)], writes=[xK(m, 2)])

    def out_gemm(st, l, kind, rhs_fn, rhs_keys_fn, gc):
        tiles = tiles_of(st)
        slots = [W.get((kind, 0)), W.get((kind, 1))]
        if W.record:
            return
        for ti, (off, n) in enumerate(tiles):
            for m in range(8):
                slot = slots[m // 4]
                mm = m % 4
                bank = newbank()
                terms = [(wsl[slot][:, k, mm * 128:(mm + 1) * 128], rhs_fn(k, off, n)) for k in range(8)]
                gemm(ps[bank][:, 0:n], terms, wK(slot), bank, [rhs_keys_fn(k, ti, off, n) for k in range(8)])
                residual(st, bank, m, ti, off, n, l, gc)
        W.free(slots[0])
        W.free(slots[1])

    def mixer0(st):
        tiles = tiles_of(st)
        slots = [W.get(("in0", 0)), None, None, None]
        if W.record:
            ada_some(6)
            for c_ in (1, 2, 3):
                W.get(("in0", c_))
            W.get(("out0", 0)); W.get(("out0", 1))
            return
        mix_v = act_v

        def P_S1(g):
            ext = scr[g % 2]
            exts_t, exk_h, exk_n = exts2[g % 2], ("exts", g % 2, "h"), ("exts", g % 2, "n")
            if st == 0:
                S.add("pool", lambda e, ext=ext: e.memset(ext[:, 0:15], 0.0), writes=[scK(g % 2)])
            else:
                S.add("pool", lambda e, ext=ext, g=g: e.tensor_copy(out=ext[:, 0:15], in_=cpool_t[:, g, :]),
                      reads=[("cpool", g)], writes=[scK(g % 2)])
                S.add("sp", lambda e, g=g: e.dma_start(out=exts_t[:, :, 0:15], in_=stp_d[:, g, :, :]),
                      writes=[exk_h], dma=True)
            for ti, (off, n) in enumerate(tiles):
                bank = newbank()
                terms = [(wsl[slots[0]][:, k, g * 128:(g + 1) * 128], h_t[:, k, off:off + n]) for k in range(8)]
                gemm(ps[bank][:, 0:n], terms, wK(slots[0]), bank, [[hK(k, ti)] for k in range(8)])
                if ti < 2:
                    S.add("act", lambda e, ext=ext, off=off, bank=bank: e.activation(
                        out=ext[:, 15 + off:15 + off + 512], in_=ps[bank][:, 0:512], func=AF.Copy),
                        reads=[PK(bank)], writes=[("scr", g % 2, ti)])
                else:
                    S.add("act", lambda e, bank=bank: e.activation(
                        out=exts_t[:, :, 15:23], in_=v3(ps[bank][:, 0:128]), func=AF.Copy),
                        reads=[PK(bank)], writes=[exk_n])

        def P_S2(g):
            Wd = 2 << g
            ext = scr[g % 2]
            exts_t, exk_h, exk_n = exts2[g % 2], ("exts", g % 2, "h"), ("exts", g % 2, "n")
            dbf = scrbf(4 + g % 2)
            cur, ln, curk = ext, 1039, scK(g % 2)
            for step in range(g + 1):
                w = 1 << step
                nxt = scr[2 + step % 2]
                S.add("dve", lambda e, cur=cur, nxt=nxt, w=w, ln=ln: e.tensor_tensor(
                    out=nxt[:, 0:ln - w], in0=cur[:, 0:ln - w], in1=cur[:, w:ln], op=ALU.add),
                    reads=[curk], writes=[scK(2 + step % 2)])
                cur, ln, curk = nxt, ln - w, scK(2 + step % 2)
            o = 16 - Wd
            S.add("dve", lambda e, cur=cur, o=o, Wd=Wd, ext=ext, dbf=dbf: e.scalar_tensor_tensor(
                out=dbf[:, 0:1024], in0=cur[:, o:o + 1024], scalar=1.0 / Wd, in1=ext[:, 15:1039],
                op0=ALU.mult, op1=ALU.subtract),
                reads=[curk, scK(g % 2)], writes=[scK(4 + g % 2)])
            if st == 0:
                nf = Wd - 1
                S.add("dve", lambda e, cur=cur, o=o, nf=nf: e.tensor_tensor(
                    out=fixt[:, 0:nf], in0=cur[:, o:o + nf], in1=rc_t[:, 0:nf], op=ALU.mult),
                    reads=[curk, "rc"], writes=["fix"])
                S.add("dve", lambda e, nf=nf, ext=ext, dbf=dbf: e.tensor_tensor(
                    out=dbf[:, 0:nf], in0=fixt[:, 0:nf], in1=ext[:, 15:15 + nf], op=ALU.subtract),
                    reads=["fix", scK(g % 2)], writes=[scK(4 + g % 2)])
                S.add("pool", lambda e, ext=ext, g=g: e.tensor_copy(out=cpool_t[:, g, :], in_=ext[:, 1024:1039]),
                      reads=[scK(g % 2)], writes=[("cpool", g)])
            else:
                S.add("pool", lambda e, ext=ext, g=g: e.tensor_copy(out=opp_t[:, g, :], in_=ext[:, 1024:1039]),
                      reads=[scK(g % 2)], writes=[("opp", g)])
                curs, lns, cursk = exts_t, 23, None
                for step in range(g + 1):
                    w = 1 << step
                    nxt = sps[step % 2]
                    S.add("dve", lambda e, curs=curs, nxt=nxt, w=w, lns=lns: e.tensor_tensor(
                        out=nxt[:, :, 0:lns - w], in0=curs[:, :, 0:lns - w], in1=curs[:, :, w:lns], op=ALU.add),
                        reads=([exk_h, exk_n] if cursk is None else [cursk]), writes=[("sps", step % 2)])
                    curs, lns, cursk = nxt, lns - w, ("sps", step % 2)
                S.add("dve", lambda e, curs=curs, o=o, Wd=Wd, dbf=dbf: e.scalar_tensor_tensor(
                    out=v3(dbf[:, 1024:1152]), in0=curs[:, :, o:o + 8], scalar=1.0 / Wd, in1=exts_t[:, :, 15:23],
                    op0=ALU.mult, op1=ALU.subtract),
                    reads=[cursk, exk_n], writes=[scK(4 + g % 2)])
                S.add("sp", lambda e, g=g: e.dma_start(out=ops_d[:, g, :, :], in_=exts_t[:, :, 8:23]),
                      reads=[exk_h, exk_n], writes=[("out", "ops", g)], dma=True)
            if st == 1 and g == 3:
                S.add("sp", lambda e: e.dma_start(out=opp_d[:, :, :], in_=opp_t[:, :, :]),
                      reads=[("opp", g_) for g_ in range(4)], writes=[("out", "opp")], dma=True)

        def P_S2b(g):
            dbf = scrbf(4 + g % 2)
            for ti, (off, n) in enumerate(tiles):
                bank = newbank()
                gemm(ps[bank][:, 0:n], [(wpool_t[:, g, :], dbf[:, off:off + n])], ["wpool", scK(4 + g % 2)], bank)
                S.add("act", lambda e, bank=bank, off=off, n=n, g=g: e.activation(
                    out=mix_v[:, g, off:off + n], in_=ps[bank][:, 0:n], func=AF.Identity, scale=par(P_SPOOL + g)),
                    reads=[PK(bank), "par"], writes=actK(g, off, n))

        def cbufs(i):
            return scr[i % 2], scr[4 + i % 2], scr[6], (scr[7] if i % 2 == 0 else scr[3]), (7 if i % 2 == 0 else 3)

        def C_S1(i):
            extu, yc, xsb, bgs, bgi = cbufs(i)
            extus_t, exuk_h, exuk = extus2[i % 2], ("extus", i % 2, "h"), ("extus", i % 2, "n")
            if st == 0:
                S.add("pool", lambda e, extu=extu: e.memset(extu[:, 0:2], 0.0), writes=[scK(i % 2)])
            else:
                S.add("pool", lambda e, extu=extu, i=i: e.tensor_copy(out=extu[:, 0:2], in_=csc_t[:, i, :]),
                      reads=[("csc", i)], writes=[scK(i % 2)])
                S.add("sp", lambda e, i=i: e.dma_start(out=extus_t[:, :, 0:2], in_=sts_d[:, i, :, :]),
                      writes=[exuk_h], dma=True)
            for ti, (off, n) in enumerate(tiles):
                bx, bb, bcg = newbank(), newbank(), newbank()
                for bnk, sl in ((bx, 1), (bb, 2), (bcg, 3)):
                    terms = [(wsl[slots[sl]][:, k, i * 128:(i + 1) * 128], h_t[:, k, off:off + n]) for k in range(8)]
                    gemm(ps[bnk][:, 0:n], terms, wK(slots[sl]), bnk, [[hK(k, ti)] for k in range(8)])
                S.add("act", lambda e, bx=bx, off=off, n=n: e.activation(out=xsb[:, off:off + n], in_=ps[bx][:, 0:n], func=AF.Copy),
                      reads=[PK(bx)], writes=[("scr", 6, ti)])
                S.add("act", lambda e, bb=bb, off=off, n=n: e.activation(out=bgs[:, off:off + n], in_=ps[bb][:, 0:n], func=AF.Copy),
                      reads=[PK(bb)], writes=[("scr", bgi, ti)])
                if ti < 2:
                    S.add("dve", lambda e, bcg=bcg, off=off, extu=extu: e.tensor_tensor(
                        out=extu[:, 2 + off:2 + off + 512], in0=ps[bcg][:, 0:512], in1=xsb[:, off:off + 512], op=ALU.mult),
                        reads=[PK(bcg), ("scr", 6, ti)], writes=[("scr", i % 2, ti)])
                else:
                    S.add("dve", lambda e, bcg=bcg: e.tensor_tensor(
                        out=extus_t[:, :, 2:10], in0=v3(ps[bcg][:, 0:128]), in1=v3(xsb[:, 1024:1152]), op=ALU.mult),
                        reads=[PK(bcg), ("scr", 6, 2)], writes=[exuk])

        def C_S2(i):
            extu, yc, xsb, bgs, bgi = cbufs(i)
            extus_t, exuk_h, exuk = extus2[i % 2], ("extus", i % 2, "h"), ("extus", i % 2, "n")
            wb = P_SCW + i * 3
            S.add("act", lambda e, extu=extu, yc=yc, wb=wb, i=i: e.activation(
                out=yc[:, 0:1024], in_=extu[:, 2:1026], func=AF.Identity, scale=par(wb + 2), bias=par(P_SCB + i)),
                reads=[scK(i % 2), "par"], writes=[scK(4 + i % 2)])
            for tap in (1, 0):
                S.add("dve", lambda e, extu=extu, yc=yc, wb=wb, tap=tap: e.scalar_tensor_tensor(
                    out=yc[:, 0:1024], in0=extu[:, tap:tap + 1024], scalar=par(wb + tap), in1=yc[:, 0:1024],
                    op0=ALU.mult, op1=ALU.add),
                    reads=[scK(i % 2), scK(4 + i % 2), "par"], writes=[scK(4 + i % 2)])
            S.add("dve", lambda e, yc=yc, i=i, bgs=bgs: e.tensor_tensor(
                out=mix_v[:, 4 + i, 0:1024], in0=bgs[:, 0:1024], in1=yc[:, 0:1024], op=ALU.mult),
                reads=[scK(bgi), scK(4 + i % 2)], writes=actK(4 + i, 0, 1024))
            if st == 0:
                S.add("pool", lambda e, extu=extu, i=i: e.tensor_copy(out=csc_t[:, i, :], in_=extu[:, 1024:1026]),
                      reads=[scK(i % 2)], writes=[("csc", i)])
            else:
                S.add("pool", lambda e, extu=extu, i=i: e.tensor_copy(out=osp_t[:, i, :], in_=extu[:, 1024:1026]),
                      reads=[scK(i % 2)], writes=[("osp", i)])
                S.add("act", lambda e, wb=wb, i=i: e.activation(
                    out=ycs_t[:, :, :], in_=extus_t[:, :, 2:10], func=AF.Identity, scale=par(wb + 2), bias=par(P_SCB + i)),
                    reads=[exuk, "par"], writes=["ycs"])
                for tap in (1, 0):
                    S.add("dve", lambda e, wb=wb, tap=tap: e.scalar_tensor_tensor(
                        out=ycs_t[:, :, :], in0=extus_t[:, :, tap:tap + 8], scalar=par(wb + tap), in1=ycs_t[:, :, :],
                        op0=ALU.mult, op1=ALU.add),
                        reads=[exuk, exuk_h, "ycs", "par"], writes=["ycs"])
                S.add("dve", lambda e, i=i, bgs=bgs: e.tensor_tensor(
                    out=v3(mix_v[:, 4 + i, 1024:1152]), in0=v3(bgs[:, 1024:1152]), in1=ycs_t[:, :, :], op=ALU.mult),
                    reads=[scK(bgi), "ycs"], writes=actK(4 + i, 1024, 128))
                S.add("sp", lambda e, i=i: e.dma_start(out=oss_d[:, i, :, :], in_=extus_t[:, :, 8:10]),
                      reads=[exuk, exuk_h], writes=[("out", "oss", i)], dma=True)
                if i == 3:
                    S.add("sp", lambda e: e.dma_start(out=osp_d[:, :, :], in_=osp_t[:, :, :]),
                          reads=[("osp", i_) for i_ in range(4)], writes=[("out", "osp")], dma=True)

        units = [("P", g) for g in range(4)] + [("C", i) for i in range(4)]
        s1 = {"P": P_S1, "C": C_S1}
        s2 = {"P": P_S2, "C": C_S2}
        for n_, (kind, idx) in enumerate(units):
            if (kind, idx) == ("C", 0):
                for c_ in (1, 2, 3):
                    slots[c_] = W.get(("in0", c_))
            s1[kind](idx)
            if kind == "P" and idx < 3:
                ada_some(2)
            if (kind, idx) == ("P", 3):
                W.free(slots[0])
            if (kind, idx) == ("C", 3):
                for c_ in (1, 2, 3):
                    W.free(slots[c_])
            if n_ > 0:
                pk, pi = units[n_ - 1]
                s2[pk](pi)
            if n_ > 1:
                pk2, pi2 = units[n_ - 2]
                if pk2 == "P":
                    P_S2b(pi2)
        s2["C"](3)
        out_gemm(st, 0, "out0", lambda k, off, n: mix_v[:, k, off:off + n],
                 lambda k, ti, off, n: actK(k, off, n), 16)

    def mixer1(st):
        tiles = tiles_of(st)
        sv = [W.get(("uv1", 2)), W.get(("uv1", 3))]
        if W.record:
            W.get(("uv1", 0)); W.get(("uv1", 1))
            W.get(("out1", 0)); W.get(("out1", 1))
            return
        nblk = 9 if st == 1 else 8
        for jb in range(nblk):
            ti = jb // 4
            vg = scr[jb % 2]
            for half in range(2):
                bank = newbank()
                terms = [(h_t[:, k, jb * 128:(jb + 1) * 128], wsl[sv[half]][:, k, :]) for k in range(8)]
                gemm(ps[bank][:, :], terms, wK(sv[half]), bank, [[hK(k, ti)] for k in range(8)])
                S.add("act", lambda e, bank=bank, vg=vg, half=half: e.activation(
                    out=vg[:, half * 512:(half + 1) * 512], in_=ps[bank][:, :], func=AF.Gelu_apprx_tanh),
                    reads=[PK(bank)], writes=[("scr", jb % 2, half)])
            S.add("act", lambda e, vg=vg, jb=jb: e.activation(
                out=scrbf(2)[:, 0:1024], in_=vg[:, 0:1024], func=AF.Square, accum_out=ssv_t[:, jb:jb + 1]),
                reads=[scK(jb % 2)], writes=[scK(2), ("ssv", jb)])
            S.add("dve", lambda e, jb=jb: e.tensor_scalar(
                out=rv_t[:, jb:jb + 1], in0=ssv_t[:, jb:jb + 1], scalar1=1.0 / D, scalar2=EPS, op0=ALU.mult, op1=ALU.add),
                reads=[("ssv", jb)], writes=[("rv", jb)])
            S.add("pool", lambda e, jb=jb: e.tensor_tensor(
                out=rv_t[:, jb:jb + 1], in0=rv_t[:, jb:jb + 1], in1=nh_t[:, 0:1], op=ALU.pow),
                reads=[("rv", jb), "nh"], writes=[("rv", jb)])
            S.add("dve", lambda e, vg=vg, jb=jb: e.scalar_tensor_tensor(
                out=vn_v[:, jb, :], in0=vg[:, 0:1024], scalar=rv_t[:, jb:jb + 1], in1=gv_t[:, :], op0=ALU.mult, op1=ALU.mult),
                reads=[scK(jb % 2), ("rv", jb), "gv"], writes=vnK(jb))
            if jb == 8:
                S.add("dve", lambda e, vg=vg, jb=jb: e.scalar_tensor_tensor(
                    out=scr[3][:, 0:1024], in0=vg[:, 0:1024], scalar=rv_t[:, jb:jb + 1], in1=gv_t[:, :],
                    op0=ALU.mult, op1=ALU.mult),
                    reads=[scK(jb % 2), ("rv", jb), "gv"], writes=[scK(3)])
                S.add("sp", lambda e: e.dma_start(out=osg_d[:, :], in_=scr[3][:, 0:1024]),
                      reads=[scK(3)], writes=[("out", "osg")], dma=True)
        W.free(sv[0])
        W.free(sv[1])
        su = [W.get(("uv1", 0)), W.get(("uv1", 1))]
        for hd in range(8):
            for ti, (off, n) in enumerate(tiles):
                bank = newbank()
                sl = su[hd // 4]
                terms = [(wsl[sl][:, k, (hd % 4) * 128:(hd % 4 + 1) * 128], h_t[:, k, off:off + n]) for k in range(8)]
                gemm(ps[bank][:, 0:n], terms, wK(sl), bank, [[hK(k, ti)] for k in range(8)])
                S.add("act", lambda e, bank=bank, hd=hd, off=off, n=n: e.activation(
                    out=scr[hd][:, off:off + n], in_=ps[bank][:, 0:n], func=AF.Gelu_apprx_tanh),
                    reads=[PK(bank)], writes=[("scr", hd, ti)])
        W.free(su[0])
        W.free(su[1])
        for hd in range(8):
            for ti, (off, n) in enumerate(tiles):
                bank = newbank()

                def fn(e, hd=hd, ti=ti, off=off, n=n, bank=bank):
                    ins = None
                    for bi in range(n // 128):
                        jb = off // 128 + bi
                        o = ps[bank][:, bi * 128:(bi + 1) * 128]
                        rhs = wmT_t[:, hd, :] if ti < 2 else wblk_t[:, hd, :]
                        bh = bsh_t if ti < 2 else bssh_t
                        bl = bsl_t if ti < 2 else bssl_t
                        e.matmul(o, lhsT=vn_v[:, jb, hd * 128:(hd + 1) * 128], rhs=rhs, start=True, stop=False)
                        e.matmul(o, lhsT=ones_bf[0:1, :], rhs=bh[0:1, hd * 128:(hd + 1) * 128], start=False, stop=False)
                        ins = e.matmul(o, lhsT=ones_bf[0:1, :], rhs=bl[0:1, hd * 128:(hd + 1) * 128], start=False, stop=True)
                    return ins
                reads = ["wmT", "wblk", "ones", "bsh", "bsl", ("bssh", hd), ("bssl", hd)]
                for bi in range(n // 128):
                    reads += vnK(off // 128 + bi, hd * 128, 128)
                zop = S.add("pe", fn, reads=reads, writes=[PK(bank)])
                zop.nmm = 3 * (n // 128)
                S.add("dve", lambda e, bank=bank, hd=hd, off=off, n=n: e.tensor_tensor(
                    out=h_t[:, hd, off:off + n], in0=ps[bank][:, 0:n], in1=scr[hd][:, off:off + n], op=ALU.mult),
                    reads=[PK(bank), ("scr", hd, ti)], writes=[hK(hd, ti)])
        out_gemm(st, 1, "out1", lambda k, off, n: h_t[:, k, off:off + n],
                 lambda k, ti, off, n: [hK(k, ti)], 16)

    def ffn(st, l):
        tiles = tiles_of(st)
        if not W.record:
            norm_mod(st, l, 1)
            if st == 1:
                S.add("sp", lambda e: e.dma_start(out=stf_t[:, :, :, :], in_=stf_d[l]),
                      writes=[("stf", c) for c in range(NCH_FF)], dma=True)
        cur = {}

        def q_of(j):
            return max(q for q in range(4) if QSTART[q] <= j)

        def abase(q):
            return 0 if q % 2 == 0 else 6

        def group_start(j):
            q = q_of(j)
            return (j - QSTART[q]) % 2 == 0, min(2, QSTART[q + 1] - j)

        def S1(j):
            first, npair = group_start(j)
            if first:
                cur["slot"] = W.get(("up", l, j, npair))
                cur["j0"], cur["np"] = j, npair
                ada_some(1)
            if W.record:
                return
            slot = cur["slot"]
            jj = j - cur["j0"]
            pr = j % 2
            for c, cc in ((0, j), (1, 22 + j)):
                si = (0 if c == 0 else 2) + pr
                ext = scr[si]
                es_ = extcs[c][pr]
                esk = ("extcs", c, pr)
                if st == 0:
                    S.add("pool", lambda e, ext=ext: e.memset(ext[:, 0:2], 0.0), writes=[scK(si)])
                else:
                    S.add("pool", lambda e, ext=ext, cc=cc: e.tensor_copy(out=ext[:, 0:2], in_=cffn_t[:, l, cc, :]),
                          reads=[("cffn", l, cc)], writes=[scK(si)])
                    S.add("pool", lambda e, es_=es_, cc=cc: e.tensor_copy(out=es_[:, :, 0:2], in_=stf_t[:, cc, :, :]),
                          reads=[("stf", cc)], writes=[esk + ("h",)])
                for ti, (off, n) in enumerate(tiles):
                    bank = newbank()
                    col = c * 256 + jj * 128
                    terms = [(wsl[slot][:, k, col:col + 128], h_t[:, k, off:off + n]) for k in range(8)]
                    gemm(ps[bank][:, 0:n], terms, wK(slot), bank, [[hK(k, ti)] for k in range(8)])
                    if ti < 2:
                        S.add("act", lambda e, ext=ext, off=off, bank=bank: e.activation(
                            out=ext[:, 2 + off:2 + off + 512], in_=ps[bank][:, 0:512], func=AF.Copy),
                            reads=[PK(bank)], writes=[("scr", si, ti)])
                    else:
                        S.add("act", lambda e, es_=es_, bank=bank: e.activation(
                            out=es_[:, :, 2:10], in_=v3(ps[bank][:, 0:128]), func=AF.Copy),
                            reads=[PK(bank)], writes=[esk])
            if jj == cur["np"] - 1:
                W.free(slot)

        def S2(j):
            if W.record:
                return
            q = q_of(j)
            jl = abase(q) + j - QSTART[q]
            pr = j % 2
            for c, cc in ((0, j), (1, 22 + j)):
                si = (0 if c == 0 else 2) + pr
                yi = (4 if c == 0 else 6) + pr
                ext, y = scr[si], scr[yi]
                es_, ys_ = extcs[c][pr], yss[c][pr]
                esk, ysk = ("extcs", c, pr), ("yss", c, pr)
                wb = P_FCW + (l * NCH_FF + cc) * 3
                bb = P_FCB + l * NCH_FF + cc
                S.add("dve", lambda e, ext=ext, y=y, wb=wb, bb=bb: e.tensor_scalar(
                    out=y[:, 0:1024], in0=ext[:, 2:1026], scalar1=par(wb + 2), scalar2=par(bb),
                    op0=ALU.mult, op1=ALU.add),
                    reads=[scK(si), "par"], writes=[scK(yi)])
                for tap in (1, 0):
                    S.add("dve", lambda e, ext=ext, y=y, wb=wb, tap=tap: e.scalar_tensor_tensor(
                        out=y[:, 0:1024], in0=ext[:, tap:tap + 1024], scalar=par(wb + tap), in1=y[:, 0:1024],
                        op0=ALU.mult, op1=ALU.add),
                        reads=[scK(si), scK(yi), "par"], writes=[scK(yi)])
                if st == 0:
                    S.add("pool", lambda e, ext=ext, cc=cc: e.tensor_copy(out=cffn_t[:, l, cc, :], in_=ext[:, 1024:1026]),
                          reads=[scK(si)], writes=[("cffn", l, cc)])
                else:
                    S.add("pool", lambda e, ext=ext, cc=cc: e.tensor_copy(out=ffp_t[:, cc, :], in_=ext[:, 1024:1026]),
                          reads=[scK(si)], writes=[("ffp", cc)])
                if st == 1:
                    S.add("act", lambda e, es_=es_, ys_=ys_, wb=wb, bb=bb: e.activation(
                        out=ys_[:, :, :], in_=es_[:, :, 2:10], func=AF.Identity, scale=par(wb + 2), bias=par(bb)),
                        reads=[esk, "par"], writes=[ysk])
                    for tap in (1, 0):
                        S.add("dve", lambda e, es_=es_, ys_=ys_, wb=wb, tap=tap: e.scalar_tensor_tensor(
                            out=ys_[:, :, :], in0=es_[:, :, tap:tap + 8], scalar=par(wb + tap), in1=ys_[:, :, :],
                            op0=ALU.mult, op1=ALU.add),
                            reads=[esk, esk + ("h",), ysk, "par"], writes=[ysk])
                    S.add("pool", lambda e, es_=es_, cc=cc: e.tensor_copy(out=stf_t[:, cc, :, :], in_=es_[:, :, 8:10]),
                          reads=[esk, esk + ("h",)], writes=[("stf", cc)])
                if c == 0:
                    S.add("act", lambda e, y=y: e.activation(out=y[:, 0:1024], in_=y[:, 0:1024], func=AF.Gelu_apprx_tanh),
                          reads=[scK(yi)], writes=[scK(yi)])
                    if st == 1:
                        S.add("act", lambda e, ys_=ys_: e.activation(out=ys_[:, :, :], in_=ys_[:, :, :], func=AF.Gelu_apprx_tanh),
                              reads=[ysk], writes=[ysk])
            yg, yv = scr[4 + pr], scr[6 + pr]
            S.add("dve", lambda e, yg=yg, yv=yv, jl=jl: e.tensor_tensor(
                out=act_v[:, jl, 0:1024], in0=yg[:, 0:1024], in1=yv[:, 0:1024], op=ALU.mult),
                reads=[scK(4 + pr), scK(6 + pr)], writes=actK(jl, 0, 1024))
            if st == 1:
                S.add("dve", lambda e, pr=pr, jl=jl: e.tensor_tensor(
                    out=v3(act_v[:, jl, 1024:1152]), in0=yss[0][pr][:, :, :], in1=yss[1][pr][:, :, :], op=ALU.mult),
                    reads=[("yss", 0, pr), ("yss", 1, pr)], writes=actK(jl, 1024, 128))

        pending = []
        dstate = {}

        def down_group(q, cg, ti, mm):
            key = (q, cg)
            if key not in dstate:
                dstate[key] = [W.get(("down", l, q, cg)), 0]
            slot = dstate[key][0]
            dstate[key][1] += 1
            last = dstate[key][1] == 4 * len(tiles)
            if not W.record:
                nk = QSTART[q + 1] - QSTART[q]
                off, n = tiles[ti]
                m = cg * 4 + mm
                bank = newbank()
                terms = [(dsl[slot][:, jl, mm * 128:(mm + 1) * 128], act_v[:, abase(q) + jl, off:off + n]) for jl in range(nk)]
                gemm(ps[bank][:, 0:n], terms, [("d", slot)], bank, [actK(abase(q) + jl, off, n) for jl in range(nk)])
                residual(st, bank, m, ti, off, n, l, 40)
            if last and not W.record:
                W.free(slot)

        def push_down(q):
            if q == 3:
                for ti in range(len(tiles)):
                    for cg in range(2):
                        for mm in range(4):
                            pending.append((q, cg, ti, mm))
                return
            for cg in range(2):
                for mm in range(4):
                    for ti in range(len(tiles)):
                        pending.append((q, cg, ti, mm))

        def emit_down(k):
            for _ in range(min(k, len(pending))):
                down_group(*pending.pop(0))

        per_step = -(-8 * len(tiles) // 5) + 1
        def flush_quarter(qq):
            while pending and pending[0][0] <= qq:
                down_group(*pending.pop(0))

        for j in range(22):
            S1(j)
            if j > 0:
                qp = q_of(j - 1)
                if j - 1 == QSTART[qp] and qp >= 2:
                    flush_quarter(qp - 2)
                S2(j - 1)
            emit_down(per_step)
            q = q_of(j)
            if q > 0 and j == QSTART[q] + 1:
                push_down(q - 1)
        S2(21)
        push_down(3)
        emit_down(10 ** 6)
        ada_some(2)
        if st == 1 and not W.record:
            S.add("sp", lambda e: e.dma_start(out=ofp_d[l], in_=ffp_t[:, :, :]),
                  reads=[("ffp", c) for c in range(NCH_FF)], writes=[("out", "ofp", l)], dma=True)
            S.add("sp", lambda e: e.dma_start(out=ofs_d[l], in_=stf_t[:, :, :, :]),
                  reads=[("stf", c) for c in range(NCH_FF)], writes=[("out", "ofs", l)], dma=True)

    ADA = {"pending": []}

    def ada_some(n):
        for _ in range(n):
            if ADA["pending"]:
                l, sidx = ADA["pending"].pop(0)
                adaln(l, [sidx])

    def program():
        ADA["pending"] = [(0, s_) for s_ in range(4, 12)] + [(1, s_) for s_ in range(12)]
        if not W.record:
            setup_a()
        pre = W.get(("ada", 0, 0))
        if not W.record:
            W.unget(pre)
            setup_b()
        if not W.record:
            load_x(0)
            sumsq_rstd(0)
        adaln(0, range(0, 4))
        for st in range(2):
            S.phase = f"st{st}.norm0"
            if not W.record:
                if st == 0:
                    norm_mod(st, 0, 0, do_stats=False)
                else:
                    load_x(st)
                    norm_mod(st, 0, 0)
            S.phase = f"st{st}.mixer0"
            mixer0(st)
            S.phase = f"st{st}.ffn0"
            ffn(st, 0)
            S.phase = f"st{st}.norm1"
            if not W.record:
                norm_mod(st, 1, 0)
            S.phase = f"st{st}.mixer1"
            mixer1(st)
            S.phase = f"st{st}.ffn1"
            ffn(st, 1)
            S.phase = f"st{st}.final"
            if not W.record:
                final_norm(st)

    program()
    W.record = False
    bank_ctr[0] = 0
    program()

    out_keys = [k for k in S.lw if isinstance(k, tuple) and k[0] == "out"]
    S.add("sp", lambda e: None, reads=out_keys)
    S.resolve()

    sems = {e: es.enter_context(nc.semaphore(f"sem_{e}")) for e in Sched.CE}
    lane_sems = {}
    for q, n in nlanes.items():
        for i in range(n):
            lane_sems[(q, i)] = es.enter_context(nc.semaphore(f"dl_{q}{i}"))
    block = es.enter_context(nc.Block())

    names = {"pe": "tensor", "act": "scalar", "dve": "vector", "pool": "gpsimd", "sp": "sync"}
    for eng in Sched.ENGS:
        ops = S.eng_ops[eng]

        def body(e, ops=ops, eng=eng):
            for op in ops:
                for y in op.need:
                    if y.dma:
                        e.wait_ge(lane_sems[(y.eng, y.lane)], 16 * (y.gen + 1))
                    else:
                        e.wait_ge(sems[y.eng], y.val)
                ins = op.fn(e)
                if ins is None:
                    continue
                if op.dma:
                    ins.then_inc(lane_sems[(eng, op.lane)], 16)
                elif op.sig:
                    ins.then_inc(sems[eng], 1)

        getattr(block, names[eng])(body)
    es.close()
    global _LAST_SCHED
    _LAST_SCHED = S
    return nc


def _fm(v, nch):
    return np.ascontiguousarray(np.asarray(v, np.float32).reshape(nch, 128).T)


_PROG = None
_LAST_SCHED = None


def kernel(x_prompt, x_sample, state_pool, state_sconv, state_ffn, c_prompt, c_sample,
           norm_mix, norm_ffn, w_ada, b_ada, w_in0, w_pool, s_pool, sconv_w, sconv_b,
           w_out0, w_uv1, g_v1, w_s1, b_s1, w_out1, ffn_up, ffn_conv_w, ffn_conv_b,
           ffn_down, norm_final):
    global _PROG
    f32 = np.float32
    A = lambda a: np.ascontiguousarray(np.asarray(a, f32))
    x_prompt, x_sample = A(x_prompt), A(x_sample)
    state_pool, state_sconv, state_ffn = A(state_pool), A(state_sconv), A(state_ffn)
    c_prompt, c_sample = A(c_prompt), A(c_sample)

    par = np.zeros((128, NPAR), f32)
    for l in range(2):
        par[:, P_NM + l * 8:P_NM + l * 8 + 8] = _fm(norm_mix[l], 8)
        par[:, P_NF + l * 8:P_NF + l * 8 + 8] = _fm(norm_ffn[l], 8)
        par[:, P_BADA + l * 48:P_BADA + l * 48 + 48] = _fm(b_ada[l], 48)
        fcw = np.asarray(ffn_conv_w[l], f32)
        par[:, P_FCW + l * 132:P_FCW + (l + 1) * 132] = \
            fcw.reshape(3, NCH_FF, 128).transpose(2, 1, 0).reshape(128, 132)
        par[:, P_FCB + l * 44:P_FCB + (l + 1) * 44] = _fm(ffn_conv_b[l], 44)
    par[:, P_NFIN:P_NFIN + 8] = _fm(norm_final, 8)
    par[:, P_SPOOL:P_SPOOL + 4] = _fm(s_pool[0], 4)
    scw = np.asarray(sconv_w[0], f32)
    par[:, P_SCW:P_SCW + 12] = scw.reshape(3, 4, 128).transpose(2, 1, 0).reshape(128, 12)
    par[:, P_SCB:P_SCB + 4] = _fm(sconv_b[0], 4)

    gv = np.ascontiguousarray(np.broadcast_to(np.asarray(g_v1[0], f32)[None, :], (128, D)))
    ws1 = np.ascontiguousarray(np.asarray(w_s1[0], f32).transpose(2, 0, 1))
    bs1 = np.ascontiguousarray(np.asarray(b_s1[0], f32).reshape(1, D))
    wpool = np.ascontiguousarray(np.asarray(w_pool[0], f32).transpose(1, 0, 2))
    shared = {
        "par": par, "gv": gv, "ws1": ws1, "bs1": bs1, "w_pool": wpool,
        "w_ada": A(w_ada), "w_in0": A(w_in0[0]), "w_out0": A(w_out0[0]), "w_uv1": A(w_uv1[0]),
        "w_out1": A(w_out1[0]), "ffn_up": A(ffn_up), "ffn_down": A(ffn_down),
    }
    in_maps = []
    for i in range(NCORES):
        sl = slice(16 * i, 16 * i + 16)
        xp = x_prompt[i].T.reshape(8, 128, SEQ).transpose(1, 0, 2)
        xs = x_sample[sl].reshape(128, D).T.reshape(8, 128, 128).transpose(1, 0, 2)
        c17 = np.concatenate([c_prompt[i:i + 1], c_sample[sl]], axis=0)
        ct = c17.T.reshape(8, 128, 17).transpose(1, 0, 2)
        stp = state_pool[0, sl].transpose(2, 0, 1).reshape(4, 128, 16, 15).transpose(1, 0, 2, 3)
        sts = state_sconv[0, sl].transpose(2, 0, 1).reshape(4, 128, 16, 2).transpose(1, 0, 2, 3)
        stf = state_ffn[:, sl].transpose(0, 3, 1, 2).reshape(2, NCH_FF, 128, 16, 2).transpose(0, 2, 1, 3, 4)
        m = dict(shared)
        m.update({"xp": A(xp), "xs": A(xs), "ct": A(ct), "stp": A(stp), "sts": A(sts), "stf": A(stf)})
        in_maps.append(m)

    if _PROG is None:
        _PROG = build_program()
    res = run_bass_kernel_spmd(_PROG, in_maps, core_ids=list(range(NCORES)))
    R = res.results

    y_prompt = np.empty((8, SEQ, D), f32)
    y_sample = np.empty((128, 8, D), f32)
    pool_prompt = np.empty((1, 8, 15, 512), f32)
    pool_sample = np.empty((1, 128, 15, 512), f32)
    sconv_prompt = np.empty((1, 8, 2, 512), f32)
    sconv_sample = np.empty((1, 128, 2, 512), f32)
    ffn_prompt = np.empty((2, 8, 2, 2 * DFF), f32)
    ffn_sample = np.empty((2, 128, 2, 2 * DFF), f32)
    sg_v_sample = np.empty((1, 128, 8, D), f32)
    for i in range(NCORES):
        r = R[i]
        sl = slice(16 * i, 16 * i + 16)
        y_prompt[i] = r["yp"].transpose(1, 0, 2).reshape(D, SEQ).T
        y_sample[sl] = r["ys"].transpose(1, 0, 2).reshape(D, 128).T.reshape(16, 8, D)
        pool_prompt[0, i] = r["o_pool_p"].transpose(1, 0, 2).reshape(512, 15).T
        pool_sample[0, sl] = r["o_pool_s"].transpose(1, 0, 2, 3).reshape(512, 16, 15).transpose(1, 2, 0)
        sconv_prompt[0, i] = r["o_sconv_p"].transpose(1, 0, 2).reshape(512, 2).T
        sconv_sample[0, sl] = r["o_sconv_s"].transpose(1, 0, 2, 3).reshape(512, 16, 2).transpose(1, 2, 0)
        ffn_prompt[:, i] = r["o_ffn_p"].transpose(0, 2, 1, 3).reshape(2, 2 * DFF, 2).transpose(0, 2, 1)
        ffn_sample[:, sl] = r["o_ffn_s"].transpose(0, 2, 1, 3, 4).reshape(2, 2 * DFF, 16, 2).transpose(0, 2, 3, 1)
        sg_v_sample[0, sl] = r["o_sgv"].reshape(16, 8, D)
    return (y_prompt, y_sample, pool_prompt, pool_sample, sconv_prompt, sconv_sample,
            ffn_prompt, ffn_sample, sg_v_sample)
```

```python
import numpy as np
from contextlib import ExitStack
import concourse.bass as bass
import concourse.mybir as mybir
from concourse.bass_utils import run_bass_kernel_spmd

F32 = mybir.dt.float32
BF16 = mybir.dt.bfloat16
AF = mybir.ActivationFunctionType
ALU = mybir.AluOpType

NCORES = 8
D = 1024
SEQ = 2048
DFF = 2816
NCH_FF = 44
EPS = 1e-6
NS_W = 4
NS_D = 2
PREFETCH = 3
QSTART = (0, 6, 11, 17, 22)

P_NM, P_NF, P_NFIN, P_BADA, P_SPOOL, P_SCW, P_SCB, P_FCW, P_FCB = 0, 16, 32, 40, 136, 140, 152, 156, 420
NPAR = 508


class _Op:
    __slots__ = ("eng", "fn", "deps", "lidx", "sig", "val", "dma", "lane", "gen", "need", "phase", "nmm")


class Sched:
    ENGS = ("pe", "act", "dve", "pool", "sp")
    CE = ("pe", "act", "dve", "pool")

    def __init__(self, nlanes):
        self.ops = []
        self.eng_ops = {e: [] for e in self.ENGS}
        self.lw = {}
        self.rd = {}
        self.nlanes = nlanes
        self.dma_count = {q: 0 for q in nlanes}
        self.lane_last = {}
        self.phase = "setup"

    @staticmethod
    def _expand(keys):
        out = []
        for k in keys:
            if isinstance(k, tuple) and k[0] == "scr" and len(k) == 2:
                out += [("scr", k[1], 0), ("scr", k[1], 1), ("scr", k[1], 2)]
            else:
                out.append(k)
        return out

    def add(self, eng, fn, reads=(), writes=(), dma=False):
        reads = self._expand(reads)
        writes = self._expand(writes)
        op = _Op()
        op.phase, op.nmm = self.phase, 1
        op.eng, op.fn, op.dma, op.sig, op.val = eng, fn, dma, False, None
        op.lidx = len(self.eng_ops[eng])
        deps = {}
        for k in reads:
            w = self.lw.get(k)
            if w is not None:
                deps[w] = True
        for k in writes:
            w = self.lw.get(k)
            if w is not None and w not in deps:
                deps[w] = False
            for r in self.rd.get(k, ()):
                if r not in deps:
                    deps[r] = False
        if dma:
            nl = self.nlanes[eng]
            i = self.dma_count[eng]
            self.dma_count[eng] += 1
            op.lane, op.gen = i % nl, i // nl
            prev = self.lane_last.get((eng, op.lane))
            if prev is not None:
                deps[prev] = True
            self.lane_last[(eng, op.lane)] = op
        op.deps = deps
        for k in reads:
            self.rd.setdefault(k, []).append(op)
        for k in writes:
            self.lw[k] = op
            self.rd[k] = []
        self.ops.append(op)
        self.eng_ops[eng].append(op)
        return op

    def resolve(self):
        waited = {e: {} for e in self.ENGS}
        for op in self.ops:
            F = op.eng
            best = {}
            for y, raw in op.deps.items():
                if y.dma:
                    key = ("dma", y.eng, y.lane)
                    if key not in best or best[key].gen < y.gen:
                        best[key] = y
                else:
                    E = y.eng
                    if E == F and not op.dma:
                        if F == "pe":
                            continue
                    if E not in best or best[E].lidx < y.lidx:
                        best[E] = y
            need = []
            for key, y in best.items():
                if y.dma:
                    if waited[F].get(key, -1) >= y.gen:
                        continue
                    waited[F][key] = y.gen
                else:
                    if waited[F].get(key, -1) >= y.lidx:
                        continue
                    waited[F][key] = y.lidx
                    y.sig = True
                need.append(y)
            op.need = need
        for E in self.CE:
            c = 0
            for op in self.eng_ops[E]:
                if op.sig and not op.dma:
                    c += 1
                    op.val = c

    def emit(self, block, handles, sems, lane_sems):
        names = {"pe": "tensor", "act": "scalar", "dve": "vector", "pool": "gpsimd", "sp": "sync"}
        for eng in self.ENGS:
            ops = self.eng_ops[eng]

            def body(e, ops=ops, eng=eng):
                for op in ops:
                    for y in op.need:
                        if y.dma:
                            e.wait_ge(lane_sems[(y.eng, y.lane)], 16 * (y.gen + 1))
                        else:
                            e.wait_ge(sems[y.eng], y.val)
                    ins = op.fn(e)
                    if op.dma:
                        ins.then_inc(lane_sems[(eng, op.lane)], 16)
                    elif op.sig:
                        ins.then_inc(sems[eng], 1)

            getattr(block, names[eng])(body)


def build_program():
    nc = bass.Bass("TRN2", target_bir_lowering=False)

    def din(name, shape):
        return nc.dram_tensor(name, list(shape), F32, kind="ExternalInput").ap()

    def dout(name, shape):
        return nc.dram_tensor(name, list(shape), F32, kind="ExternalOutput").ap()

    xp_d = din("xp", [128, 8, SEQ])
    xs_d = din("xs", [128, 8, 128])
    ct_d = din("ct", [128, 8, 17])
    par_d = din("par", [128, NPAR])
    stp_d = din("stp", [128, 4, 16, 15])
    sts_d = din("sts", [128, 4, 16, 2])
    stf_d = din("stf", [2, 128, NCH_FF, 16, 2])
    gv_d = din("gv", [128, D])
    ws1_d = din("ws1", [128, 8, 128])
    bs1_d = din("bs1", [1, D])
    wada_d = din("w_ada", [2, D, 6 * D])
    win0_d = din("w_in0", [D, 2 * D])
    wpool_d = din("w_pool", [128, 4, 128])
    wout0_d = din("w_out0", [D, D])
    wuv1_d = din("w_uv1", [D, 2 * D])
    wout1_d = din("w_out1", [D, D])
    fup_d = din("ffn_up", [2, D, 2 * DFF])
    fdn_d = din("ffn_down", [2, DFF, D])

    yp_d = dout("yp", [128, 8, SEQ])
    ys_d = dout("ys", [128, 8, 128])
    opp_d = dout("o_pool_p", [128, 4, 15])
    ops_d = dout("o_pool_s", [128, 4, 16, 15])
    osp_d = dout("o_sconv_p", [128, 4, 2])
    oss_d = dout("o_sconv_s", [128, 4, 16, 2])
    ofp_d = dout("o_ffn_p", [2, 128, NCH_FF, 2])
    ofs_d = dout("o_ffn_s", [2, 128, NCH_FF, 16, 2])
    osg_d = dout("o_sgv", [128, D])

    es = ExitStack()

    def sb(name, shape, dt=F32):
        return es.enter_context(nc.sbuf_tensor(name, list(shape), dt))

    SW = 1152
    x_t = sb("x_t", [128, 8, SW])
    h_t = sb("h_t", [128, 8, SW], BF16)
    R_t = sb("R_t", [128, 11 * SW], BF16)
    NSCR = 8
    SCRW = 1152
    scr = [sb(f"scr{i}", [128, SCRW]) for i in range(NSCR)]
    wsl = [sb(f"wsl{i}", [128, 8, 512], BF16) for i in range(NS_W)]
    dsl = [sb(f"dsl{i}", [128, 6, 512], BF16) for i in range(NS_D)]
    par_t = sb("par_t", [128, NPAR])
    ct_t = sb("ct_t", [128, 8, 17])
    sc_t = sb("sc_t", [128, 8, 17], BF16)
    mod_t = sb("mod_t", [128, 2, 48, 17])
    A_t = sb("A_t", [128, 2, 2, 8, 17])
    ones_bf = sb("ones_bf", [128, 128], BF16)
    nh_t = sb("nh_t", [128, 1])
    eps_t = sb("eps_t", [128, 1])
    dmy_t = sb("dmy_t", [128, 1])
    rc_t = sb("rc_t", [128, 16])
    gv_t = sb("gv_t", [128, D])
    wpool_t = sb("wpool_t", [128, 4, 128], BF16)
    wmT_t = sb("wmT_t", [128, 8, 128], BF16)
    wblk_t = sb("wblk_t", [128, 8, 128], BF16)
    bsh_t = sb("bsh_t", [1, D], BF16)
    bsl_t = sb("bsl_t", [1, D], BF16)
    bssh_t = sb("bssh_t", [1, D], BF16)
    bssl_t = sb("bssl_t", [1, D], BF16)
    exts2 = [sb(f"exts{i}", [128, 16, 23]) for i in range(2)]
    sps = [sb(f"sps{i}", [128, 16, 23]) for i in range(2)]
    extus2 = [sb(f"extus{i}", [128, 16, 10]) for i in range(2)]
    ycs_t = sb("ycs_t", [128, 16, 8])
    extcs = [[sb(f"extcs{c}{p}", [128, 16, 10]) for p in range(2)] for c in range(2)]
    yss = [[sb(f"yss{c}{p}", [128, 16, 8]) for p in range(2)] for c in range(2)]
    tms_t = sb("tms_t", [128, 16, 8])
    stf_t = sb("stf_t", [128, NCH_FF, 16, 2])
    cpool_t = sb("cpool_t", [128, 4, 15])
    csc_t = sb("csc_t", [128, 4, 2])
    cffn_t = sb("cffn_t", [128, 2, NCH_FF, 2])
    opp_t = sb("opp_t", [128, 4, 15])
    osp_t = sb("osp_t", [128, 4, 2])
    ffp_t = sb("ffp_t", [128, NCH_FF, 2])
    ssv_t = sb("ssv_t", [128, 16])
    rv_t = sb("rv_t", [128, 16])
    fixt = sb("fixt", [128, 16])

    ps = [es.enter_context(nc.psum_tensor(f"ps{i}", [128, 512], F32)) for i in range(8)]

    nlanes = {"sp": 8, "pool": 8}
    S = Sched(nlanes)
    bank_ctr = [0]

    def newbank():
        b = bank_ctr[0] % 8
        bank_ctr[0] += 1
        return b

    def PK(b):
        return ("ps", b)

    def RK(b0, nb):
        return [("R", g) for g in range(b0 // 256, (b0 + nb + 255) // 256)]

    act_v = R_t[:, :].rearrange("p (j s) -> p j s", s=SW)
    vn_v = R_t[:, 0:9 * D].rearrange("p (j d) -> p j d", d=D)

    def actK(j, off, n):
        return RK(j * SW * 2 + off * 2, n * 2)

    def vnK(jb, d0=0, nd=D):
        return RK(jb * D * 2 + d0 * 2, nd * 2)

    bsf_t = scr[5][0:1, 0:D]
    bshf_t = scr[4][0:1, 0:D]

    def scrbf(i):
        return scr[i][:, :].bitcast(BF16)

    def par(c0, n=1):
        return par_t[:, c0:c0 + n]

    def v3(ap):
        return ap.rearrange("p (b s) -> p b s", s=8)

    def bc3(ap16):
        return ap16.unsqueeze(2).to_broadcast([128, 16, 8])

    def wK(s):
        return [("w", s, 0), ("w", s, 1)]

    class SlotRef(int):
        pass

    class WStream:
        def __init__(self):
            self.specs = []
            self.record = True
            self.i = 0
            self.issued = 0
            self.busy = {"w": [None] * NS_W, "d": [None] * NS_D}
            self.slot_of = {}

        @staticmethod
        def typ(spec):
            return "d" if spec[0] == "down" else "w"

        def _pump(self):
            while self.issued < min(len(self.specs), self.i + PREFETCH):
                t = self.typ(self.specs[self.issued])
                if None not in self.busy[t]:
                    break
                self._issue(self.issued)
                self.issued += 1

        def get(self, spec):
            if self.record:
                if self.specs and self.specs[-1] == spec and spec == ("ada", 0, 0):
                    return 0
                self.specs.append(spec)
                return 0
            i = self.i
            self.i += 1
            assert self.specs[i] == spec, (self.specs[i], spec)
            self._pump()
            assert self.issued > i, ("weight slot ring exhausted", spec)
            r = SlotRef(self.slot_of[i])
            r.item = i
            return r

        def unget(self, ref):
            self.i -= 1

        def free(self, ref):
            if self.record:
                return
            t = self.typ(self.specs[ref.item])
            assert self.busy[t][int(ref)] == ref.item
            self.busy[t][int(ref)] = None
            self._pump()

        def _issue(self, j):
            spec = self.specs[j]
            typ = self.typ(spec)
            s = self.busy[typ].index(None)
            self.busy[typ][s] = j
            self.slot_of[j] = s
            kind = spec[0]
            if kind == "down":
                _, l, q, cg = spec
                j0, j1 = QSTART[q], QSTART[q + 1]
                src = fdn_d[l].rearrange("(j p) n -> p j n", p=128)[:, j0:j1, cg * 512:(cg + 1) * 512]
                dst = dsl[s][:, 0:j1 - j0, :]
                S.add("pool", lambda e, d=dst, r=src: e.dma_start(out=d, in_=r), writes=[("d", s)], dma=True)
            elif kind == "up":
                _, l, j0, npair = spec
                w = fup_d[l].rearrange("(k p) n -> p k n", p=128)
                for c in range(2):
                    src = w[:, :, c * DFF + j0 * 128: c * DFF + (j0 + npair) * 128]
                    dst = wsl[s][:, :, c * 256: c * 256 + npair * 128]
                    S.add("pool", lambda e, d=dst, r=src: e.dma_start(out=d, in_=r), writes=[("w", s, c)], dma=True)
            else:
                if kind == "ada":
                    w = wada_d[spec[1]]
                else:
                    w = {"in0": win0_d, "out0": wout0_d, "uv1": wuv1_d, "out1": wout1_d}[kind]
                c = spec[-1]
                src = w.rearrange("(k p) n -> p k n", p=128)[:, :, c * 512:(c + 1) * 512]
                dst = wsl[s][:, :, :]
                S.add("pool", lambda e, d=dst, r=src: e.dma_start(out=d, in_=r), writes=wK(s), dma=True)

    W = WStream()

    def gemm(out_ap, terms, reads, bank, per_term_reads=None):
        n = len(terms)
        for i, (l, r) in enumerate(terms):
            rk = list(reads) + (list(per_term_reads[i]) if per_term_reads is not None else [])
            S.add("pe", lambda e, l=l, r=r, i=i: e.matmul(out_ap, lhsT=l, rhs=r, start=(i == 0), stop=(i == n - 1)),
                  reads=rk, writes=[PK(bank)])

    def tiles_of(st):
        return [(0, 512), (512, 512)] + ([(1024, 128)] if st == 1 else [])

    def hK(k, ti):
        return ("h", k, ti)

    def xK(k, ti):
        return ("x", k, ti)

    def scK(i, ti=None):
        return ("scr", i)

    def modp(l, c):
        return mod_t[:, l, c, 0:1]

    def mods(l, c):
        return mod_t[:, l, c, 1:17]

    def setup_a():
        S.add("sp", lambda e: e.dma_start(out=par_t[:, :], in_=par_d[:, :]), writes=["par"], dma=True)
        S.add("sp", lambda e: e.dma_start(out=ct_t[:, :, :], in_=ct_d[:, :, :]), writes=["ct"], dma=True)
        S.add("act", lambda e: e.activation(out=sc_t[:, :, :], in_=ct_t[:, :, :], func=AF.Silu),
              reads=["ct"], writes=["sc"])

    def setup_b():
        S.add("sp", lambda e: e.dma_start(out=gv_t[:, :], in_=gv_d[:, :]), writes=["gv"], dma=True)
        S.add("sp", lambda e: e.dma_start(out=bsf_t, in_=bs1_d[:, :]), writes=["bsf", scK(5)], dma=True)
        ws_stage = scr[7][:, 0:1024].rearrange("p (h t) -> p h t", t=128)
        blk_stage = scr[6][:, 0:1024].rearrange("p (h t) -> p h t", t=128)
        S.add("sp", lambda e: e.dma_start(out=ws_stage, in_=ws1_d[:, :, :]), writes=[scK(7)], dma=True)
        S.add("pool", lambda e: e.memset(ones_bf[:, :], 1.0), writes=["ones"])
        S.add("pool", lambda e: e.memset(nh_t[:, :], -0.5), writes=["nh"])
        S.add("pool", lambda e: e.memset(eps_t[:, :], EPS), writes=["eps"])
        for t in range(15):
            S.add("pool", lambda e, t=t: e.memset(rc_t[:, t:t + 1], 1.0 / (t + 1)), writes=["rc"])
        S.add("pool", lambda e: e.memset(scr[6][:, :], 0.0), writes=[scK(6)])
        for b in range(16):
            S.add("sp", lambda e, b=b: e.dma_start(out=blk_stage[b * 8:(b + 1) * 8, :, b * 8:(b + 1) * 8],
                                                   in_=ws1_d[0:8, :, 0:8]),
                  writes=[("scr6b", b)], reads=[scK(6)], dma=True)
        S.add("pool", lambda e: e.affine_select(out=wmT_t[:, :, :], in_=ws_stage, pattern=[[0, 8], [1, 128]],
                                                compare_op=ALU.is_ge, fill=0.0, base=0, channel_multiplier=-1),
              reads=[scK(7)], writes=["wmT"])
        S.add("pool", lambda e: e.affine_select(out=wblk_t[:, :, :], in_=blk_stage, pattern=[[0, 8], [1, 128]],
                                                compare_op=ALU.is_ge, fill=0.0, base=0, channel_multiplier=-1),
              reads=[scK(6)] + [("scr6b", b) for b in range(16)], writes=["wblk"])
        S.add("dve", lambda e: e.tensor_copy(out=bsh_t[:, :], in_=bsf_t), reads=["bsf", scK(5)], writes=["bsh"])
        S.add("dve", lambda e: e.tensor_copy(out=bshf_t, in_=bsh_t[:, :]), reads=["bsh"], writes=["bshf", scK(4)])
        S.add("dve", lambda e: e.tensor_tensor(out=bsl_t[:, :], in0=bsf_t, in1=bshf_t, op=ALU.subtract),
              reads=["bsf", "bshf", scK(4), scK(5)], writes=["bsl"])
        for hd in range(8):
            for src, dst, kn in ((bsh_t, bssh_t, "bssh"), (bsl_t, bssl_t, "bssl")):
                S.add("dve", lambda e, hd=hd, src=src, dst=dst: e.tensor_copy(
                    out=dst[0:1, hd * 128:(hd + 1) * 128].rearrange("p (b s) -> p b s", s=8),
                    in_=src[0:1, hd * 128:hd * 128 + 8].unsqueeze(1).to_broadcast([1, 16, 8])),
                    reads=["bsh", "bsl"], writes=[(kn, hd)])
        S.add("pool", lambda e: e.dma_start(out=wpool_t[:, :, :], in_=wpool_d[:, :, :]), writes=["wpool"], dma=True)

    def adaln(l, srange):
        for s in srange:
            slot = W.get(("ada", l, s))
            if W.record:
                continue
            bank = newbank()
            for j in range(4):
                for k in range(8):
                    S.add("pe", lambda e, slot=slot, bank=bank, j=j, k=k: e.matmul(
                        ps[bank][:, j * 17:(j + 1) * 17], lhsT=wsl[slot][:, k, j * 128:(j + 1) * 128],
                        rhs=sc_t[:, k, :], start=(k == 0), stop=(k == 7)),
                        reads=wK(slot) + ["sc"], writes=[PK(bank)])
            S.add("dve", lambda e, s=s, bank=bank, l=l: e.tensor_tensor(
                out=mod_t[:, l, 4 * s:4 * s + 4, :],
                in0=ps[bank][:, 0:68].rearrange("p (j n) -> p j n", n=17),
                in1=par_t[:, P_BADA + l * 48 + 4 * s: P_BADA + l * 48 + 4 * s + 4].unsqueeze(2).to_broadcast([128, 4, 17]),
                op=ALU.add), reads=[PK(bank), "par"], writes=[("mod", l, s)])
            W.free(slot)
            if W.record:
                continue
            for which, (cbase, pbase) in enumerate(((8, P_NM), (32, P_NF))):
                if s == cbase // 4 + 1:
                    S.add("dve", lambda e, which=which, cbase=cbase, pbase=pbase, l=l: e.scalar_tensor_tensor(
                        out=A_t[:, l, which, :, :], in0=mod_t[:, l, cbase:cbase + 8, :], scalar=1.0,
                        in1=par_t[:, pbase + l * 8: pbase + l * 8 + 8].unsqueeze(2).to_broadcast([128, 8, 17]),
                        op0=ALU.add, op1=ALU.mult),
                        reads=[("mod", l, cbase // 4), ("mod", l, cbase // 4 + 1), "par"], writes=[("A", l, which)])

    def load_x_tile(st, ti):
        for k in range(8):
            S.add("sp", lambda e, k=k, st=st, ti=ti: e.dma_start(
                out=x_t[:, k, ti * 512:(ti + 1) * 512], in_=xp_d[:, k, st * 1024 + ti * 512: st * 1024 + (ti + 1) * 512]),
                writes=[xK(k, ti)], dma=True)

    def load_x(st):
        if st == 0:
            S.add("sp", lambda e: e.dma_start(out=x_t[:, :, 1024:1152], in_=xs_d[:, :, :]),
                  writes=[xK(k, 2) for k in range(8)], dma=True)
        for ti in range(2):
            load_x_tile(st, ti)

    def sumsq_rstd(st):
        tiles = tiles_of(st)
        S.add("act", lambda e: e.activation(out=dmy_t[:, 0:1], in_=eps_t[:, 0:1], func=AF.Sqrt), reads=["eps"], writes=["dmy"])
        for ti, (off, n) in enumerate(tiles):
            b = newbank()
            for k in range(8):
                sq = scrbf(k % 2)
                S.add("act", lambda e, k=k, sq=sq, off=off, n=n: e.activation(
                    out=sq[:, off:off + n], in_=x_t[:, k, off:off + n], func=AF.Square),
                    reads=[xK(k, ti)], writes=[("scr", k % 2, ti)])
                S.add("pe", lambda e, k=k, sq=sq, off=off, n=n, b=b: e.matmul(
                    ps[b][:, 0:n], lhsT=ones_bf[:, :], rhs=sq[:, off:off + n], start=(k == 0), stop=(k == 7)),
                    reads=[("scr", k % 2, ti), "ones"], writes=[PK(b)])
            S.add("act", lambda e, off=off, n=n, b=b: e.activation(
                out=scr[2][:, off:off + n], in_=ps[b][:, 0:n], func=AF.Sqrt, scale=1.0 / D, bias=eps_t[:, 0:1]),
                reads=[PK(b), "eps"], writes=[("scr", 2, ti)])
            S.add("dve", lambda e, off=off, n=n: e.reciprocal(out=scr[2][:, off:off + n], in_=scr[2][:, off:off + n]),
                  reads=[("scr", 2, ti)], writes=[("scr", 2, ti)])

    def norm_mod(st, l, which, do_stats=True):
        if do_stats:
            sumsq_rstd(st)
        tiles = tiles_of(st)
        shc = 0 if which == 0 else 24
        for ti, (off, n) in enumerate(tiles):
            for k in range(8):
                tmp = scr[3 + (k % 2)]
                tk = ("scr", 3 + k % 2, ti)
                if ti < 2:
                    S.add("dve", lambda e, k=k, tmp=tmp, off=off, n=n: e.scalar_tensor_tensor(
                        out=tmp[:, off:off + n], in0=x_t[:, k, off:off + n], scalar=A_t[:, l, which, k, 0:1],
                        in1=scr[2][:, off:off + n], op0=ALU.mult, op1=ALU.mult),
                        reads=[xK(k, ti), ("A", l, which), ("scr", 2, ti)], writes=[tk])
                    S.add("act", lambda e, k=k, tmp=tmp, off=off, n=n: e.activation(
                        out=h_t[:, k, off:off + n], in_=tmp[:, off:off + n], func=AF.Identity, bias=modp(l, shc + k)),
                        reads=[tk, ("mod", l, (shc + k) // 4)], writes=[hK(k, ti)])
                else:
                    S.add("dve", lambda e, k=k: e.tensor_tensor(
                        out=tms_t[:, :, :], in0=v3(x_t[:, k, 1024:1152]), in1=v3(scr[2][:, 1024:1152]), op=ALU.mult),
                        reads=[xK(k, 2), ("scr", 2, 2)], writes=["tms"])
                    S.add("dve", lambda e, k=k: e.tensor_tensor(
                        out=tms_t[:, :, :], in0=tms_t[:, :, :], in1=bc3(A_t[:, l, which, k, 1:17]), op=ALU.mult),
                        reads=["tms", ("A", l, which)], writes=["tms"])
                    S.add("dve", lambda e, k=k: e.tensor_tensor(
                        out=v3(h_t[:, k, 1024:1152]), in0=tms_t[:, :, :], in1=bc3(mods(l, shc + k)), op=ALU.add),
                        reads=["tms", ("mod", l, (shc + k) // 4)], writes=[hK(k, 2)])

    def final_norm(st, after_tile=None):
        sumsq_rstd(st)
        tiles = tiles_of(st)
        for ti, (off, n) in enumerate(tiles):
            if after_tile is not None and ti > 0:
                after_tile(ti - 1)
            for k in range(8):
                tmp = scr[3 + (k % 2)]
                tk = ("scr", 3 + k % 2, ti)
                S.add("dve", lambda e, k=k, tmp=tmp, off=off, n=n: e.scalar_tensor_tensor(
                    out=tmp[:, off:off + n], in0=x_t[:, k, off:off + n], scalar=par(P_NFIN + k), in1=scr[2][:, off:off + n],
                    op0=ALU.mult, op1=ALU.mult),
                    reads=[xK(k, ti), "par", ("scr", 2, ti)], writes=[tk])
                if ti < 2:
                    S.add("sp", lambda e, k=k, tmp=tmp, off=off, n=n: e.dma_start(
                        out=yp_d[:, k, st * 1024 + off: st * 1024 + off + n], in_=tmp[:, off:off + n]),
                        reads=[tk], writes=[("out", "yp", st, k, ti)], dma=True)
                else:
                    S.add("sp", lambda e, k=k, tmp=tmp: e.dma_start(out=ys_d[:, k, :], in_=tmp[:, 1024:1152]),
                          reads=[tk], writes=[("out", "ys", k)], dma=True)
        if after_tile is not None:
            after_tile(len(tiles) - 1)

    def residual(st, bank, m, ti, off, n, l, gc):
        if ti < 2:
            S.add("dve", lambda e: e.scalar_tensor_tensor(
                out=x_t[:, m, off:off + n], in0=ps[bank][:, 0:n], scalar=modp(l, gc + m), in1=x_t[:, m, off:off + n],
                op0=ALU.mult, op1=ALU.add),
                reads=[PK(bank), xK(m, ti), ("mod", l, (gc + m) // 4)], writes=[xK(m, ti)])
        else:
            S.add("dve", lambda e: e.tensor_tensor(
                out=tms_t[:, :, :], in0=v3(ps[bank][:, 0:128]), in1=bc3(mods(l, gc + m)), op=ALU.mult),
                reads=[PK(bank), ("mod", l, (gc + m) // 4)], writes=["tms"])
            S.add("dve", lambda e: e.tensor_tensor(
                out=v3(x_t[:, m, 1024:1152]), in0=v3(x_t[:, m, 1024:1152]), in1=tms_t[:, :, :], op=ALU.add),
                reads=["tms", xK(m, 2)], writes=[xK(m, 2)])

    def out_gemm(st, l, kind, rhs_fn, rhs_keys_fn, gc):
        tiles = tiles_of(st)
        slots = [W.get((kind, 0)), W.get((kind, 1))]
        if W.record:
            return
        for ti, (off, n) in enumerate(tiles):
            for m in range(8):
                slot = slots[m // 4]
                mm = m % 4
                bank = newbank()
                terms = [(wsl[slot][:, k, mm * 128:(mm + 1) * 128], rhs_fn(k, off, n)) for k in range(8)]
                gemm(ps[bank][:, 0:n], terms, wK(slot), bank, [rhs_keys_fn(k, ti, off, n) for k in range(8)])
                residual(st, bank, m, ti, off, n, l, gc)
        W.free(slots[0])
        W.free(slots[1])

    def mixer0(st):
        tiles = tiles_of(st)
        slots = [W.get(("in0", 0)), None, None, None]
        if W.record:
            ada_some(6)
            for c_ in (1, 2, 3):
                W.get(("in0", c_))
            W.get(("out0", 0)); W.get(("out0", 1))
            return
        mix_v = act_v

        def P_S1(g):
            ext = scr[g % 2]
            exts_t, exk_h, exk_n = exts2[g % 2], ("exts", g % 2, "h"), ("exts", g % 2, "n")
            if st == 0:
                S.add("pool", lambda e, ext=ext: e.memset(ext[:, 0:15], 0.0), writes=[scK(g % 2)])
            else:
                S.add("pool", lambda e, ext=ext, g=g: e.tensor_copy(out=ext[:, 0:15], in_=cpool_t[:, g, :]),
                      reads=[("cpool", g)], writes=[scK(g % 2)])
                S.add("sp", lambda e, g=g: e.dma_start(out=exts_t[:, :, 0:15], in_=stp_d[:, g, :, :]),
                      writes=[exk_h], dma=True)
            for ti, (off, n) in enumerate(tiles):
                bank = newbank()
                terms = [(wsl[slots[0]][:, k, g * 128:(g + 1) * 128], h_t[:, k, off:off + n]) for k in range(8)]
                gemm(ps[bank][:, 0:n], terms, wK(slots[0]), bank, [[hK(k, ti)] for k in range(8)])
                if ti < 2:
                    S.add("act", lambda e, ext=ext, off=off, bank=bank: e.activation(
                        out=ext[:, 15 + off:15 + off + 512], in_=ps[bank][:, 0:512], func=AF.Copy),
                        reads=[PK(bank)], writes=[("scr", g % 2, ti)])
                else:
                    S.add("act", lambda e, bank=bank: e.activation(
                        out=exts_t[:, :, 15:23], in_=v3(ps[bank][:, 0:128]), func=AF.Copy),
                        reads=[PK(bank)], writes=[exk_n])

        def P_S2(g):
            Wd = 2 << g
            ext = scr[g % 2]
            exts_t, exk_h, exk_n = exts2[g % 2], ("exts", g % 2, "h"), ("exts", g % 2, "n")
            dbf = scrbf(4 + g % 2)
            cur, ln, curk = ext, 1039, scK(g % 2)
            for step in range(g + 1):
                w = 1 << step
                nxt = scr[2 + step % 2]
                S.add("dve", lambda e, cur=cur, nxt=nxt, w=w, ln=ln: e.tensor_tensor(
                    out=nxt[:, 0:ln - w], in0=cur[:, 0:ln - w], in1=cur[:, w:ln], op=ALU.add),
                    reads=[curk], writes=[scK(2 + step % 2)])
                cur, ln, curk = nxt, ln - w, scK(2 + step % 2)
            o = 16 - Wd
            S.add("dve", lambda e, cur=cur, o=o, Wd=Wd, ext=ext, dbf=dbf: e.scalar_tensor_tensor(
                out=dbf[:, 0:1024], in0=cur[:, o:o + 1024], scalar=1.0 / Wd, in1=ext[:, 15:1039],
                op0=ALU.mult, op1=ALU.subtract),
                reads=[curk, scK(g % 2)], writes=[scK(4 + g % 2)])
            if st == 0:
                nf = Wd - 1
                S.add("dve", lambda e, cur=cur, o=o, nf=nf: e.tensor_tensor(
                    out=fixt[:, 0:nf], in0=cur[:, o:o + nf], in1=rc_t[:, 0:nf], op=ALU.mult),
                    reads=[curk, "rc"], writes=["fix"])
                S.add("dve", lambda e, nf=nf, ext=ext, dbf=dbf: e.tensor_tensor(
                    out=dbf[:, 0:nf], in0=fixt[:, 0:nf], in1=ext[:, 15:15 + nf], op=ALU.subtract),
                    reads=["fix", scK(g % 2)], writes=[scK(4 + g % 2)])
                S.add("pool", lambda e, ext=ext, g=g: e.tensor_copy(out=cpool_t[:, g, :], in_=ext[:, 1024:1039]),
                      reads=[scK(g % 2)], writes=[("cpool", g)])
            else:
                S.add("pool", lambda e, ext=ext, g=g: e.tensor_copy(out=opp_t[:, g, :], in_=ext[:, 1024:1039]),
                      reads=[scK(g % 2)], writes=[("opp", g)])
                curs, lns, cursk = exts_t, 23, None
                for step in range(g + 1):
                    w = 1 << step
                    nxt = sps[step % 2]
                    S.add("dve", lambda e, curs=curs, nxt=nxt, w=w, lns=lns: e.tensor_tensor(
                        out=nxt[:, :, 0:lns - w], in0=curs[:, :, 0:lns - w], in1=curs[:, :, w:lns], op=ALU.add),
                        reads=([exk_h, exk_n] if cursk is None else [cursk]), writes=[("sps", step % 2)])
                    curs, lns, cursk = nxt, lns - w, ("sps", step % 2)
                S.add("dve", lambda e, curs=curs, o=o, Wd=Wd, dbf=dbf: e.scalar_tensor_tensor(
                    out=v3(dbf[:, 1024:1152]), in0=curs[:, :, o:o + 8], scalar=1.0 / Wd, in1=exts_t[:, :, 15:23],
                    op0=ALU.mult, op1=ALU.subtract),
                    reads=[cursk, exk_n], writes=[scK(4 + g % 2)])
                S.add("sp", lambda e, g=g: e.dma_start(out=ops_d[:, g, :, :], in_=exts_t[:, :, 8:23]),
                      reads=[exk_h, exk_n], writes=[("out", "ops", g)], dma=True)
            if st == 1 and g == 3:
                S.add("sp", lambda e: e.dma_start(out=opp_d[:, :, :], in_=opp_t[:, :, :]),
                      reads=[("opp", g_) for g_ in range(4)], writes=[("out", "opp")], dma=True)

        def P_S2b(g):
            dbf = scrbf(4 + g % 2)
            for ti, (off, n) in enumerate(tiles):
                bank = newbank()
                gemm(ps[bank][:, 0:n], [(wpool_t[:, g, :], dbf[:, off:off + n])], ["wpool", scK(4 + g % 2)], bank)
                S.add("act", lambda e, bank=bank, off=off, n=n, g=g: e.activation(
                    out=mix_v[:, g, off:off + n], in_=ps[bank][:, 0:n], func=AF.Identity, scale=par(P_SPOOL + g)),
                    reads=[PK(bank), "par"], writes=actK(g, off, n))

        def cbufs(i):
            return scr[i % 2], scr[4 + i % 2], scr[6], (scr[7] if i % 2 == 0 else scr[3]), (7 if i % 2 == 0 else 3)

        def C_S1(i):
            extu, yc, xsb, bgs, bgi = cbufs(i)
            extus_t, exuk_h, exuk = extus2[i % 2], ("extus", i % 2, "h"), ("extus", i % 2, "n")
            if st == 0:
                S.add("pool", lambda e, extu=extu: e.memset(extu[:, 0:2], 0.0), writes=[scK(i % 2)])
            else:
                S.add("pool", lambda e, extu=extu, i=i: e.tensor_copy(out=extu[:, 0:2], in_=csc_t[:, i, :]),
                      reads=[("csc", i)], writes=[scK(i % 2)])
                S.add("sp", lambda e, i=i: e.dma_start(out=extus_t[:, :, 0:2], in_=sts_d[:, i, :, :]),
                      writes=[exuk_h], dma=True)
            for ti, (off, n) in enumerate(tiles):
                bx, bb, bcg = newbank(), newbank(), newbank()
                for bnk, sl in ((bx, 1), (bb, 2), (bcg, 3)):
                    terms = [(wsl[slots[sl]][:, k, i * 128:(i + 1) * 128], h_t[:, k, off:off + n]) for k in range(8)]
                    gemm(ps[bnk][:, 0:n], terms, wK(slots[sl]), bnk, [[hK(k, ti)] for k in range(8)])
                S.add("act", lambda e, bx=bx, off=off, n=n: e.activation(out=xsb[:, off:off + n], in_=ps[bx][:, 0:n], func=AF.Copy),
                      reads=[PK(bx)], writes=[("scr", 6, ti)])
                S.add("act", lambda e, bb=bb, off=off, n=n: e.activation(out=bgs[:, off:off + n], in_=ps[bb][:, 0:n], func=AF.Copy),
                      reads=[PK(bb)], writes=[("scr", bgi, ti)])
                if ti < 2:
                    S.add("dve", lambda e, bcg=bcg, off=off, extu=extu: e.tensor_tensor(
                        out=extu[:, 2 + off:2 + off + 512], in0=ps[bcg][:, 0:512], in1=xsb[:, off:off + 512], op=ALU.mult),
                        reads=[PK(bcg), ("scr", 6, ti)], writes=[("scr", i % 2, ti)])
                else:
                    S.add("dve", lambda e, bcg=bcg: e.tensor_tensor(
                        out=extus_t[:, :, 2:10], in0=v3(ps[bcg][:, 0:128]), in1=v3(xsb[:, 1024:1152]), op=ALU.mult),
                        reads=[PK(bcg), ("scr", 6, 2)], writes=[exuk])

        def C_S2(i):
            extu, yc, xsb, bgs, bgi = cbufs(i)
            extus_t, exuk_h, exuk = extus2[i % 2], ("extus", i % 2, "h"), ("extus", i % 2, "n")
            wb = P_SCW + i * 3
            S.add("act", lambda e, extu=extu, yc=yc, wb=wb, i=i: e.activation(
                out=yc[:, 0:1024], in_=extu[:, 2:1026], func=AF.Identity, scale=par(wb + 2), bias=par(P_SCB + i)),
                reads=[scK(i % 2), "par"], writes=[scK(4 + i % 2)])
            for tap in (1, 0):
                S.add("dve", lambda e, extu=extu, yc=yc, wb=wb, tap=tap: e.scalar_tensor_tensor(
                    out=yc[:, 0:1024], in0=extu[:, tap:tap + 1024], scalar=par(wb + tap), in1=yc[:, 0:1024],
                    op0=ALU.mult, op1=ALU.add),
                    reads=[scK(i % 2), scK(4 + i % 2), "par"], writes=[scK(4 + i % 2)])
            S.add("dve", lambda e, yc=yc, i=i, bgs=bgs: e.tensor_tensor(
                out=mix_v[:, 4 + i, 0:1024], in0=bgs[:, 0:1024], in1=yc[:, 0:1024], op=ALU.mult),
                reads=[scK(bgi), scK(4 + i % 2)], writes=actK(4 + i, 0, 1024))
            if st == 0:
                S.add("pool", lambda e, extu=extu, i=i: e.tensor_copy(out=csc_t[:, i, :], in_=extu[:, 1024:1026]),
                      reads=[scK(i % 2)], writes=[("csc", i)])
            else:
                S.add("pool", lambda e, extu=extu, i=i: e.tensor_copy(out=osp_t[:, i, :], in_=extu[:, 1024:1026]),
                      reads=[scK(i % 2)], writes=[("osp", i)])
                S.add("act", lambda e, wb=wb, i=i: e.activation(
                    out=ycs_t[:, :, :], in_=extus_t[:, :, 2:10], func=AF.Identity, scale=par(wb + 2), bias=par(P_SCB + i)),
                    reads=[exuk, "par"], writes=["ycs"])
                for tap in (1, 0):
                    S.add("dve", lambda e, wb=wb, tap=tap: e.scalar_tensor_tensor(
                        out=ycs_t[:, :, :], in0=extus_t[:, :, tap:tap + 8], scalar=par(wb + tap), in1=ycs_t[:, :, :],
                        op0=ALU.mult, op1=ALU.add),
                        reads=[exuk, exuk_h, "ycs", "par"], writes=["ycs"])
                S.add("dve", lambda e, i=i, bgs=bgs: e.tensor_tensor(
                    out=v3(mix_v[:, 4 + i, 1024:1152]), in0=v3(bgs[:, 1024:1152]), in1=ycs_t[:, :, :], op=ALU.mult),
                    reads=[scK(bgi), "ycs"], writes=actK(4 + i, 1024, 128))
                S.add("sp", lambda e, i=i: e.dma_start(out=oss_d[:, i, :, :], in_=extus_t[:, :, 8:10]),
                      reads=[exuk, exuk_h], writes=[("out", "oss", i)], dma=True)
                if i == 3:
                    S.add("sp", lambda e: e.dma_start(out=osp_d[:, :, :], in_=osp_t[:, :, :]),
                          reads=[("osp", i_) for i_ in range(4)], writes=[("out", "osp")], dma=True)

        units = [("P", g) for g in range(4)] + [("C", i) for i in range(4)]
        s1 = {"P": P_S1, "C": C_S1}
        s2 = {"P": P_S2, "C": C_S2}
        for n_, (kind, idx) in enumerate(units):
            if (kind, idx) == ("C", 0):
                for c_ in (1, 2, 3):
                    slots[c_] = W.get(("in0", c_))
            s1[kind](idx)
            if kind == "P" and idx < 3:
                ada_some(2)
            if (kind, idx) == ("P", 3):
                W.free(slots[0])
            if (kind, idx) == ("C", 3):
                for c_ in (1, 2, 3):
                    W.free(slots[c_])
            if n_ > 0:
                pk, pi = units[n_ - 1]
                s2[pk](pi)
            if n_ > 1:
                pk2, pi2 = units[n_ - 2]
                if pk2 == "P":
                    P_S2b(pi2)
        s2["C"](3)
        out_gemm(st, 0, "out0", lambda k, off, n: mix_v[:, k, off:off + n],
                 lambda k, ti, off, n: actK(k, off, n), 16)

    def mixer1(st):
        tiles = tiles_of(st)
        sv = [W.get(("uv1", 2)), W.get(("uv1", 3))]
        if W.record:
            W.get(("uv1", 0)); W.get(("uv1", 1))
            W.get(("out1", 0)); W.get(("out1", 1))
            return
        nblk = 9 if st == 1 else 8
        for jb in range(nblk):
            ti = jb // 4
            vg = scr[jb % 2]
            for half in range(2):
                bank = newbank()
                terms = [(h_t[:, k, jb * 128:(jb + 1) * 128], wsl[sv[half]][:, k, :]) for k in range(8)]
                gemm(ps[bank][:, :], terms, wK(sv[half]), bank, [[hK(k, ti)] for k in range(8)])
                S.add("act", lambda e, bank=bank, vg=vg, half=half: e.activation(
                    out=vg[:, half * 512:(half + 1) * 512], in_=ps[bank][:, :], func=AF.Gelu_apprx_tanh),
                    reads=[PK(bank)], writes=[("scr", jb % 2, half)])
            S.add("act", lambda e, vg=vg, jb=jb: e.activation(
                out=scrbf(2)[:, 0:1024], in_=vg[:, 0:1024], func=AF.Square, accum_out=ssv_t[:, jb:jb + 1]),
                reads=[scK(jb % 2)], writes=[scK(2), ("ssv", jb)])
            S.add("dve", lambda e, jb=jb: e.tensor_scalar(
                out=rv_t[:, jb:jb + 1], in0=ssv_t[:, jb:jb + 1], scalar1=1.0 / D, scalar2=EPS, op0=ALU.mult, op1=ALU.add),
                reads=[("ssv", jb)], writes=[("rv", jb)])
            S.add("pool", lambda e, jb=jb: e.tensor_tensor(
                out=rv_t[:, jb:jb + 1], in0=rv_t[:, jb:jb + 1], in1=nh_t[:, 0:1], op=ALU.pow),
                reads=[("rv", jb), "nh"], writes=[("rv", jb)])
            S.add("dve", lambda e, vg=vg, jb=jb: e.scalar_tensor_tensor(
                out=vn_v[:, jb, :], in0=vg[:, 0:1024], scalar=rv_t[:, jb:jb + 1], in1=gv_t[:, :], op0=ALU.mult, op1=ALU.mult),
                reads=[scK(jb % 2), ("rv", jb), "gv"], writes=vnK(jb))
            if jb == 8:
                S.add("dve", lambda e, vg=vg, jb=jb: e.scalar_tensor_tensor(
                    out=scr[3][:, 0:1024], in0=vg[:, 0:1024], scalar=rv_t[:, jb:jb + 1], in1=gv_t[:, :],
                    op0=ALU.mult, op1=ALU.mult),
                    reads=[scK(jb % 2), ("rv", jb), "gv"], writes=[scK(3)])
                S.add("sp", lambda e: e.dma_start(out=osg_d[:, :], in_=scr[3][:, 0:1024]),
                      reads=[scK(3)], writes=[("out", "osg")], dma=True)
        W.free(sv[0])
        W.free(sv[1])
        su = [W.get(("uv1", 0)), W.get(("uv1", 1))]
        for hd in range(8):
            for ti, (off, n) in enumerate(tiles):
                bank = newbank()
                sl = su[hd // 4]
                terms = [(wsl[sl][:, k, (hd % 4) * 128:(hd % 4 + 1) * 128], h_t[:, k, off:off + n]) for k in range(8)]
                gemm(ps[bank][:, 0:n], terms, wK(sl), bank, [[hK(k, ti)] for k in range(8)])
                S.add("act", lambda e, bank=bank, hd=hd, off=off, n=n: e.activation(
                    out=scr[hd][:, off:off + n], in_=ps[bank][:, 0:n], func=AF.Gelu_apprx_tanh),
                    reads=[PK(bank)], writes=[("scr", hd, ti)])
        W.free(su[0])
        W.free(su[1])
        for hd in range(8):
            for ti, (off, n) in enumerate(tiles):
                bank = newbank()

                def fn(e, hd=hd, ti=ti, off=off, n=n, bank=bank):
                    ins = None
                    for bi in range(n // 128):
                        jb = off // 128 + bi
                        o = ps[bank][:, bi * 128:(bi + 1) * 128]
                        rhs = wmT_t[:, hd, :] if ti < 2 else wblk_t[:, hd, :]
                        bh = bsh_t if ti < 2 else bssh_t
                        bl = bsl_t if ti < 2 else bssl_t
                        e.matmul(o, lhsT=vn_v[:, jb, hd * 128:(hd + 1) * 128], rhs=rhs, start=True, stop=False)
                        e.matmul(o, lhsT=ones_bf[0:1, :], rhs=bh[0:1, hd * 128:(hd + 1) * 128], start=False, stop=False)
                        ins = e.matmul(o, lhsT=ones_bf[0:1, :], rhs=bl[0:1, hd * 128:(hd + 1) * 128], start=False, stop=True)
                    return ins
                reads = ["wmT", "wblk", "ones", "bsh", "bsl", ("bssh", hd), ("bssl", hd)]
                for bi in range(n // 128):
                    reads += vnK(off // 128 + bi, hd * 128, 128)
                zop = S.add("pe", fn, reads=reads, writes=[PK(bank)])
                zop.nmm = 3 * (n // 128)
                S.add("dve", lambda e, bank=bank, hd=hd, off=off, n=n: e.tensor_tensor(
                    out=h_t[:, hd, off:off + n], in0=ps[bank][:, 0:n], in1=scr[hd][:, off:off + n], op=ALU.mult),
                    reads=[PK(bank), ("scr", hd, ti)], writes=[hK(hd, ti)])
        out_gemm(st, 1, "out1", lambda k, off, n: h_t[:, k, off:off + n],
                 lambda k, ti, off, n: [hK(k, ti)], 16)

    def ffn(st, l):
        tiles = tiles_of(st)
        if not W.record:
            norm_mod(st, l, 1)
            if st == 1:
                S.add("sp", lambda e: e.dma_start(out=stf_t[:, :, :, :], in_=stf_d[l]),
                      writes=[("stf", c) for c in range(NCH_FF)], dma=True)
        cur = {}

        def q_of(j):
            return max(q for q in range(4) if QSTART[q] <= j)

        def abase(q):
            return 0 if q % 2 == 0 else 6

        def group_start(j):
            q = q_of(j)
            return (j - QSTART[q]) % 2 == 0, min(2, QSTART[q + 1] - j)

        def S1(j):
            first, npair = group_start(j)
            if first:
                cur["slot"] = W.get(("up", l, j, npair))
                cur["j0"], cur["np"] = j, npair
                ada_some(1)
            if W.record:
                return
            slot = cur["slot"]
            jj = j - cur["j0"]
            pr = j % 2
            for c, cc in ((0, j), (1, 22 + j)):
                si = (0 if c == 0 else 2) + pr
                ext = scr[si]
                es_ = extcs[c][pr]
                esk = ("extcs", c, pr)
                if st == 0:
                    S.add("pool", lambda e, ext=ext: e.memset(ext[:, 0:2], 0.0), writes=[scK(si)])
                else:
                    S.add("pool", lambda e, ext=ext, cc=cc: e.tensor_copy(out=ext[:, 0:2], in_=cffn_t[:, l, cc, :]),
                          reads=[("cffn", l, cc)], writes=[scK(si)])
                    S.add("pool", lambda e, es_=es_, cc=cc: e.tensor_copy(out=es_[:, :, 0:2], in_=stf_t[:, cc, :, :]),
                          reads=[("stf", cc)], writes=[esk + ("h",)])
                for ti, (off, n) in enumerate(tiles):
                    bank = newbank()
                    col = c * 256 + jj * 128
                    terms = [(wsl[slot][:, k, col:col + 128], h_t[:, k, off:off + n]) for k in range(8)]
                    gemm(ps[bank][:, 0:n], terms, wK(slot), bank, [[hK(k, ti)] for k in range(8)])
                    if ti < 2:
                        S.add("act", lambda e, ext=ext, off=off, bank=bank: e.activation(
                            out=ext[:, 2 + off:2 + off + 512], in_=ps[bank][:, 0:512], func=AF.Copy),
                            reads=[PK(bank)], writes=[("scr", si, ti)])
                    else:
                        S.add("act", lambda e, es_=es_, bank=bank: e.activation(
                            out=es_[:, :, 2:10], in_=v3(ps[bank][:, 0:128]), func=AF.Copy),
                            reads=[PK(bank)], writes=[esk])
            if jj == cur["np"] - 1:
                W.free(slot)

        def S2(j):
            if W.record:
                return
            q = q_of(j)
            jl = abase(q) + j - QSTART[q]
            pr = j % 2
            for c, cc in ((0, j), (1, 22 + j)):
                si = (0 if c == 0 else 2) + pr
                yi = (4 if c == 0 else 6) + pr
                ext, y = scr[si], scr[yi]
                es_, ys_ = extcs[c][pr], yss[c][pr]
                esk, ysk = ("extcs", c, pr), ("yss", c, pr)
                wb = P_FCW + (l * NCH_FF + cc) * 3
                bb = P_FCB + l * NCH_FF + cc
                S.add("dve", lambda e, ext=ext, y=y, wb=wb, bb=bb: e.tensor_scalar(
                    out=y[:, 0:1024], in0=ext[:, 2:1026], scalar1=par(wb + 2), scalar2=par(bb),
                    op0=ALU.mult, op1=ALU.add),
                    reads=[scK(si), "par"], writes=[scK(yi)])
                for tap in (1, 0):
                    S.add("dve", lambda e, ext=ext, y=y, wb=wb, tap=tap: e.scalar_tensor_tensor(
                        out=y[:, 0:1024], in0=ext[:, tap:tap + 1024], scalar=par(wb + tap), in1=y[:, 0:1024],
                        op0=ALU.mult, op1=ALU.add),
                        reads=[scK(si), scK(yi), "par"], writes=[scK(yi)])
                if st == 0:
                    S.add("pool", lambda e, ext=ext, cc=cc: e.tensor_copy(out=cffn_t[:, l, cc, :], in_=ext[:, 1024:1026]),
                          reads=[scK(si)], writes=[("cffn", l, cc)])
                else:
                    S.add("pool", lambda e, ext=ext, cc=cc: e.tensor_copy(out=ffp_t[:, cc, :], in_=ext[:, 1024:1026]),
                          reads=[scK(si)], writes=[("ffp", cc)])
                if st == 1:
                    S.add("act", lambda e, es_=es_, ys_=ys_, wb=wb, bb=bb: e.activation(
                        out=ys_[:, :, :], in_=es_[:, :, 2:10], func=AF.Identity, scale=par(wb + 2), bias=par(bb)),
                        reads=[esk, "par"], writes=[ysk])
                    for tap in (1, 0):
                        S.add("dve", lambda e, es_=es_, ys_=ys_, wb=wb, tap=tap: e.scalar_tensor_tensor(
                            out=ys_[:, :, :], in0=es_[:, :, tap:tap + 8], scalar=par(wb + tap), in1=ys_[:, :, :],
                            op0=ALU.mult, op1=ALU.add),
                            reads=[esk, esk + ("h",), ysk, "par"], writes=[ysk])
                    S.add("pool", lambda e, es_=es_, cc=cc: e.tensor_copy(out=stf_t[:, cc, :, :], in_=es_[:, :, 8:10]),
                          reads=[esk, esk + ("h",)], writes=[("stf", cc)])
                if c == 0:
                    S.add("act", lambda e, y=y: e.activation(out=y[:, 0:1024], in_=y[:, 0:1024], func=AF.Gelu_apprx_tanh),
                          reads=[scK(yi)], writes=[scK(yi)])
                    if st == 1:
                        S.add("act", lambda e, ys_=ys_: e.activation(out=ys_[:, :, :], in_=ys_[:, :, :], func=AF.Gelu_apprx_tanh),
                              reads=[ysk], writes=[ysk])
            yg, yv = scr[4 + pr], scr[6 + pr]
            S.add("dve", lambda e, yg=yg, yv=yv, jl=jl: e.tensor_tensor(
                out=act_v[:, jl, 0:1024], in0=yg[:, 0:1024], in1=yv[:, 0:1024], op=ALU.mult),
                reads=[scK(4 + pr), scK(6 + pr)], writes=actK(jl, 0, 1024))
            if st == 1:
                S.add("dve", lambda e, pr=pr, jl=jl: e.tensor_tensor(
                    out=v3(act_v[:, jl, 1024:1152]), in0=yss[0][pr][:, :, :], in1=yss[1][pr][:, :, :], op=ALU.mult),
                    reads=[("yss", 0, pr), ("yss", 1, pr)], writes=actK(jl, 1024, 128))

        pending = []
        dstate = {}

        def down_group(q, cg, ti, mm):
            key = (q, cg)
            if key not in dstate:
                dstate[key] = [W.get(("down", l, q, cg)), 0]
            slot = dstate[key][0]
            dstate[key][1] += 1
            last = dstate[key][1] == 4 * len(tiles)
            if not W.record:
                nk = QSTART[q + 1] - QSTART[q]
                off, n = tiles[ti]
                m = cg * 4 + mm
                bank = newbank()
                terms = [(dsl[slot][:, jl, mm * 128:(mm + 1) * 128], act_v[:, abase(q) + jl, off:off + n]) for jl in range(nk)]
                gemm(ps[bank][:, 0:n], terms, [("d", slot)], bank, [actK(abase(q) + jl, off, n) for jl in range(nk)])
                residual(st, bank, m, ti, off, n, l, 40)
            if last and not W.record:
                W.free(slot)

        def push_down(q):
            if q == 3:
                for ti in range(len(tiles)):
                    for cg in range(2):
                        for mm in range(4):
                            pending.append((q, cg, ti, mm))
                return
            for cg in range(2):
                for mm in range(4):
                    for ti in range(len(tiles)):
                        pending.append((q, cg, ti, mm))

        def emit_down(k):
            for _ in range(min(k, len(pending))):
                down_group(*pending.pop(0))

        per_step = -(-8 * len(tiles) // 5) + 1
        def flush_quarter(qq):
            while pending and pending[0][0] <= qq:
                down_group(*pending.pop(0))

        for j in range(22):
            S1(j)
            if j > 0:
                qp = q_of(j - 1)
                if j - 1 == QSTART[qp] and qp >= 2:
                    flush_quarter(qp - 2)
                S2(j - 1)
            emit_down(per_step)
            q = q_of(j)
            if q > 0 and j == QSTART[q] + 1:
                push_down(q - 1)
        S2(21)
        push_down(3)
        emit_down(10 ** 6)
        ada_some(2)
        if st == 1 and not W.record:
            S.add("sp", lambda e: e.dma_start(out=ofp_d[l], in_=ffp_t[:, :, :]),
                  reads=[("ffp", c) for c in range(NCH_FF)], writes=[("out", "ofp", l)], dma=True)
            S.add("sp", lambda e: e.dma_start(out=ofs_d[l], in_=stf_t[:, :, :, :]),
                  reads=[("stf", c) for c in range(NCH_FF)], writes=[("out", "ofs", l)], dma=True)

    ADA = {"pending": []}

    def ada_some(n):
        for _ in range(n):
            if ADA["pending"]:
                l, sidx = ADA["pending"].pop(0)
                adaln(l, [sidx])

    def program():
        ADA["pending"] = [(0, s_) for s_ in range(4, 12)] + [(1, s_) for s_ in range(12)]
        if not W.record:
            setup_a()
        pre = W.get(("ada", 0, 0))
        if not W.record:
            W.unget(pre)
            setup_b()
        if not W.record:
            load_x(0)
            sumsq_rstd(0)
        adaln(0, range(0, 4))
        for st in range(2):
            S.phase = f"st{st}.norm0"
            if not W.record:
                if st == 0:
                    norm_mod(st, 0, 0, do_stats=False)
                else:
                    norm_mod(st, 0, 0)
            S.phase = f"st{st}.mixer0"
            mixer0(st)
            S.phase = f"st{st}.ffn0"
            ffn(st, 0)
            S.phase = f"st{st}.norm1"
            if not W.record:
                norm_mod(st, 1, 0)
            S.phase = f"st{st}.mixer1"
            mixer1(st)
            S.phase = f"st{st}.ffn1"
            ffn(st, 1)
            S.phase = f"st{st}.final"
            if not W.record:
                if st == 0:
                    final_norm(st, after_tile=lambda ti_: load_x_tile(1, ti_))
                else:
                    final_norm(st)

    program()
    W.record = False
    bank_ctr[0] = 0
    program()

    out_keys = [k for k in S.lw if isinstance(k, tuple) and k[0] == "out"]
    S.add("sp", lambda e: None, reads=out_keys)
    S.resolve()

    sems = {e: es.enter_context(nc.semaphore(f"sem_{e}")) for e in Sched.CE}
    lane_sems = {}
    for q, n in nlanes.items():
        for i in range(n):
            lane_sems[(q, i)] = es.enter_context(nc.semaphore(f"dl_{q}{i}"))
    block = es.enter_context(nc.Block())

    names = {"pe": "tensor", "act": "scalar", "dve": "vector", "pool": "gpsimd", "sp": "sync"}
    for eng in Sched.ENGS:
        ops = S.eng_ops[eng]

        def body(e, ops=ops, eng=eng):
            for op in ops:
                for y in op.need:
                    if y.dma:
                        e.wait_ge(lane_sems[(y.eng, y.lane)], 16 * (y.gen + 1))
                    else:
                        e.wait_ge(sems[y.eng], y.val)
                ins = op.fn(e)
                if ins is None:
                    continue
                if op.dma:
                    ins.then_inc(lane_sems[(eng, op.lane)], 16)
                elif op.sig:
                    ins.then_inc(sems[eng], 1)

        getattr(block, names[eng])(body)
    es.close()
    global _LAST_SCHED
    _LAST_SCHED = S
    return nc


def _fm(v, nch):
    return np.ascontiguousarray(np.asarray(v, np.float32).reshape(nch, 128).T)


_PROG = None
_LAST_SCHED = None


def kernel(x_prompt, x_sample, state_pool, state_sconv, state_ffn, c_prompt, c_sample,
           norm_mix, norm_ffn, w_ada, b_ada, w_in0, w_pool, s_pool, sconv_w, sconv_b,
           w_out0, w_uv1, g_v1, w_s1, b_s1, w_out1, ffn_up, ffn_conv_w, ffn_conv_b,
           ffn_down, norm_final):
    global _PROG
    f32 = np.float32
    A = lambda a: np.ascontiguousarray(np.asarray(a, f32))
    x_prompt, x_sample = A(x_prompt), A(x_sample)
    state_pool, state_sconv, state_ffn = A(state_pool), A(state_sconv), A(state_ffn)
    c_prompt, c_sample = A(c_prompt), A(c_sample)

    par = np.zeros((128, NPAR), f32)
    for l in range(2):
        par[:, P_NM + l * 8:P_NM + l * 8 + 8] = _fm(norm_mix[l], 8)
        par[:, P_NF + l * 8:P_NF + l * 8 + 8] = _fm(norm_ffn[l], 8)
        par[:, P_BADA + l * 48:P_BADA + l * 48 + 48] = _fm(b_ada[l], 48)
        fcw = np.asarray(ffn_conv_w[l], f32)
        par[:, P_FCW + l * 132:P_FCW + (l + 1) * 132] = \
            fcw.reshape(3, NCH_FF, 128).transpose(2, 1, 0).reshape(128, 132)
        par[:, P_FCB + l * 44:P_FCB + (l + 1) * 44] = _fm(ffn_conv_b[l], 44)
    par[:, P_NFIN:P_NFIN + 8] = _fm(norm_final, 8)
    par[:, P_SPOOL:P_SPOOL + 4] = _fm(s_pool[0], 4)
    scw = np.asarray(sconv_w[0], f32)
    par[:, P_SCW:P_SCW + 12] = scw.reshape(3, 4, 128).transpose(2, 1, 0).reshape(128, 12)
    par[:, P_SCB:P_SCB + 4] = _fm(sconv_b[0], 4)

    gv = np.ascontiguousarray(np.broadcast_to(np.asarray(g_v1[0], f32)[None, :], (128, D)))
    ws1 = np.ascontiguousarray(np.asarray(w_s1[0], f32).transpose(2, 0, 1))
    bs1 = np.ascontiguousarray(np.asarray(b_s1[0], f32).reshape(1, D))
    wpool = np.ascontiguousarray(np.asarray(w_pool[0], f32).transpose(1, 0, 2))
    shared = {
        "par": par, "gv": gv, "ws1": ws1, "bs1": bs1, "w_pool": wpool,
        "w_ada": A(w_ada), "w_in0": A(w_in0[0]), "w_out0": A(w_out0[0]), "w_uv1": A(w_uv1[0]),
        "w_out1": A(w_out1[0]), "ffn_up": A(ffn_up), "ffn_down": A(ffn_down),
    }
    in_maps = []
    for i in range(NCORES):
        sl = slice(16 * i, 16 * i + 16)
        xp = x_prompt[i].T.reshape(8, 128, SEQ).transpose(1, 0, 2)
        xs = x_sample[sl].reshape(128, D).T.reshape(8, 128, 128).transpose(1, 0, 2)
        c17 = np.concatenate([c_prompt[i:i + 1], c_sample[sl]], axis=0)
        ct = c17.T.reshape(8, 128, 17).transpose(1, 0, 2)
        stp = state_pool[0, sl].transpose(2, 0, 1).reshape(4, 128, 16, 15).transpose(1, 0, 2, 3)
        sts = state_sconv[0, sl].transpose(2, 0, 1).reshape(4, 128, 16, 2).transpose(1, 0, 2, 3)
        stf = state_ffn[:, sl].transpose(0, 3, 1, 2).reshape(2, NCH_FF, 128, 16, 2).transpose(0, 2, 1, 3, 4)
        m = dict(shared)
        m.update({"xp": A(xp), "xs": A(xs), "ct": A(ct), "stp": A(stp), "sts": A(sts), "stf": A(stf)})
        in_maps.append(m)

    if _PROG is None:
        _PROG = build_program()
    res = run_bass_kernel_spmd(_PROG, in_maps, core_ids=list(range(NCORES)))
    R = res.results

    y_prompt = np.empty((8, SEQ, D), f32)
    y_sample = np.empty((128, 8, D), f32)
    pool_prompt = np.empty((1, 8, 15, 512), f32)
    pool_sample = np.empty((1, 128, 15, 512), f32)
    sconv_prompt = np.empty((1, 8, 2, 512), f32)
    sconv_sample = np.empty((1, 128, 2, 512), f32)
    ffn_prompt = np.empty((2, 8, 2, 2 * DFF), f32)
    ffn_sample = np.empty((2, 128, 2, 2 * DFF), f32)
    sg_v_sample = np.empty((1, 128, 8, D), f32)
    for i in range(NCORES):
        r = R[i]
        sl = slice(16 * i, 16 * i + 16)
        y_prompt[i] = r["yp"].transpose(1, 0, 2).reshape(D, SEQ).T
        y_sample[sl] = r["ys"].transpose(1, 0, 2).reshape(D, 128).T.reshape(16, 8, D)
        pool_prompt[0, i] = r["o_pool_p"].transpose(1, 0, 2).reshape(512, 15).T
        pool_sample[0, sl] = r["o_pool_s"].transpose(1, 0, 2, 3).reshape(512, 16, 15).transpose(1, 2, 0)
        sconv_prompt[0, i] = r["o_sconv_p"].transpose(1, 0, 2).reshape(512, 2).T
        sconv_sample[0, sl] = r["o_sconv_s"].transpose(1, 0, 2, 3).reshape(512, 16, 2).transpose(1, 2, 0)
        ffn_prompt[:, i] = r["o_ffn_p"].transpose(0, 2, 1, 3).reshape(2, 2 * DFF, 2).transpose(0, 2, 1)
        ffn_sample[:, sl] = r["o_ffn_s"].transpose(0, 2, 1, 3, 4).reshape(2, 2 * DFF, 16, 2).transpose(0, 2, 3, 1)
        sg_v_sample[0, sl] = r["o_sgv"].reshape(16, 8, D)
    return (y_prompt, y_sample, pool_prompt, pool_sample, sconv_prompt, sconv_sample,
            ffn_prompt, ffn_sample, sg_v_sample)
```
